# Optimizing a Trainium2 kernel written in Bass

```python
import jax
import jax.numpy as jnp
from jax import lax
import numpy as np


D_MODEL = 2048
BATCH = 8
SEQ = 2048
DEPTH = 2

MEM_LEN = 256
D_FF = 5632
EPS = 1e-6
N_BRANCH = 3

SGU_GROUPS = 4
SGU_GROUP_DIM = D_MODEL // 8
SGU_WIDTH = SGU_GROUPS * SGU_GROUP_DIM
SGU_CHUNK = 128

FOX_HEADS = 8
FOX_HEAD_DIM = D_MODEL // 16
FOX_WIDTH = FOX_HEADS * FOX_HEAD_DIM
FOX_BLOCK = 128

GLA_HEADS = 4
GLA_DK = D_MODEL // 16
GLA_DV = D_MODEL // 8
GLA_KW = GLA_HEADS * GLA_DK
GLA_VW = GLA_HEADS * GLA_DV
GLA_GATE_RANK = 16
GLA_GATE_TAU = 16.0
GLA_CHUNK = 64

XA_HEADS = 4
XA_HEAD_DIM = D_MODEL // 16
XA_WIDTH = XA_HEADS * XA_HEAD_DIM

N_IN = (2 * SGU_WIDTH + 3 * FOX_WIDTH + FOX_HEADS + 2 * GLA_KW + GLA_VW
        + GLA_GATE_RANK + GLA_VW + N_BRANCH * D_MODEL)

kernel_name = 'hybrid_sgu_fox_gla_macaron_block'


def _in_widths():
    return (SGU_WIDTH, SGU_WIDTH,
            FOX_WIDTH, FOX_WIDTH, FOX_WIDTH, FOX_HEADS,
            GLA_KW, GLA_KW, GLA_VW, GLA_GATE_RANK,
            GLA_VW,
            N_BRANCH * D_MODEL)


def _split_cols(z, widths):
    idx = [int(i) for i in np.cumsum(widths)[:-1]]
    return jnp.split(z, idx, axis=-1)


def rms_norm(x, g):
    xf = x.astype(jnp.float32)
    y = xf * lax.rsqrt(jnp.mean(xf * xf, axis=-1, keepdims=True) + EPS)
    return (y * g.astype(jnp.float32)).astype(x.dtype)


def layer_norm(x, g, b):
    xf = x.astype(jnp.float32)
    mu = jnp.mean(xf, axis=-1, keepdims=True)
    xc = xf - mu
    var = jnp.mean(xc * xc, axis=-1, keepdims=True)
    y = xc * lax.rsqrt(var + EPS) * g.astype(jnp.float32) + b.astype(jnp.float32)
    return y.astype(x.dtype)


def swiglu_ffn(x, w_in, w_out):
    gate, up = jnp.split(x @ w_in, 2, axis=-1)
    return (jax.nn.silu(gate) * up) @ w_out


def sgu_branch(u, v, ln_g, ln_b, w_s, b_s):
    B, S, _ = u.shape
    n_chunks = S // SGU_CHUNK
    u = jax.nn.gelu(u)
    v = jax.nn.gelu(v)
    v = layer_norm(v.reshape(B, S, SGU_GROUPS, SGU_GROUP_DIM), ln_g, ln_b)
    v = v.reshape(B, n_chunks, SGU_CHUNK, SGU_GROUPS, SGU_GROUP_DIM)
    causal = jnp.tril(jnp.ones((SGU_CHUNK, SGU_CHUNK), dtype=bool))
    w = jnp.where(causal[None], w_s, jnp.zeros_like(w_s))
    mixed = jnp.einsum('gts,bnsgc->bntgc', w, v) + jnp.swapaxes(b_s, 0, 1)[:, :, None]
    return u * mixed.reshape(B, S, SGU_WIDTH)


def fox_branch(q, k, v, f_logit, b_f):
    B, S, _ = q.shape
    H, Dh = FOX_HEADS, FOX_HEAD_DIM
    q = q.reshape(B, S, H, Dh).transpose(0, 2, 1, 3) * (Dh ** -0.5)
    k = k.reshape(B, S, H, Dh).transpose(0, 2, 1, 3)
    v = v.reshape(B, S, H, Dh).transpose(0, 2, 1, 3)
    log_f = jax.nn.log_sigmoid((f_logit + b_f).astype(jnp.float32))
    cum = jnp.cumsum(log_f, axis=1).transpose(0, 2, 1)
    outs = []
    for i in range(S // FOX_BLOCK):
        lo = i * FOX_BLOCK
        hi = lo + FOX_BLOCK
        logits = jnp.einsum('bhtd,bhsd->bhts', q[:, :, lo:hi], k[:, :, :hi]).astype(jnp.float32)
        logits = logits + cum[:, :, lo:hi, None] - cum[:, :, None, :hi]
        causal = (lo + jnp.arange(FOX_BLOCK))[:, None] >= jnp.arange(hi)[None, :]
        logits = jnp.where(causal, logits, -jnp.inf)
        p = jax.nn.softmax(logits, axis=-1).astype(v.dtype)
        outs.append(jnp.einsum('bhts,bhsd->bhtd', p, v[:, :, :hi]))
    o = jnp.concatenate(outs, axis=2)
    return o.transpose(0, 2, 1, 3).reshape(B, S, FOX_WIDTH)


def gla_branch(q, k, v, a_low, r, w_gate, b_gate, o_norm):
    B, S, _ = q.shape
    H, dk, dv, C = GLA_HEADS, GLA_DK, GLA_DV, GLA_CHUNK
    n = S // C
    dt = v.dtype
    f32 = jnp.float32
    g = jax.nn.log_sigmoid((a_low @ w_gate + b_gate).astype(f32)) / GLA_GATE_TAU

    def chunks(t, d):
        return t.astype(f32).reshape(B, n, C, H, d).transpose(0, 1, 3, 2, 4)

    qc = chunks(q, dk) * (dk ** -0.5)
    kc = chunks(k, dk)
    vc = chunks(v, dv)
    bc = jnp.cumsum(chunks(g, dk), axis=3)
    b_last = bc[:, :, :, -1:, :]
    b_ref = bc[:, :, :, C // 2:C // 2 + 1, :]
    causal = jnp.tril(jnp.ones((C, C), dtype=bool))
    att = jnp.einsum('bnhtd,bnhsd->bnhts', qc * jnp.exp(bc - b_ref), kc * jnp.exp(b_ref - bc))
    att = jnp.where(causal, att, 0.0)
    o_intra = jnp.einsum('bnhts,bnhsv->bnhtv', att, vc)
    upd = jnp.einsum('bnhsd,bnhsv->bnhdv', kc * jnp.exp(b_last - bc), vc)
    decay = jnp.swapaxes(jnp.exp(b_last), -1, -2)

    def step(state, inp):
        dec, u_n = inp
        return dec * state + u_n, state

    init = jnp.zeros((B, H, dk, dv), f32)
    _, s_prev = lax.scan(step, init, (jnp.moveaxis(decay, 1, 0), jnp.moveaxis(upd, 1, 0)))
    s_prev = jnp.moveaxis(s_prev, 0, 1)
    o_inter = jnp.einsum('bnhtd,bnhdv->bnhtv', qc * jnp.exp(bc), s_prev)
    o = (o_intra + o_inter).transpose(0, 1, 3, 2, 4).reshape(B, S, H, dv)
    o = rms_norm(o, o_norm).reshape(B, S, GLA_VW).astype(dt)
    return o * jax.nn.silu(r)


def cross_attention(n, m, w_q, w_kv, w_o):
    B, S, _ = n.shape
    M = m.shape[1]
    q = (n @ w_q).reshape(B, S, XA_HEADS, XA_HEAD_DIM) * (XA_HEAD_DIM ** -0.5)
    k, v = jnp.split(m @ w_kv, 2, axis=-1)
    k = k.reshape(B, M, XA_HEADS, XA_HEAD_DIM)
    v = v.reshape(B, M, XA_HEADS, XA_HEAD_DIM)
    logits = jnp.einsum('bthd,bmhd->bhtm', q, k).astype(jnp.float32)
    p = jax.nn.softmax(logits, axis=-1).astype(v.dtype)
    o = jnp.einsum('bhtm,bmhd->bthd', p, v).reshape(B, S, XA_WIDTH)
    return o @ w_o


def setup_inputs(seed: int = 0) -> dict:
    key = jax.random.key(seed)
    ks = iter(jax.random.split(key, 48))
    L, D = DEPTH, D_MODEL

    def nrm(shape, scale):
        return scale * jax.random.normal(next(ks), shape, jnp.float32)

    def gain(shape):
        return 1.0 + 0.02 * jax.random.normal(next(ks), shape, jnp.float32)

    return {
        'x': nrm((BATCH, SEQ, D), 1.0),
        'mem': nrm((BATCH, MEM_LEN, D), 1.0),
        'ffn1_norm': gain((L, D)),
        'ffn1_w_in': nrm((L, D, 2 * D_FF), D ** -0.5),
        'ffn1_w_out': nrm((L, D_FF, D), D_FF ** -0.5),
        'mix_norm': gain((L, D)),
        'w_in': nrm((L, D, N_IN), D ** -0.5),
        'sgu_ln_g': gain((L, SGU_GROUPS, SGU_GROUP_DIM)),
        'sgu_ln_b': nrm((L, SGU_GROUPS, SGU_GROUP_DIM), 0.02),
        'sgu_w_s': nrm((L, SGU_GROUPS, SGU_CHUNK, SGU_CHUNK), SGU_CHUNK ** -0.5),
        'sgu_b_s': gain((L, SGU_GROUPS, SGU_CHUNK)),
        'fox_b_f': 2.0 + nrm((L, FOX_HEADS), 0.5),
        'gla_w_gate': nrm((L, GLA_GATE_RANK, GLA_KW), GLA_GATE_RANK ** -0.5),
        'gla_b_gate': nrm((L, GLA_KW), 0.01),
        'gla_o_norm': gain((L, GLA_HEADS, GLA_DV)),
        'w_branch_a': nrm((L, SGU_WIDTH, D), SGU_WIDTH ** -0.5),
        'w_branch_b': nrm((L, FOX_WIDTH, D), FOX_WIDTH ** -0.5),
        'w_branch_c': nrm((L, GLA_VW, D), GLA_VW ** -0.5),
        'w_out': nrm((L, D, D), D ** -0.5),
        'xa_norm': gain((L, D)),
        'mem_norm': gain((L, D)),
        'xa_w_q': nrm((L, D, XA_WIDTH), D ** -0.5),
        'xa_w_kv': nrm((L, D, 2 * XA_WIDTH), D ** -0.5),
        'xa_w_o': nrm((L, XA_WIDTH, D), XA_WIDTH ** -0.5),
        'ffn2_norm': gain((L, D)),
        'ffn2_w_in': nrm((L, D, 2 * D_FF), D ** -0.5),
        'ffn2_w_out': nrm((L, D_FF, D), D_FF ** -0.5),
        'final_norm': gain((D,)),
    }


def reference(x, mem, ffn1_norm, ffn1_w_in, ffn1_w_out, mix_norm, w_in,
              sgu_ln_g, sgu_ln_b, sgu_w_s, sgu_b_s, fox_b_f,
              gla_w_gate, gla_b_gate, gla_o_norm,
              w_branch_a, w_branch_b, w_branch_c, w_out,
              xa_norm, mem_norm, xa_w_q, xa_w_kv, xa_w_o,
              ffn2_norm, ffn2_w_in, ffn2_w_out, final_norm):
    h = x
    for l in range(DEPTH):
        h = h + 0.5 * swiglu_ffn(rms_norm(h, ffn1_norm[l]), ffn1_w_in[l], ffn1_w_out[l])

        n = rms_norm(h, mix_norm[l])
        (su, sv, fq, fk, fv, ff, gq, gk, gv, ga, gr, gates) = _split_cols(n @ w_in[l], _in_widths())
        y_a = sgu_branch(su, sv, sgu_ln_g[l], sgu_ln_b[l], sgu_w_s[l], sgu_b_s[l]) @ w_branch_a[l]
        y_b = fox_branch(fq, fk, fv, ff, fox_b_f[l]) @ w_branch_b[l]
        y_c = gla_branch(gq, gk, gv, ga, gr, gla_w_gate[l], gla_b_gate[l], gla_o_norm[l]) @ w_branch_c[l]
        g_a, g_b, g_c = jnp.split(jax.nn.sigmoid(gates), N_BRANCH, axis=-1)
        h = h + (g_a * y_a + g_b * y_b + g_c * y_c) @ w_out[l]

        h = h + cross_attention(rms_norm(h, xa_norm[l]), rms_norm(mem, mem_norm[l]),
                                xa_w_q[l], xa_w_kv[l], xa_w_o[l])

        h = h + 0.5 * swiglu_ffn(rms_norm(h, ffn2_norm[l]), ffn2_w_in[l], ffn2_w_out[l])
    return rms_norm(h, final_norm)
```

```python
import numpy as np
from contextlib import ExitStack
import concourse.bass as bass
import concourse.mybir as mybir
from concourse.bass_utils import run_bass_kernel_spmd

F32 = mybir.dt.float32
BF16 = mybir.dt.bfloat16
AF = mybir.ActivationFunctionType
ALU = mybir.AluOpType
AX = mybir.AxisListType

D = 2048
S = 2048
L = 2
MEM = 256
DFF = 5632
NT = S // 128
DC = D // 128
EPS = 1e-6
N_IN = 14360
O_SU, O_SV, O_FQ, O_FK, O_FV, O_FF = 0, 1024, 2048, 3072, 4096, 5120
O_GQ, O_GK, O_GV, O_GA, O_GR, O_GATES = 5128, 5640, 6152, 7176, 7192, 8216

import os
GLA_STOP = int(os.environ.get("GLA_STOP", "0"))
COMPUTE = ("pe", "act", "dve", "pool")


class Buf:
    __slots__ = ("name", "last_w", "readers", "dsem", "last_dma")

    def __init__(self, name):
        self.name = name
        self.last_w = None
        self.readers = {}
        self.dsem = None
        self.last_dma = None


class Tile:
    def __init__(self, t, buf):
        self.t = t
        self.b = buf

    def __getitem__(self, k):
        return self.t[k]


class KB:
    def __init__(self, nc):
        self.nc = nc
        self.eng = {"pe": nc.tensor, "act": nc.scalar, "dve": nc.vector, "pool": nc.gpsimd, "sp": nc.sync}
        self.semh = {}
        self.cnt = {}
        for e in COMPUTE:
            self.semh[e] = nc.alloc_semaphore(name="prog_" + e)
            self.cnt[e] = 0
        self.waited = {e: {} for e in self.eng}
        self.free_dsems = []
        for i in range(80):
            k = "d%d" % i
            self.semh[k] = nc.alloc_semaphore(name="dma_%d" % i)
            self.cnt[k] = 0
            self.free_dsems.append(k)
        self.stage_dsems = []
        self.dbufs = {}
        self.psum = []
        self.psum_i = 0
        self.n_inst = 0

    def d(self, name, *idx):
        key = (name,) + idx
        b = self.dbufs.get(key)
        if b is None:
            b = Buf(str(key))
            self.dbufs[key] = b
        return b

    def next_psum(self):
        p = self.psum[self.psum_i % len(self.psum)]
        self.psum_i += 1
        return p

    def _need(self, reads, writes):
        need = {}
        raw = {}

        def add(d, ev):
            if ev is not None and d.get(ev[0], 0) < ev[1]:
                d[ev[0]] = ev[1]

        for b in reads:
            add(need, b.last_w)
            add(raw, b.last_w)
        for b in writes:
            add(need, b.last_w)
            for k, v in b.readers.items():
                add(need, (k, v))
        return need, raw

    def _wait(self, e, need_raw, skip_self):
        need, raw = need_raw
        w = self.waited[e]
        for k, v in need.items():
            if skip_self and k == e:
                v = raw.get(k, 0)
                if v == 0 or e == "pe":
                    continue
            if w.get(k, 0) >= v:
                continue
            self.eng[e].wait_ge(self.semh[k], v)
            w[k] = v

    def _record(self, ev, reads, writes):
        for b in reads:
            if b.readers.get(ev[0], 0) < ev[1]:
                b.readers[ev[0]] = ev[1]
        for b in writes:
            b.last_w = ev
            b.readers = {}

    def op(self, e, fn, reads=(), writes=()):
        reads = [x.b if isinstance(x, Tile) else x for x in reads]
        writes = [x.b if isinstance(x, Tile) else x for x in writes]
        self._wait(e, self._need(reads, writes), True)
        ins = fn(self.eng[e])
        self.cnt[e] += 1
        ins.then_inc(self.semh[e], 1)
        self._record((e, self.cnt[e]), reads, writes)
        self.n_inst += 1

    def mm(self, ps, pairs, reads, extra_writes=()):
        reads = [x.b if isinstance(x, Tile) else x for x in reads]
        writes = [ps.b] + [x.b if isinstance(x, Tile) else x for x in extra_writes]
        self._wait("pe", self._need(reads, writes), True)
        n = len(pairs)
        ins = None
        for i, (o, l, r) in enumerate(pairs):
            ins = self.nc.tensor.matmul(o, l, r, start=(i == 0), stop=(i == n - 1))
        self.cnt["pe"] += 1
        ins.then_inc(self.semh["pe"], 1)
        self._record(("pe", self.cnt["pe"]), reads, writes)
        self.n_inst += n

    def mm_raw(self, fn, reads, writes):
        self.op("pe", fn, reads, writes)

    def dma(self, q, out, in_, sb, reads=(), writes=(), **kw):
        sbb = sb.b if isinstance(sb, Tile) else sb
        reads = [x.b if isinstance(x, Tile) else x for x in reads]
        writes = [x.b if isinstance(x, Tile) else x for x in writes]
        if sbb.dsem is None:
            sbb.dsem = self.free_dsems.pop()
            self.stage_dsems.append(sbb.dsem)
        need, raw = self._need(reads, writes)
        if sbb.last_dma is not None:
            ev = sbb.last_dma
            if need.get(ev[0], 0) < ev[1]:
                need[ev[0]] = ev[1]
        self._wait(q, (need, raw), False)
        ins = self.eng[q].dma_start(out=out, in_=in_, **kw)
        k = sbb.dsem
        self.cnt[k] += 16
        ins.then_inc(self.semh[k], 16)
        ev = (k, self.cnt[k])
        sbb.last_dma = ev
        self._record(ev, reads, writes)
        self.n_inst += 1

    def barrier(self):
        need = {k: v for k, v in self.cnt.items() if v > 0}
        for e in self.eng:
            self._wait(e, (need, {}), True)
        self.free_dsems.extend(self.stage_dsems)
        self.stage_dsems = []


class Stage:
    def __init__(self, kb, name):
        self.kb = kb
        self.name = name
        self.es = ExitStack()
        self.n = 0

    def __enter__(self):
        self.es.__enter__()
        return self

    def __exit__(self, *a):
        self.kb.barrier()
        return self.es.__exit__(*a)

    def tile(self, shape, dtype, name=None):
        self.n += 1
        nm = "%s_%s%d" % (self.name, name or "t", self.n)
        t = self.es.enter_context(self.kb.nc.sbuf_tensor(nm, list(shape), dtype))
        return Tile(t, Buf(nm))

    def ring(self, n, shape, dtype, name=None):
        return Ring([self.tile(shape, dtype, name) for _ in range(n)])


class Ring:
    def __init__(self, tiles):
        self.tiles = tiles
        self.i = 0

    def next(self):
        t = self.tiles[self.i % len(self.tiles)]
        self.i += 1
        return t


class Prog:
    def __init__(self, n_layers=L, stages=None, debug_out=None):
        self.nc = nc = bass.Bass("TRN2", target_bir_lowering=False)
        self.kb = KB(nc)
        self.n_layers = n_layers
        self.inp = {}
        self.debug_out = debug_out or {}

    def get(self, name, shape, dtype=F32):
        if name not in self.inp:
            self.inp[name] = self.nc.dram_tensor(name, list(shape), dtype, kind="ExternalInput").ap()
        return self.inp[name]

    def dump(self, name, tile, shape, dtype=F32):
        if ("dbg_" + name) not in self.debug_out:
            return
        ap = self.nc.dram_tensor("dbg_" + name, list(shape), dtype, kind="ExternalOutput").ap()
        self.kb.dma("sp", ap, tile.t[tuple(slice(None) for _ in shape)], tile, reads=[tile], writes=[self.kb.d("dbg_" + name)])

    def dscr(self, name, shape, dtype=F32):
        kind = "ExternalOutput" if name in self.debug_out else "Internal"
        return self.nc.dram_tensor(name, list(shape), dtype, kind=kind).ap()


def transpose_stage(P, src, dst, rows, cols, sname):
    kb, nc = P.kb, P.nc
    with Stage(kb, "tr" + sname) as st:
        ident = P.ident_tile
        inr = st.ring(2, [128, cols], F32, "in")
        outr = st.ring(3, [128, 512], F32, "o")
        for r in range(rows // 128):
            it = inr.next()
            kb.dma("sp", it[:, :], src[r * 128:(r + 1) * 128, :], it, reads=[kb.d(sname + "src", r)], writes=[it])
            for c4 in range(cols // 512):
                ps = kb.next_psum()

                def f(pe, ps=ps, it=it, c4=c4):
                    ins = None
                    for c in range(4):
                        ins = pe.transpose(ps[:, c * 128:(c + 1) * 128], it[:, (c4 * 4 + c) * 128:(c4 * 4 + c + 1) * 128], ident[:, :])
                    return ins
                kb.op("pe", f, reads=[it, ident], writes=[ps])
                ot = outr.next()
                eng = "act" if (c4 % 2 == 0) else "dve"
                if eng == "act":
                    kb.op("act", lambda e, ot=ot, ps=ps: e.copy(ot[:, :], ps[:, :]), reads=[ps], writes=[ot])
                else:
                    kb.op("dve", lambda e, ot=ot, ps=ps: e.tensor_copy(ot[:, :], ps[:, :]), reads=[ps], writes=[ot])
                dview = dst[c4 * 512:(c4 + 1) * 512, r * 128:(r + 1) * 128].rearrange("(c p) j -> p c j", p=128)
                kb.dma("sp", dview, ot[:, :].rearrange("p (c j) -> p c j", c=4), ot,
                       reads=[ot], writes=[kb.d(sname + "dst", c4, r)])


def rms_stats(P, st, xt, TB, sq_ring, rstd):
    kb = P.kb
    ps = kb.next_psum()
    sqs = []
    for c in range(DC):
        sq = sq_ring.next()
        import os
        if c % 2 == 0 or "c" in os.environ.get("KDBG", ""):
            kb.op("act", lambda e, sq=sq, c=c: e.activation(out=sq[:, :TB], in_=xt[:, c, :], func=AF.Square), reads=[xt], writes=[sq])
        else:
            kb.op("pool", lambda e, sq=sq, c=c: e.tensor_tensor(sq[:, :TB], xt[:, c, :], xt[:, c, :], ALU.mult), reads=[xt], writes=[sq])

        def f(pe, sq=sq, c=c, ps=ps):
            return pe.matmul(ps[:, :TB], P.ones_f32[:, :], sq[:, :TB], start=(c == 0), stop=(c == DC - 1))
        kb.op("pe", f, reads=[sq, P.ones_f32], writes=[ps])
    import os
    dbg = os.environ.get("KDBG", "")
    if "a" in dbg:
        kb.op("act", lambda e: e.copy(rstd[:, :TB], ps[:, :TB]), reads=[ps, P.eps_tile], writes=[rstd])
    else:
        kb.op("act", lambda e: e.activation(out=rstd[:, :TB], in_=ps[:, :TB], func=AF.Sqrt, bias=P.eps_tile[:, 0:1], scale=1.0 / D), reads=[ps, P.eps_tile], writes=[rstd])
    if "b" not in dbg:
        kb.op("dve", lambda e: e.reciprocal(rstd[:, :TB], rstd[:, :TB]), reads=[rstd], writes=[rstd])


def norm_all_tokens(P, st, hT, gain, nT, dname="hT"):
    kb = P.kb
    TB = 512
    xr = st.ring(2, [128, DC, TB], F32, "nx")
    sqr = st.ring(3, [128, TB], F32, "nsq")
    rr = st.ring(2, [128, TB], F32, "nr")
    hv = hT.rearrange("(c p) t -> p c t", p=128)
    for tb in range(S // TB):
        xt = xr.next()
        for half in range(2):
            cs = slice(half * 8, half * 8 + 8)
            kb.dma("sp", xt[:, cs, :], hv[:, cs, tb * TB:(tb + 1) * TB], xt,
                   reads=[kb.d(dname, c, tb) for c in range(half * 8, half * 8 + 8)], writes=[xt])
        rstd = rr.next()
        rms_stats(P, st, xt, TB, sqr, rstd)
        for c in range(DC):
            eng = "dve"
            kb.op(eng, lambda e, c=c, xt=xt, rstd=rstd: e.scalar_tensor_tensor(
                out=nT[:, c, tb * TB:(tb + 1) * TB], in0=xt[:, c, :], scalar=gain[:, c:c + 1], in1=rstd[:, :TB],
                op0=ALU.mult, op1=ALU.mult), reads=[xt, rstd, gain], writes=[nT])


def load_vec(P, st, src_ap, shape, name, q="sp"):
    t = st.tile(shape, F32, name)
    P.kb.dma(q, t[:, :], src_ap, t, reads=[], writes=[t])
    return t


def ffn_stage(P, l, which):
    kb, nc = P.kb, P.nc
    w_in = P.get(which + "_w_in", [P.lw, D, 2 * DFF])[l]
    w_out = P.get(which + "_w_out", [P.lw, DFF, D])[l]
    hT = P.hT
    FC = DFF // 128
    NQ = 4
    QC = FC // NQ
    TB = 512
    NTB = S // TB
    w_in_v = w_in.rearrange("(kc p) n -> p kc n", p=128)
    w_out_v = w_out.rearrange("(fc p) n -> p fc n", p=128)
    hv = hT.rearrange("(c p) t -> p c t", p=128)
    with Stage(kb, "%s%d" % (which, l)) as st:
        gain = load_vec(P, st, P.get(which + "_norm_t", [P.lw, 128, DC])[l], [128, DC], "g")
        nT = st.tile([128, DC, S], BF16, "nT")
        with Stage(kb, "%s%dn" % (which, l)) as st2:
            norm_all_tokens(P, st2, hT, gain, nT)
        aT = st.tile([128, QC, S], BF16, "aT")
        wg_r = st.ring(3, [128, DC, 128], BF16, "wg")
        wu_r = st.ring(3, [128, DC, 128], BF16, "wu")
        wo_r = st.ring(2, [128, QC, 128], BF16, "wo")
        sil_r = st.ring(3, [128, TB], F32, "sil")
        h_r = st.ring(4, [128, TB], F32, "h")
        for q in range(NQ):
            for jq in range(QC):
                j = q * QC + jq
                wg = wg_r.next()
                wu = wu_r.next()
                kb.dma("pool", wg[:, :, :], w_in_v[:, :, j * 128:(j + 1) * 128], wg, reads=[], writes=[wg])
                kb.dma("pool", wu[:, :, :], w_in_v[:, :, DFF + j * 128:DFF + (j + 1) * 128], wu, reads=[], writes=[wu])
                for tb in range(NTB):
                    ts_ = slice(tb * TB, (tb + 1) * TB)
                    pg = kb.next_psum()
                    kb.mm(pg, [(pg[:, :], wg[:, k, :], nT[:, k, ts_]) for k in range(DC)], reads=[wg, nT])
                    pu = kb.next_psum()
                    kb.mm(pu, [(pu[:, :], wu[:, k, :], nT[:, k, ts_]) for k in range(DC)], reads=[wu, nT])
                    sil = sil_r.next()
                    kb.op("act", lambda e, sil=sil, pg=pg: e.activation(out=sil[:, :], in_=pg[:, :], func=AF.Silu), reads=[pg], writes=[sil])
                    kb.op("dve", lambda e, sil=sil, pu=pu, jq=jq, ts_=ts_: e.tensor_tensor(aT[:, jq, ts_], sil[:, :], pu[:, :], ALU.mult),
                          reads=[sil, pu], writes=[aT])
            for i in range(DC):
                wo = wo_r.next()
                kb.dma("pool", wo[:, :, :], w_out_v[:, q * QC:(q + 1) * QC, i * 128:(i + 1) * 128], wo, reads=[], writes=[wo])
                for tb in range(NTB):
                    ts_ = slice(tb * TB, (tb + 1) * TB)
                    ht = h_r.next()
                    kb.dma("sp", ht[:, :], hv[:, i, ts_], ht, reads=[kb.d("hT", i, tb)], writes=[ht])
                    po = kb.next_psum()
                    kb.mm(po, [(po[:, :], wo[:, f, :], aT[:, f, ts_]) for f in range(QC)], reads=[wo, aT])
                    kb.op("dve", lambda e, ht=ht, po=po: e.scalar_tensor_tensor(out=ht[:, :], in0=po[:, :], scalar=0.5, in1=ht[:, :],
                                                                                  op0=ALU.mult, op1=ALU.add), reads=[po, ht], writes=[ht])
                    kb.dma("sp", hv[:, i, ts_], ht[:, :], ht, reads=[ht], writes=[kb.d("hT", i, tb)])


def proj_B(P, st, nT, w_v, col0, ncols, wring, epi):
    kb = P.kb
    nch = (ncols + 127) // 128
    for c in range(nch):
        m = min(128, ncols - c * 128)
        w = wring.next()
        kb.dma("pool", w[:, :, :m], w_v[:, :, col0 + c * 128:col0 + c * 128 + m], w, reads=[], writes=[w])
        for tb in range(S // 512):
            ps = kb.next_psum()
            kb.mm(ps, [(ps[:m, :], w[:, k, :m], nT[:, k, tb * 512:(tb + 1) * 512]) for k in range(DC)], reads=[w, nT])
            epi(c, m, tb, ps)


def proj_A(P, st, nT, w_v, col0, ncols, wring, epi):
    kb = P.kb
    w = wring.next()
    kb.dma("pool", w[:, :, :ncols], w_v[:, :, col0:col0 + ncols], w, reads=[], writes=[w])
    for tt in range(NT):
        ps = kb.next_psum()
        kb.mm(ps, [(ps[:, :ncols], nT[:, k, tt * 128:(tt + 1) * 128], w[:, k, :ncols]) for k in range(DC)], reads=[w, nT])
        epi(tt, ps)


def mixproj_stage(P, l):
    kb = P.kb
    w_in = P.get("w_in", [P.lw, D, N_IN])[l]
    w_v = w_in.rearrange("(kc p) n -> p kc n", p=128)
    sc = P.scr
    with Stage(kb, "mp%d" % l) as st:
        gain = load_vec(P, st, P.get("mix_norm_t", [P.lw, 128, DC])[l], [128, DC], "g")
        nT = st.tile([128, DC, S], BF16, "nT")
        with Stage(kb, "mp%dn" % l) as st2:
            norm_all_tokens(P, st2, P.hT, gain, nT)
        wB = st.ring(3, [128, DC, 128], BF16, "wB")
        wA = st.ring(2, [128, DC, 512], BF16, "wA")
        sB32 = st.ring(2, [128, S], F32, "sB32")
        sB16 = st.ring(2, [128, S], BF16, "sB16")
        sA32 = st.ring(3, [128, 512], F32, "sA32")
        sA16 = st.ring(3, [128, 512], BF16, "sA16")
        cnt = [0]

        def B(dst, col0, ncols, dt, func=None, scale=1.0):
            ring = sB32 if dt == F32 else sB16
            cur = [None]

            def epi(c, m, tb, ps):
                if tb == 0:
                    cur[0] = ring.next()
                stg = cur[0]
                cnt[0] += 1
                if func is not None:
                    kb.op("act", lambda e: e.activation(out=stg[:m, tb * 512:(tb + 1) * 512], in_=ps[:m, :], func=func, scale=scale), reads=[ps], writes=[stg])
                elif cnt[0] % 2 == 0:
                    kb.op("act", lambda e: e.mul(stg[:m, tb * 512:(tb + 1) * 512], ps[:m, :], scale), reads=[ps], writes=[stg])
                else:
                    kb.op("dve", lambda e: e.tensor_scalar_mul(stg[:m, tb * 512:(tb + 1) * 512], ps[:m, :], scale), reads=[ps], writes=[stg])
                if tb == S // 512 - 1:
                    kb.dma("sp", dst[c * 128:c * 128 + m, :], stg[:m, :], stg, reads=[stg], writes=[kb.d(dst.name, c)])
            proj_B(P, st, nT, w_v, col0, ncols, wB, epi)

        def A(dst, dcol0, col0, ncols, dt, func=None):
            ring = sA32 if dt == F32 else sA16

            def epi(tt, ps):
                stg = ring.next()
                cnt[0] += 1
                if func is not None:
                    kb.op("act", lambda e: e.activation(out=stg[:, :ncols], in_=ps[:, :ncols], func=func), reads=[ps], writes=[stg])
                elif cnt[0] % 2 == 0:
                    kb.op("act", lambda e: e.copy(stg[:, :ncols], ps[:, :ncols]), reads=[ps], writes=[stg])
                else:
                    kb.op("dve", lambda e: e.tensor_copy(stg[:, :ncols], ps[:, :ncols]), reads=[ps], writes=[stg])
                kb.dma("sp", dst[tt * 128:(tt + 1) * 128, dcol0:dcol0 + ncols], stg[:, :ncols], stg, reads=[stg], writes=[kb.d(dst.name, tt, dcol0)])
            proj_A(P, st, nT, w_v, col0, ncols, wA, epi)

        B(sc["uT"], O_SU, 1024, BF16, AF.Gelu_apprx_tanh)
        for hh in range(2):
            A(sc["v"], hh * 512, O_SV + hh * 512, 512, F32, AF.Gelu_apprx_tanh)
        B(sc["fqT"], O_FQ, 1024, BF16, None, 128 ** -0.5)
        B(sc["fkT"], O_FK, 1024, BF16)
        for hh in range(2):
            A(sc["fv"], hh * 512, O_FV + hh * 512, 512, BF16)
        A(sc["ff"], 0, O_FF, 8, F32)
        B(sc["gqT"], O_GQ, 512, F32, None, 128 ** -0.5)
        B(sc["gkT"], O_GK, 512, F32)
        A(sc["gk"], 0, O_GK, 512, F32)
        for hh in range(2):
            A(sc["gv"], hh * 512, O_GV + hh * 512, 512, BF16)
        B(sc["gaT"], O_GA, 16, F32)
        for hh in range(2):
            A(sc["gr"], hh * 512, O_GR + hh * 512, 512, F32, AF.Silu)
        B(sc["gates"], O_GATES, 6144, F32, AF.Sigmoid)


def fox_stage(P, l):
    kb = P.kb
    sc = P.scr
    H = 8
    qv = sc["fqT"].rearrange("(h p) t -> p h t", p=128)
    kv = sc["fkT"].rearrange("(h p) t -> p h t", p=128)
    vv = sc["fv"].rearrange("(c p) d -> p c d", p=128)
    ffv = sc["ff"].rearrange("(i p) h -> p i h", p=128)
    fo = sc["foT"]
    with Stage(kb, "fox%d" % l) as st:
        tri = P.tri_tile
        xf = st.tile([128, NT * H], F32, "xf")
        bfb = st.tile([128, NT * H], F32, "bfb")
        kb.dma("sp", xf[:, :].rearrange("p (i h) -> p i h", h=H), ffv, xf, reads=[], writes=[xf], allow_slow_non_contiguous=True)
        kb.dma("sp", bfb[:, :], P.get("fox_b_f_b", [P.lw, 128, NT * H])[l], bfb, reads=[], writes=[bfb])
        kb.op("dve", lambda e: e.tensor_tensor(xf[:, :], xf[:, :], bfb[:, :], ALU.add), reads=[xf, bfb], writes=[xf])
        kb.op("act", lambda e: e.activation(out=xf[:, :], in_=xf[:, :], func=AF.Exp, scale=-1.0), reads=[xf], writes=[xf])
        kb.op("act", lambda e: e.activation(out=xf[:, :], in_=xf[:, :], func=AF.Ln, bias=P.one_tile[:, 0:1], scale=1.0), reads=[xf, P.one_tile], writes=[xf])
        pre = st.tile([128, NT * H], F32, "pre")
        kb.op("dve", lambda e: e.memset(pre[:, 0:H], 0.0), writes=[pre])
        for i in range(1, NT):
            kb.op("dve", lambda e, i=i: e.tensor_tensor(pre[:, i * H:(i + 1) * H], pre[:, (i - 1) * H:i * H], xf[:, (i - 1) * H:i * H], ALU.add),
                  reads=[pre, xf], writes=[pre])
        ps = kb.next_psum()
        kb.mm(ps, [(ps[:, :NT * H], tri[:, :], xf[:, :]), (ps[:, :NT * H], P.ones_f32[:, :], pre[:, :])], reads=[tri, xf, pre, P.ones_f32])
        csb = st.tile([128, NT * H], F32, "csb")
        kb.op("dve", lambda e: e.tensor_copy(csb[:, :], ps[:, :NT * H]), reads=[ps], writes=[csb])
        ps2 = kb.next_psum()
        kb.mm(ps2, [(ps2[:, :NT * H], P.sel64_tile[:, :], csb[:, :])], reads=[P.sel64_tile, csb])
        cmid = st.tile([128, NT * H], F32, "cmid")
        kb.op("dve", lambda e: e.tensor_copy(cmid[:, :], ps2[:, :NT * H]), reads=[ps2], writes=[cmid])
        P.dump("fox_xf", xf, [128, NT * H])
        P.dump("fox_csb", csb, [128, NT * H])
        P.dump("fox_cmid", cmid, [128, NT * H])
        bias = st.tile([128, H, NT, NT], F32, "bias")
        cm3 = cmid[:, :].rearrange("p (i h) -> p h i", h=H)
        for h in range(H):
            for c in range(NT):
                kb.op("dve", lambda e, h=h, c=c: e.tensor_scalar(out=bias[:, h, c, :], in0=cm3[:, h, :], scalar1=csb[:, c * H + h:c * H + h + 1],
                                                                    scalar2=-1.0, op0=ALU.subtract, op1=ALU.mult), reads=[cmid, csb], writes=[bias])
        q_r = st.ring(2, [128, S], BF16, "q")
        k_r = st.ring(2, [128, S], BF16, "k")
        v_r = st.ring(2, [128, NT, 128], BF16, "v")
        e_r = st.ring(3, [128, 512], BF16, "e")
        rd_r = st.ring(2, [128, 512], F32, "rd")
        o_r = st.ring(2, [128, S], BF16, "o")
        accR = Ring(kb.psum[0:4])
        qkR = Ring(kb.psum[4:8])
        for h in range(H):
            qt, kt, vt = q_r.next(), k_r.next(), v_r.next()
            kb.dma("sp", qt[:, :], qv[:, h, :], qt, reads=[], writes=[qt])
            kb.dma("sp", kt[:, :], kv[:, h, :], kt, reads=[], writes=[kt])
            kb.dma("sp", vt[:, :, :], vv[:, :, h * 128:(h + 1) * 128], vt, reads=[], writes=[vt])
            ot = o_r.next()
            for j in range(4):
                po, pd = accR.next(), accR.next()
                nchunks = 4 * j + 4
                for c in range(nchunks):
                    r = c - 4 * j
                    sb0 = max(0, r)
                    t0 = sb0 * 128
                    ps = qkR.next()
                    kb.mm(ps, [(ps[:, t0:512], kt[:, c * 128:(c + 1) * 128], qt[:, j * 512 + t0:(j + 1) * 512])], reads=[kt, qt])
                    et = e_r.next()
                    for sb in range(sb0, 4):
                        b = 4 * j + sb
                        kb.op("act", lambda e, sb=sb, b=b, c=c, ps=ps, et=et, h=h: e.activation(
                            out=et[:, sb * 128:(sb + 1) * 128], in_=ps[:, sb * 128:(sb + 1) * 128], func=AF.Exp,
                            bias=bias[:, h, c, b:b + 1], scale=1.0), reads=[ps, bias], writes=[et])
                    if r >= 0:
                        kb.op("pool", lambda e, r=r, et=et: e.tensor_tensor(et[:, r * 128:(r + 1) * 128], et[:, r * 128:(r + 1) * 128], tri[:, :], ALU.mult),
                              reads=[et, tri], writes=[et])
                    first, last = (c == 0), (c == nchunks - 1)
                    kb.op("pe", lambda pe, po=po, vt=vt, c=c, et=et, t0=t0, first=first, last=last: pe.matmul(
                        po[:, t0:512], vt[:, c, :], et[:, t0:512], start=first, stop=last), reads=[vt, et], writes=[po])
                    kb.op("pe", lambda pe, pd=pd, et=et, t0=t0, first=first, last=last: pe.matmul(
                        pd[:, t0:512], P.ones_bf[:, :], et[:, t0:512], start=first, stop=last), reads=[P.ones_bf, et], writes=[pd])
                rd = rd_r.next()
                kb.op("dve", lambda e, rd=rd, pd=pd: e.reciprocal(rd[:, :], pd[:, :]), reads=[pd], writes=[rd])
                kb.op("dve", lambda e, rd=rd, po=po, ot=ot, j=j: e.tensor_tensor(ot[:, j * 512:(j + 1) * 512], po[:, :], rd[:, :], ALU.mult),
                      reads=[po, rd], writes=[ot])
            kb.dma("sp", fo[h * 128:(h + 1) * 128, :], ot[:, :], ot, reads=[ot], writes=[kb.d("foT", h)])


def sgu_stage(P, l):
    kb = P.kb
    sc = P.scr
    uv = sc["uT"].rearrange("(q p) t -> p q t", p=128)
    sav = sc["saT"].rearrange("(q p) t -> p q t", p=128)
    with Stage(kb, "sgu%d" % l) as st:
        lng = load_vec(P, st, P.get("sgu_ln_g_b", [P.lw, 128, 1024])[l], [128, 1024], "lng")
        lnb = load_vec(P, st, P.get("sgu_ln_b_b", [P.lw, 128, 1024])[l], [128, 1024], "lnb")
        bsb = load_vec(P, st, P.get("sgu_b_s_b", [P.lw, 128, 1024])[l], [128, 1024], "bsb")
        wsT = load_vec(P, st, P.get("sgu_w_sT", [P.lw, 128, 512])[l], [128, 512], "wsT")
        wm = st.tile([128, 4, 128], BF16, "wm")
        for g in range(4):
            kb.op("pool", lambda e, g=g: e.tensor_tensor(wm[:, g, :], wsT[:, g * 128:(g + 1) * 128], P.tri_tile[:, :], ALU.mult),
                  reads=[wsT, P.tri_tile], writes=[wm])
        v_r = st.ring(2, [128, 1024], F32, "v")
        sq_r = st.ring(2, [128, 1024], F32, "sq")
        vn_r = st.ring(2, [128, 1024], F32, "vn")
        vb_r = st.ring(2, [128, 1024], BF16, "vb")
        u_r = st.ring(2, [128, 8, 128], BF16, "u")
        o_r = st.ring(2, [128, 8, 128], BF16, "o")
        t_r = st.ring(2, [128, 512], F32, "t")
        s_r = st.ring(2, [128, 16], F32, "s")
        for tt in range(NT):
            vt = v_r.next()
            kb.dma("sp", vt[:, :], sc["v"][tt * 128:(tt + 1) * 128, :], vt, reads=[], writes=[vt])
            ut = u_r.next()
            kb.dma("sp", ut[:, :, :], uv[:, :, tt * 128:(tt + 1) * 128], ut, reads=[], writes=[ut])
            sq = sq_r.next()
            kb.op("pool", lambda e, sq=sq, vt=vt: e.tensor_tensor(sq[:, :], vt[:, :], vt[:, :], ALU.mult), reads=[vt], writes=[sq])
            s = s_r.next()
            kb.op("dve", lambda e, s=s, vt=vt: e.tensor_reduce(out=s[:, 0:4], in_=vt[:, :].rearrange("p (g c) -> p g c", g=4), axis=AX.X, op=ALU.add), reads=[vt], writes=[s])
            kb.op("dve", lambda e, s=s, sq=sq: e.tensor_reduce(out=s[:, 4:8], in_=sq[:, :].rearrange("p (g c) -> p g c", g=4), axis=AX.X, op=ALU.add), reads=[sq], writes=[s])
            kb.op("dve", lambda e, s=s: e.tensor_scalar_mul(s[:, 0:4], s[:, 0:4], 1.0 / 256), reads=[s], writes=[s])
            kb.op("dve", lambda e, s=s: e.tensor_tensor(s[:, 8:12], s[:, 0:4], s[:, 0:4], ALU.mult), reads=[s], writes=[s])
            kb.op("dve", lambda e, s=s: e.scalar_tensor_tensor(out=s[:, 4:8], in0=s[:, 4:8], scalar=1.0 / 256, in1=s[:, 8:12], op0=ALU.mult, op1=ALU.subtract), reads=[s], writes=[s])
            kb.op("act", lambda e, s=s: e.activation(out=s[:, 4:8], in_=s[:, 4:8], func=AF.Sqrt, bias=P.eps_tile[:, 0:1], scale=1.0), reads=[s, P.eps_tile], writes=[s])
            kb.op("dve", lambda e, s=s: e.reciprocal(s[:, 4:8], s[:, 4:8]), reads=[s], writes=[s])
            vn = vn_r.next()
            for g in range(4):
                kb.op("dve", lambda e, g=g, vn=vn, vt=vt, s=s: e.tensor_scalar(out=vn[:, g * 256:(g + 1) * 256], in0=vt[:, g * 256:(g + 1) * 256],
                      scalar1=s[:, g:g + 1], scalar2=s[:, 4 + g:5 + g], op0=ALU.subtract, op1=ALU.mult), reads=[vt, s], writes=[vn])
            kb.op("pool", lambda e, vn=vn: e.tensor_tensor(vn[:, :], vn[:, :], lng[:, :], ALU.mult), reads=[vn, lng], writes=[vn])
            vb = vb_r.next()
            kb.op("pool", lambda e, vn=vn, vb=vb: e.tensor_tensor(vb[:, :], vn[:, :], lnb[:, :], ALU.add), reads=[vn, lnb], writes=[vb])
            if tt == 0:
                P.dump("sgu_s", s, [128, 16])
                P.dump("sgu_vn", vn, [128, 1024])
                P.dump("sgu_v", vt, [128, 1024])
                P.dump("sgu_lng", lng, [128, 1024])
                P.dump("sgu_wsT", wsT, [128, 512])
                P.dump("sgu_bsb", bsb, [128, 1024])
            ot = o_r.next()
            for half in range(2):
                ps = kb.next_psum()

                def f(pe, ps=ps, vb=vb, half=half):
                    ins = None
                    for qq in range(4):
                        q = half * 4 + qq
                        ins = pe.matmul(ps[:, qq * 128:(qq + 1) * 128], vb[:, q * 128:(q + 1) * 128], wm[:, q // 2, :], start=True, stop=True)
                    return ins
                kb.op("pe", f, reads=[vb, wm], writes=[ps])
                t = t_r.next()
                kb.op("dve", lambda e, t=t, ps=ps, half=half: e.tensor_tensor(t[:, :], ps[:, :], bsb[:, half * 512:(half + 1) * 512], ALU.add), reads=[ps, bsb], writes=[t])
                kb.op("pool", lambda e, t=t, ot=ot, ut=ut, half=half: e.tensor_tensor(
                    ot[:, half * 4:(half + 1) * 4, :], t[:, :].rearrange("p (q t) -> p q t", q=4), ut[:, half * 4:(half + 1) * 4, :], ALU.mult),
                    reads=[t, ut], writes=[ot])
            kb.dma("sp", sav[:, :, tt * 128:(tt + 1) * 128], ot[:, :, :], ot, reads=[ot], writes=[kb.d("saT", tt)])


def gla_stage(P, l):
    kb = P.kb
    sc = P.scr
    H = 4
    gqv = sc["gqT"].rearrange("(h p) t -> p h t", p=128)
    gkv = sc["gkT"].rearrange("(h p) t -> p h t", p=128)
    gov = sc["goT"].rearrange("(q p) t -> p q t", p=128)
    with Stage(kb, "gla%d" % l) as st:
        TT = load_vec(P, st, P.get("c_TT", [128, 256])[:, :], [128, 256], "TT")
        REV = load_vec(P, st, P.get("c_rev", [128, 128])[:, :], [128, 128], "REV")
        M64 = load_vec(P, st, P.get("c_m64", [128, 512])[:, :], [128, 512], "M64")
        RM = load_vec(P, st, P.get("c_rowmask", [128, 2])[:, :], [128, 2], "RM")
        onb = load_vec(P, st, P.get("gla_o_norm_b", [P.lw, 128, 1024])[l], [128, 1024], "onb")
        aT = st.tile([32, S], F32, "aT")
        kb.op("dve", lambda e: e.memset(aT[:, :], 1.0), writes=[aT])
        kb.dma("sp", aT[0:16, :], sc["gaT"][:, :], aT, reads=[], writes=[aT])
        wg = st.tile([32, 512], F32, "wg")
        kb.op("dve", lambda e: e.memset(wg[:, :], 0.0), writes=[wg])
        kb.dma("sp", wg[0:16, :], P.get("gla_w_gate", [P.lw, 16, 512])[l], wg, reads=[], writes=[wg])
        kb.dma("sp", wg[16:17, :], P.get("gla_b_gate", [P.lw, 1, 512])[l], wg, reads=[], writes=[wg])
        S32 = [st.tile([128, 256], F32, "S32_%d" % h) for h in range(H)]
        S16_r = [st.ring(3, [128, 256], BF16, "S16_%d" % h) for h in range(H)]
        S16 = []
        for h in range(H):
            kb.op("pool", lambda e, h=h: e.memset(S32[h][:, :], 0.0), writes=[S32[h]])
            t0 = S16_r[h].next()
            kb.op("pool", lambda e, t0=t0: e.memset(t0[:, :], 0.0), writes=[t0])
            S16.append(t0)
        qiA_r = st.ring(2, [128, H, 128], BF16, "qiA")
        qiB_r = st.ring(2, [128, H, 128], BF16, "qiB")
        for r_ in (qiA_r, qiB_r):
            for t in r_.tiles:
                kb.op("pool", lambda e, t=t: e.memset(t[:, :, :], 0.0), writes=[t])
        sp_r = st.ring(2, [128, 512], F32, "sp")
        E_r = [st.ring(2, [128, H, 128], F32, "E%d" % i) for i in range(3)]
        q_r = st.ring(2, [128, H, 128], F32, "q")
        k_r = st.ring(2, [128, H, 128], F32, "k")
        qd_r = st.ring(2, [128, H, 128], BF16, "qd")
        kd_r = st.ring(2, [128, H, 128], BF16, "kd")
        E4_r = st.ring(2, [128, 512], F32, "E4")
        gk_r = st.ring(2, [128, 512], F32, "gk")
        ku_r = st.ring(4, [128, 512], BF16, "ku")
        gv_r = st.ring(2, [128, 1024], BF16, "gv")
        att_r = st.ring(2, [128, 512], BF16, "att")
        sq_r = st.ring(2, [128, 1024], F32, "sq")
        og_r = st.ring(2, [128, 1024], F32, "og")
        gr_r = st.ring(2, [128, 1024], F32, "gr")
        ss_r = st.ring(2, [128, 8], F32, "ss")
        go_r = st.ring(2, [128, 8, 128], BF16, "go")
        for tt in range(NT):
            tsl = slice(tt * 128, (tt + 1) * 128)
            ps = kb.next_psum()
            kb.mm(ps, [(ps[:, :], aT[:, tsl], wg[:, :])], reads=[aT, wg])
            sp = sp_r.next()
            kb.op("act", lambda e, sp=sp, ps=ps: e.activation(out=sp[:, :], in_=ps[:, :], func=AF.Exp, scale=-1.0), reads=[ps], writes=[sp])
            kb.op("act", lambda e, sp=sp: e.activation(out=sp[:, :], in_=sp[:, :], func=AF.Ln, bias=P.one_tile[:, 0:1], scale=1.0), reads=[sp, P.one_tile], writes=[sp])
            qt, kt, gkt, gvt, grt = q_r.next(), k_r.next(), gk_r.next(), gv_r.next(), gr_r.next()
            kb.dma("sp", qt[:, :, :], gqv[:, :, tsl], qt, reads=[], writes=[qt])
            kb.dma("sp", kt[:, :, :], gkv[:, :, tsl], kt, reads=[], writes=[kt])
            kb.dma("sp", gkt[:, :], sc["gk"][tsl, :], gkt, reads=[], writes=[gkt])
            kb.dma("sp", gvt[:, :], sc["gv"][tsl, :], gvt, reads=[], writes=[gvt])
            kb.dma("sp", grt[:, :], sc["gr"][tsl, :], grt, reads=[], writes=[grt])
            if GLA_STOP == 1:
                continue
            E1, E2, E3 = [r_.next() for r_ in E_r]
            for pair in range(2):
                psA = kb.next_psum()

                def fA(pe, psA=psA, sp=sp, pair=pair):
                    ins = None
                    for hh in range(2):
                        h = pair * 2 + hh
                        ins = pe.matmul(psA[:, hh * 256:(hh + 1) * 256], sp[:, h * 128:(h + 1) * 128], TT[:, :], start=True, stop=True)
                    return ins
                kb.op("pe", fA, reads=[sp, TT], writes=[psA])
                pv = psA[:, :].rearrange("p (h c) -> p h c", h=2)
                hs = slice(pair * 2, pair * 2 + 2)
                kb.op("act", lambda e, pv=pv, hs=hs, E1=E1: e.activation(out=E1[:, hs, :], in_=pv[:, :, 0:128], func=AF.Exp), reads=[psA], writes=[E1])
                kb.op("act", lambda e, pv=pv, hs=hs, E2=E2: e.activation(out=E2[:, hs, :], in_=pv[:, :, 128:256], func=AF.Exp), reads=[psA], writes=[E2])
                kb.op("act", lambda e, pv=pv, hs=hs, E3=E3: e.activation(out=E3[:, hs, :], in_=pv[:, :, 128:256], func=AF.Exp, scale=-1.0), reads=[psA], writes=[E3])
            qd, kd, qiA, qiB = qd_r.next(), kd_r.next(), qiA_r.next(), qiB_r.next()
            kb.op("dve", lambda e, qd=qd, qt=qt, E2=E2: e.tensor_tensor(qd[:, :, :], qt[:, :, :], E2[:, :, :], ALU.mult), reads=[qt, E2], writes=[qd])
            kb.op("pool", lambda e, kd=kd, kt=kt, E3=E3: e.tensor_tensor(kd[:, :, :], kt[:, :, :], E3[:, :, :], ALU.mult), reads=[kt, E3], writes=[kd])
            kb.op("dve", lambda e, qiA=qiA, qt=qt, E1=E1: e.tensor_tensor(qiA[:, :, 0:64], qt[:, :, 0:64], E1[:, :, 0:64], ALU.mult), reads=[qt, E1], writes=[qiA])
            kb.op("pool", lambda e, qiB=qiB, qt=qt, E1=E1: e.tensor_tensor(qiB[:, :, 64:128], qt[:, :, 64:128], E1[:, :, 64:128], ALU.mult), reads=[qt, E1], writes=[qiB])
            if GLA_STOP == 2:
                continue
            psR = kb.next_psum()
            kb.mm(psR, [(psR[:, :], REV[:, :], sp[:, :])], reads=[REV, sp])
            E4 = E4_r.next()
            kb.op("act", lambda e, E4=E4, psR=psR: e.activation(out=E4[:, :], in_=psR[:, :], func=AF.Exp), reads=[psR], writes=[E4])
            ku = [ku_r.next(), ku_r.next()]
            for a in range(2):
                kb.op("dve", lambda e, a=a, ku=ku, gkt=gkt, E4=E4: e.scalar_tensor_tensor(out=ku[a][:, :], in0=gkt[:, :], scalar=RM[:, a:a + 1], in1=E4[:, :],
                                                                                        op0=ALU.mult, op1=ALU.mult), reads=[gkt, E4, RM], writes=[ku[a]])
            if GLA_STOP == 3:
                continue
            psT = kb.next_psum()

            def fT(pe, psT=psT, kd=kd, qd=qd):
                ins = None
                for h in range(H):
                    ins = pe.matmul(psT[:, h * 128:(h + 1) * 128], kd[:, h, :], qd[:, h, :], start=True, stop=True)
                return ins
            kb.op("pe", fT, reads=[kd, qd], writes=[psT])
            att = att_r.next()
            kb.op("dve", lambda e, att=att, psT=psT: e.tensor_tensor(att[:, :], psT[:, :], M64[:, :], ALU.mult), reads=[psT, M64], writes=[att])
            if GLA_STOP == 4:
                continue
            psO = [kb.next_psum(), kb.next_psum()]
            for h in range(H):
                psU = kb.next_psum()

                def fU(pe, psU=psU, ku=ku, gvt=gvt, h=h):
                    ins = None
                    for a in range(2):
                        ins = pe.matmul(psU[:, a * 256:(a + 1) * 256], ku[a][:, h * 128:(h + 1) * 128],
                                        gvt[:, h * 256:(h + 1) * 256], start=True, stop=True)
                    return ins
                kb.op("pe", fU, reads=[ku[0], ku[1], gvt], writes=[psU])
                if GLA_STOP == 41:
                    continue
                Sn = S16[h]
                kb.op("dve", lambda e, h=h, psU=psU, E1=E1: e.scalar_tensor_tensor(out=S32[h][:, :], in0=S32[h][:, :], scalar=E1[:, h, 63:64], in1=psU[:, 0:256],
                                                                                    op0=ALU.mult, op1=ALU.add), reads=[S32[h], E1, psU], writes=[S32[h]])
                Sn1 = S16_r[h].next()
                kb.op("pool", lambda e, h=h, Sn1=Sn1: e.tensor_copy(Sn1[:, :], S32[h][:, :]), reads=[S32[h]], writes=[Sn1])
                kb.op("dve", lambda e, h=h, psU=psU, E1=E1: e.scalar_tensor_tensor(out=S32[h][:, :], in0=S32[h][:, :], scalar=E1[:, h, 127:128], in1=psU[:, 256:512],
                                                                                    op0=ALU.mult, op1=ALU.add), reads=[S32[h], E1, psU], writes=[S32[h]])
                Sn2 = S16_r[h].next()
                kb.op("pool", lambda e, h=h, Sn2=Sn2: e.tensor_copy(Sn2[:, :], S32[h][:, :]), reads=[S32[h]], writes=[Sn2])
                S16[h] = Sn2
                if GLA_STOP == 42:
                    continue
                po = psO[h // 2]
                oc = slice((h % 2) * 256, (h % 2 + 1) * 256)
                kb.mm(po, [(po[:, oc], att[:, h * 128:(h + 1) * 128], gvt[:, h * 256:(h + 1) * 256]),
                           (po[:, oc], qiA[:, h, :], Sn[:, :]),
                           (po[:, oc], qiB[:, h, :], Sn1[:, :])], reads=[att, gvt, qiA, qiB, Sn, Sn1])
            if GLA_STOP in (5, 41, 42):
                continue
            sq = sq_r.next()
            for i2 in range(2):
                kb.op("act", lambda e, sq=sq, i2=i2, psO=psO: e.activation(out=sq[:, i2 * 512:(i2 + 1) * 512], in_=psO[i2][:, :], func=AF.Square), reads=[psO[i2]], writes=[sq])
            ss = ss_r.next()
            kb.op("dve", lambda e, ss=ss, sq=sq: e.tensor_reduce(out=ss[:, 0:4], in_=sq[:, :].rearrange("p (g c) -> p g c", g=4), axis=AX.X, op=ALU.add), reads=[sq], writes=[ss])
            kb.op("act", lambda e, ss=ss: e.activation(out=ss[:, 0:4], in_=ss[:, 0:4], func=AF.Sqrt, bias=P.eps_tile[:, 0:1], scale=1.0 / 256), reads=[ss, P.eps_tile], writes=[ss])
            kb.op("dve", lambda e, ss=ss: e.reciprocal(ss[:, 0:4], ss[:, 0:4]), reads=[ss], writes=[ss])
            og = og_r.next()
            for h in range(H):
                kb.op("dve", lambda e, h=h, og=og, ss=ss, psO=psO: e.scalar_tensor_tensor(
                    out=og[:, h * 256:(h + 1) * 256], in0=psO[h // 2][:, (h % 2) * 256:(h % 2 + 1) * 256], scalar=ss[:, h:h + 1],
                    in1=onb[:, h * 256:(h + 1) * 256], op0=ALU.mult, op1=ALU.mult), reads=[psO[h // 2], ss, onb], writes=[og])
            kb.op("pool", lambda e, og=og, grt=grt: e.tensor_tensor(og[:, :], og[:, :], grt[:, :], ALU.mult), reads=[og, grt], writes=[og])
            if GLA_STOP == 6:
                continue
            go = go_r.next()
            for i2 in range(2):
                psX = kb.next_psum()

                def fX(pe, psX=psX, og=og, i2=i2):
                    ins = None
                    for qq in range(4):
                        q = i2 * 4 + qq
                        ins = pe.transpose(psX[:, qq * 128:(qq + 1) * 128], og[:, q * 128:(q + 1) * 128], P.ident_tile[:, :])
                    return ins
                kb.op("pe", fX, reads=[og, P.ident_tile], writes=[psX])
                pxv = psX[:, :].rearrange("p (q t) -> p q t", q=4)
                if i2 == 0:
                    kb.op("act", lambda e, go=go, pxv=pxv: e.copy(go[:, 0:4, :], pxv), reads=[psX], writes=[go])
                else:
                    kb.op("dve", lambda e, go=go, pxv=pxv: e.tensor_copy(go[:, 4:8, :], pxv), reads=[psX], writes=[go])
            kb.dma("sp", gov[:, :, tsl], go[:, :, :], go, reads=[go], writes=[kb.d("goT", tt)])


def xa_stage(P, l):
    kb = P.kb
    w_q = P.get("xa_w_q", [P.lw, D, 512])[l].rearrange("(kc p) n -> p kc n", p=128)
    w_kv = P.get("xa_w_kv", [P.lw, D, 1024])[l].rearrange("(kc p) n -> p kc n", p=128)
    w_o = P.get("xa_w_o", [P.lw, 512, D])[l].rearrange("(kc p) n -> p kc n", p=128)
    hv = P.hT.rearrange("(c p) t -> p c t", p=128)
    mv = P.memT.rearrange("(c p) t -> p c t", p=128)
    TB = 512
    with Stage(kb, "xa%d" % l) as st:
        gain = load_vec(P, st, P.get("xa_norm_t", [P.lw, 128, DC])[l], [128, DC], "g")
        mgain = load_vec(P, st, P.get("mem_norm_t", [P.lw, 128, DC])[l], [128, DC], "mg")
        kT = st.tile([128, 4, MEM], BF16, "kT")
        v16 = st.tile([128, 2, 512], BF16, "v16")
        q16 = st.tile([128, 4, S], BF16, "q16")
        oT = st.tile([128, 4, S], BF16, "oT")
        nT = st.tile([128, DC, S], BF16, "nT")
        with Stage(kb, "xa%dm" % l) as st2:
            mx = st2.tile([128, DC, MEM], F32, "mx")
            kb.dma("sp", mx[:, :, :], mv, mx, reads=[], writes=[mx])
            sqr = st2.ring(3, [128, 512], F32, "sq")
            rstd = st2.tile([128, 512], F32, "rstd")
            rms_stats(P, st2, mx, MEM, sqr, rstd)
            mn = st2.tile([128, DC, MEM], BF16, "mn")
            for c in range(DC):
                kb.op("dve", lambda e, c=c: e.scalar_tensor_tensor(out=mn[:, c, :], in0=mx[:, c, :], scalar=mgain[:, c:c + 1], in1=rstd[:, :MEM],
                                                                    op0=ALU.mult, op1=ALU.mult), reads=[mx, rstd, mgain], writes=[mn])
            wk_r = st2.ring(2, [128, DC, 128], BF16, "wk")
            for c in range(4):
                w = wk_r.next()
                kb.dma("pool", w[:, :, :], w_kv[:, :, c * 128:(c + 1) * 128], w, reads=[], writes=[w])
                ps = kb.next_psum()
                kb.mm(ps, [(ps[:, :MEM], w[:, k, :], mn[:, k, :]) for k in range(DC)], reads=[w, mn])
                kb.op("act", lambda e, c=c, ps=ps: e.copy(kT[:, c, :], ps[:, :MEM]), reads=[ps], writes=[kT])
            wv = st2.tile([128, DC, 512], BF16, "wv")
            kb.dma("pool", wv[:, :, :], w_kv[:, :, 512:1024], wv, reads=[], writes=[wv])
            for mt in range(2):
                ps = kb.next_psum()
                kb.mm(ps, [(ps[:, :], mn[:, k, mt * 128:(mt + 1) * 128], wv[:, k, :]) for k in range(DC)], reads=[wv, mn])
                kb.op("dve", lambda e, mt=mt, ps=ps: e.tensor_copy(v16[:, mt, :], ps[:, :]), reads=[ps], writes=[v16])
        with Stage(kb, "xa%dn" % l) as st2:
            norm_all_tokens(P, st2, P.hT, gain, nT)
        wq_r = st.ring(2, [128, DC, 128], BF16, "wq")
        for c in range(4):
            w = wq_r.next()
            kb.dma("pool", w[:, :, :], w_q[:, :, c * 128:(c + 1) * 128], w, reads=[], writes=[w])
            for tb in range(S // TB):
                ps = kb.next_psum()
                kb.mm(ps, [(ps[:, :], w[:, k, :], nT[:, k, tb * TB:(tb + 1) * TB]) for k in range(DC)], reads=[w, nT])
                kb.op("act", lambda e, c=c, tb=tb, ps=ps: e.mul(q16[:, c, tb * TB:(tb + 1) * TB], ps[:, :], 128 ** -0.5), reads=[ps], writes=[q16])
        e_r = st.ring(4, [128, TB], BF16, "e")
        rd_r = st.ring(2, [128, TB], F32, "rd")
        for h in range(4):
            for tb in range(S // TB):
                ts_ = slice(tb * TB, (tb + 1) * TB)
                es = []
                for mc in range(2):
                    ps = kb.next_psum()
                    kb.mm(ps, [(ps[:, :], kT[:, h, mc * 128:(mc + 1) * 128], q16[:, h, ts_])], reads=[kT, q16])
                    et = e_r.next()
                    kb.op("act", lambda e, et=et, ps=ps: e.activation(out=et[:, :], in_=ps[:, :], func=AF.Exp), reads=[ps], writes=[et])
                    es.append(et)
                po = kb.next_psum()
                kb.mm(po, [(po[:, :], v16[:, mc, h * 128:(h + 1) * 128], es[mc][:, :]) for mc in range(2)], reads=[v16] + es)
                pd = kb.next_psum()
                kb.mm(pd, [(pd[:, :], P.ones_bf[:, :], es[mc][:, :]) for mc in range(2)], reads=[P.ones_bf] + es)
                rd = rd_r.next()
                kb.op("dve", lambda e, rd=rd, pd=pd: e.reciprocal(rd[:, :], pd[:, :]), reads=[pd], writes=[rd])
                kb.op("dve", lambda e, rd=rd, po=po, h=h, ts_=ts_: e.tensor_tensor(oT[:, h, ts_], po[:, :], rd[:, :], ALU.mult), reads=[po, rd], writes=[oT])
        wo_r = st.ring(2, [128, 4, 128], BF16, "wo")
        h_r = st.ring(4, [128, TB], F32, "h")
        for i in range(DC):
            wo = wo_r.next()
            kb.dma("pool", wo[:, :, :], w_o[:, :, i * 128:(i + 1) * 128], wo, reads=[], writes=[wo])
            for tb in range(S // TB):
                ts_ = slice(tb * TB, (tb + 1) * TB)
                ht = h_r.next()
                kb.dma("sp", ht[:, :], hv[:, i, ts_], ht, reads=[kb.d("hT", i, tb)], writes=[ht])
                po = kb.next_psum()
                kb.mm(po, [(po[:, :], wo[:, k, :], oT[:, k, ts_]) for k in range(4)], reads=[wo, oT])
                kb.op("dve", lambda e, ht=ht, po=po: e.tensor_tensor(ht[:, :], po[:, :], ht[:, :], ALU.add), reads=[po, ht], writes=[ht])
                kb.dma("sp", hv[:, i, ts_], ht[:, :], ht, reads=[ht], writes=[kb.d("hT", i, tb)])


def merge_stage(P, l):
    kb = P.kb
    sc = P.scr
    wbr = [P.get("w_branch_" + x, [P.lw, 1024, D])[l].rearrange("(kc p) n -> p kc n", p=128) for x in "abc"]
    w_out = P.get("w_out", [P.lw, D, D])[l].rearrange("(kc p) n -> p kc n", p=128)
    brT = [sc[x].rearrange("(kc p) t -> p kc t", p=128) for x in ("saT", "foT", "goT")]
    gv = sc["gates"].rearrange("(g c p) t -> p g c t", p=128, g=3)
    hv = P.hT.rearrange("(c p) t -> p c t", p=128)
    TB = 512
    with Stage(kb, "mg%d" % l) as st:
        br_r = [st.ring(2, [128, 8, TB], BF16, "br%d" % i) for i in range(3)]
        wb_r = [st.ring(2, [128, 8, 128], BF16, "wb%d" % i) for i in range(3)]
        g_r = st.ring(2, [128, 3, TB], F32, "g")
        t_r = st.ring(2, [128, 3, TB], F32, "t")
        mT_r = st.ring(2, [128, DC, TB], BF16, "mT")
        wo_r = st.ring(2, [128, DC, 128], BF16, "wo")
        h_r = st.ring(4, [128, TB], F32, "h")
        for tb in range(S // TB):
            ts_ = slice(tb * TB, (tb + 1) * TB)
            br = [r.next() for r in br_r]
            for b in range(3):
                kb.dma("sp", br[b][:, :, :], brT[b][:, :, ts_], br[b], reads=[], writes=[br[b]])
            mT = mT_r.next()
            for i in range(DC):
                wb = [r.next() for r in wb_r]
                for b in range(3):
                    kb.dma("pool", wb[b][:, :, :], wbr[b][:, :, i * 128:(i + 1) * 128], wb[b], reads=[], writes=[wb[b]])
                g = g_r.next()
                kb.dma("sp", g[:, :, :], gv[:, :, i, ts_], g, reads=[], writes=[g])
                t = t_r.next()
                for b in range(3):
                    ps = kb.next_psum()
                    kb.mm(ps, [(ps[:, :], wb[b][:, k, :], br[b][:, k, :]) for k in range(8)], reads=[wb[b], br[b]])
                    kb.op("dve", lambda e, b=b, ps=ps, t=t, g=g: e.tensor_tensor(t[:, b, :], g[:, b, :], ps[:, :], ALU.mult), reads=[g, ps], writes=[t])
                kb.op("pool", lambda e, t=t: e.tensor_tensor(t[:, 0, :], t[:, 0, :], t[:, 1, :], ALU.add), reads=[t], writes=[t])
                kb.op("pool", lambda e, t=t, i=i, mT=mT: e.tensor_tensor(mT[:, i, :], t[:, 0, :], t[:, 2, :], ALU.add), reads=[t], writes=[mT])
            for i in range(DC):
                wo = wo_r.next()
                kb.dma("pool", wo[:, :, :], w_out[:, :, i * 128:(i + 1) * 128], wo, reads=[], writes=[wo])
                ht = h_r.next()
                kb.dma("sp", ht[:, :], hv[:, i, ts_], ht, reads=[kb.d("hT", i, tb)], writes=[ht])
                po = kb.next_psum()
                kb.mm(po, [(po[:, :], wo[:, k, :], mT[:, k, :]) for k in range(DC)], reads=[wo, mT])
                kb.op("dve", lambda e, ht=ht, po=po: e.tensor_tensor(ht[:, :], po[:, :], ht[:, :], ALU.add), reads=[po, ht], writes=[ht])
                kb.dma("sp", hv[:, i, ts_], ht[:, :], ht, reads=[ht], writes=[kb.d("hT", i, tb)])


def final_stage(P):
    kb = P.kb
    with Stage(kb, "fin") as st:
        gain = load_vec(P, st, P.get("final_norm_t", [128, DC]), [128, DC], "g")
        TB = 512
        xr = st.ring(2, [128, DC, TB], F32, "x")
        sqr = st.ring(3, [128, TB], F32, "sq")
        rr = st.ring(2, [128, TB], F32, "r")
        yr = st.ring(3, [128, TB], F32, "y")
        outr = st.ring(8, [128, D], F32, "o")
        hv = P.hT.rearrange("(c p) t -> p c t", p=128)
        for tb in range(S // TB):
            xt = xr.next()
            for half in range(2):
                cs = slice(half * 8, half * 8 + 8)
                kb.dma("sp", xt[:, cs, :], hv[:, cs, tb * TB:(tb + 1) * TB], xt,
                       reads=[kb.d("hT", c, tb) for c in range(half * 8, half * 8 + 8)], writes=[xt])
            rstd = rr.next()
            import os
            dbg = os.environ.get("KDBG", "")
            if "e" in dbg:
                kb.op("dve", lambda e: e.memset(rstd[:, :], 1.0), writes=[rstd])
            else:
                rms_stats(P, st, xt, TB, sqr, rstd)
            cur = [outr.next() for _ in range(4)]
            if "d" in dbg:
                continue
            for c in range(DC):
                y = yr.next()
                if "f" in dbg:
                    kb.op("dve", lambda e, c=c, y=y: e.tensor_tensor(y[:, :], xt[:, c, :], rstd[:, :TB], ALU.mult),
                          reads=[xt, rstd, gain], writes=[y])
                else:
                    kb.op("dve", lambda e, c=c, y=y: e.scalar_tensor_tensor(
                        out=y[:, :], in0=xt[:, c, :], scalar=gain[:, c:c + 1], in1=rstd[:, :TB], op0=ALU.mult, op1=ALU.mult),
                        reads=[xt, rstd, gain], writes=[y])
                ps = kb.next_psum()

                def f(pe, ps=ps, y=y):
                    ins = None
                    for j in range(4):
                        ins = pe.transpose(ps[:, j * 128:(j + 1) * 128], y[:, j * 128:(j + 1) * 128], P.ident_tile[:, :])
                    return ins
                kb.op("pe", f, reads=[y, P.ident_tile], writes=[ps])
                for j in range(4):
                    if c % 2 == 0:
                        kb.op("act", lambda e, j=j, ps=ps, c=c: e.copy(cur[j][:, c * 128:(c + 1) * 128], ps[:, j * 128:(j + 1) * 128]),
                              reads=[ps], writes=[cur[j]])
                    else:
                        kb.op("pool" if False else "dve", lambda e, j=j, ps=ps, c=c: e.tensor_copy(cur[j][:, c * 128:(c + 1) * 128], ps[:, j * 128:(j + 1) * 128]),
                              reads=[ps], writes=[cur[j]])
            for j in range(4):
                r0 = tb * TB + j * 128
                if "g" in dbg:
                    continue
                kb.dma("sp", P.out[r0:r0 + 128, :], cur[j][:, :], cur[j], reads=[cur[j]], writes=[kb.d("out", tb, j)])


def build(n_layers=L, stages=("ffn1", "mixproj", "sgu", "fox", "gla", "merge", "xa", "ffn2"), debug_out=None, lw=L):
    P = Prog(n_layers, stages, debug_out)
    P.lw = lw
    nc, kb = P.nc, P.kb
    x = P.get("x", [S, D])
    P.out = nc.dram_tensor("out", [S, D], F32, kind="ExternalOutput").ap()
    P.hT = P.dscr("hT", [D, S])
    P.scr = {}
    for nm, shp, dt in [("uT", [1024, S], BF16), ("v", [S, 1024], F32), ("fqT", [1024, S], BF16), ("fkT", [1024, S], BF16),
                        ("fv", [S, 1024], BF16), ("ff", [S, 8], F32), ("gqT", [512, S], F32), ("gkT", [512, S], F32),
                        ("gk", [S, 512], F32), ("gv", [S, 1024], BF16), ("gaT", [16, S], F32), ("gr", [S, 1024], F32),
                        ("gates", [6144, S], F32), ("saT", [1024, S], BF16), ("foT", [1024, S], BF16), ("goT", [1024, S], BF16)]:
        P.scr[nm] = P.dscr("s_" + nm, shp, dt)
    with ExitStack() as es:
        for i in range(8):
            t = es.enter_context(nc.psum_tensor("ps%d" % i, [128, 512], F32))
            kb.psum.append(Tile(t, Buf("ps%d" % i)))
        with Stage(kb, "glob") as g:
            P.ident_tile = g.tile([128, 128], F32, "ident")
            kb.dma("sp", P.ident_tile[:, :], P.get("ident", [128, 128])[:, :], P.ident_tile, writes=[P.ident_tile])
            P.ones_f32 = g.tile([128, 128], F32, "ones")
            kb.op("dve", lambda e: e.memset(P.ones_f32[:, :], 1.0), writes=[P.ones_f32])
            P.eps_tile = g.tile([128, 1], F32, "eps")
            kb.op("dve", lambda e: e.memset(P.eps_tile[:, :], EPS), writes=[P.eps_tile])
            P.one_tile = g.tile([128, 1], F32, "one")
            kb.op("dve", lambda e: e.memset(P.one_tile[:, :], 1.0), writes=[P.one_tile])
            P.ones_bf = g.tile([128, 128], BF16, "onesbf")
            kb.op("dve", lambda e: e.memset(P.ones_bf[:, :], 1.0), writes=[P.ones_bf])
            P.tri_tile = g.tile([128, 128], F32, "tri")
            kb.dma("sp", P.tri_tile[:, :], P.get("c_tri", [128, 128])[:, :], P.tri_tile, writes=[P.tri_tile])
            P.sel64_tile = g.tile([128, 128], F32, "sel64")
            kb.dma("sp", P.sel64_tile[:, :], P.get("c_sel64", [128, 128])[:, :], P.sel64_tile, writes=[P.sel64_tile])
            if "notr" not in stages:
                transpose_stage(P, x, P.hT, S, D, "x")
            if "xa" in stages:
                P.memT = P.dscr("memT", [D, MEM])
                transpose_stage(P, P.get("mem", [MEM, D]), P.memT, MEM, D, "mem")
            for l in range(n_layers):
                if "ffn1" in stages:
                    ffn_stage(P, l, "ffn1")
                if "mixproj" in stages:
                    mixproj_stage(P, l)
                if "sgu" in stages:
                    sgu_stage(P, l)
                if "fox" in stages:
                    fox_stage(P, l)
                if "gla" in stages:
                    gla_stage(P, l)
                if "merge" in stages:
                    merge_stage(P, l)
                if "xa" in stages:
                    xa_stage(P, l)
                if "ffn2" in stages:
                    ffn_stage(P, l, "ffn2")
            if "nofin" not in stages:
                final_stage(P)
    return P


def vt(v):
    v = np.asarray(v, np.float32)
    return np.ascontiguousarray(np.swapaxes(v.reshape(v.shape[:-1] + (-1, 128)), -1, -2))


def prep_inputs(P, inputs, b):
    m = {}
    for name in P.inp:
        if name == "x":
            m[name] = np.ascontiguousarray(inputs["x"][b])
        elif name == "mem":
            m[name] = np.ascontiguousarray(inputs["mem"][b])
        elif name == "ident":
            m[name] = np.eye(128, dtype=np.float32)
        elif name == "c_tri":
            m[name] = np.triu(np.ones((128, 128), np.float32))
        elif name in ("c_TT", "c_rev", "c_m64"):
            s_ = np.arange(128)[:, None]
            t_ = np.arange(128)[None, :]
            same = (s_ // 64) == (t_ // 64)
            tri64 = (same & (s_ <= t_)).astype(np.float32)
            refsel = (same & (s_ <= (t_ // 64) * 64 + 32)).astype(np.float32)
            if name == "c_TT":
                m[name] = np.ascontiguousarray(np.concatenate([-tri64 / 16.0, -(tri64 - refsel) / 16.0], axis=1))
            elif name == "c_rev":
                m[name] = np.ascontiguousarray(-(same & (s_ > t_)).astype(np.float32) / 16.0)
            else:
                m[name] = np.ascontiguousarray(np.tile(tri64, (1, 4)))
        elif name == "gla_o_norm_b":
            v = np.asarray(inputs["gla_o_norm"], np.float32).reshape(-1, 1, 1024)
            m[name] = np.ascontiguousarray(np.broadcast_to(v, (v.shape[0], 128, 1024)))
        elif name == "gla_b_gate":
            v = np.asarray(inputs["gla_b_gate"], np.float32)
            m[name] = np.ascontiguousarray(v.reshape(v.shape[0], 1, 512))
        elif name == "c_rowmask":
            c = np.zeros((128, 2), np.float32)
            c[:64, 0] = 1.0
            c[64:, 1] = 1.0
            m[name] = c
        elif name == "c_sel64":
            c = np.zeros((128, 128), np.float32)
            c[64, :] = 1.0
            m[name] = c
        elif name == "fox_b_f_b":
            v = np.asarray(inputs["fox_b_f"], np.float32)
            m[name] = np.ascontiguousarray(np.broadcast_to(np.tile(v, (1, NT))[:, None, :], (v.shape[0], 128, NT * 8)))
        elif name in ("sgu_ln_g_b", "sgu_ln_b_b"):
            v = np.asarray(inputs[name[:-2]], np.float32).reshape(-1, 1, 1024)
            m[name] = np.ascontiguousarray(np.broadcast_to(v, (v.shape[0], 128, 1024)))
        elif name == "sgu_b_s_b":
            v = np.asarray(inputs["sgu_b_s"], np.float32)
            v = np.repeat(v, 2, axis=1).reshape(v.shape[0], 1, 1024)
            m[name] = np.ascontiguousarray(np.broadcast_to(v, (v.shape[0], 128, 1024)))
        elif name == "sgu_w_sT":
            v = np.asarray(inputs["sgu_w_s"], np.float32)
            m[name] = np.ascontiguousarray(np.transpose(v, (0, 3, 1, 2)).reshape(v.shape[0], 128, 512))
        elif name.endswith("_t"):
            m[name] = vt(inputs[name[:-2]])
        else:
            m[name] = np.asarray(inputs[name])
    return m


def kernel(**inputs):
    P = build()
    in_maps = [prep_inputs(P, inputs, b) for b in range(8)]
    res = run_bass_kernel_spmd(P.nc, in_maps, core_ids=list(range(8)))
    return np.stack([r["out"] for r in res.results], axis=0)
```

```python
import numpy as np
from contextlib import ExitStack
import concourse.bass as bass
import concourse.mybir as mybir
from concourse.bass_utils import run_bass_kernel_spmd

F32 = mybir.dt.float32
BF16 = mybir.dt.bfloat16
AF = mybir.ActivationFunctionType
ALU = mybir.AluOpType
AX = mybir.AxisListType

D = 2048
S = 2048
L = 2
MEM = 256
DFF = 5632
NT = S // 128
DC = D // 128
EPS = 1e-6
N_IN = 14360
O_SU, O_SV, O_FQ, O_FK, O_FV, O_FF = 0, 1024, 2048, 3072, 4096, 5120
O_GQ, O_GK, O_GV, O_GA, O_GR, O_GATES = 5128, 5640, 6152, 7176, 7192, 8216

import os
GLA_STOP = int(os.environ.get("GLA_STOP", "0"))
SCOPES = bool(os.environ.get("KSCOPES", ""))
COMPUTE = ("pe", "act", "dve", "pool")


class Buf:
    __slots__ = ("name", "last_w", "readers", "dsem", "last_dma")

    def __init__(self, name):
        self.name = name
        self.last_w = None
        self.readers = {}
        self.dsem = None
        self.last_dma = None


class Tile:
    def __init__(self, t, buf):
        self.t = t
        self.b = buf

    def __getitem__(self, k):
        return self.t[k]


class KB:
    def __init__(self, nc):
        self.nc = nc
        self.eng = {"pe": nc.tensor, "act": nc.scalar, "dve": nc.vector, "pool": nc.gpsimd, "sp": nc.sync}
        self.semh = {}
        self.cnt = {}
        for e in COMPUTE:
            self.semh[e] = nc.alloc_semaphore(name="prog_" + e)
            self.cnt[e] = 0
        self.waited = {e: {} for e in self.eng}
        self.free_dsems = []
        for i in range(80):
            k = "d%d" % i
            self.semh[k] = nc.alloc_semaphore(name="dma_%d" % i)
            self.cnt[k] = 0
            self.free_dsems.append(k)
        self.stage_dsems = []
        self.dbufs = {}
        self.psum = []
        self.psum_i = 0
        self.n_inst = 0

    def d(self, name, *idx):
        key = (name,) + idx
        b = self.dbufs.get(key)
        if b is None:
            b = Buf(str(key))
            self.dbufs[key] = b
        return b

    def next_psum(self):
        p = self.psum[self.psum_i % len(self.psum)]
        self.psum_i += 1
        return p

    def _need(self, reads, writes):
        need = {}
        raw = {}

        def add(d, ev):
            if ev is not None and d.get(ev[0], 0) < ev[1]:
                d[ev[0]] = ev[1]

        for b in reads:
            add(need, b.last_w)
            add(raw, b.last_w)
        for b in writes:
            add(need, b.last_w)
            for k, v in b.readers.items():
                add(need, (k, v))
        return need, raw

    def _wait(self, e, need_raw, skip_self):
        need, raw = need_raw
        w = self.waited[e]
        for k, v in need.items():
            if skip_self and k == e:
                v = raw.get(k, 0)
                if v == 0 or e == "pe":
                    continue
            if w.get(k, 0) >= v:
                continue
            self.eng[e].wait_ge(self.semh[k], v)
            w[k] = v

    def _record(self, ev, reads, writes):
        for b in reads:
            if b.readers.get(ev[0], 0) < ev[1]:
                b.readers[ev[0]] = ev[1]
        for b in writes:
            b.last_w = ev
            b.readers = {}

    def op(self, e, fn, reads=(), writes=()):
        reads = [x.b if isinstance(x, Tile) else x for x in reads]
        writes = [x.b if isinstance(x, Tile) else x for x in writes]
        self._wait(e, self._need(reads, writes), True)
        ins = fn(self.eng[e])
        self.cnt[e] += 1
        ins.then_inc(self.semh[e], 1)
        self._record((e, self.cnt[e]), reads, writes)
        self.n_inst += 1

    def mm(self, ps, pairs, reads, extra_writes=()):
        reads = [x.b if isinstance(x, Tile) else x for x in reads]
        writes = [ps.b] + [x.b if isinstance(x, Tile) else x for x in extra_writes]
        self._wait("pe", self._need(reads, writes), True)
        n = len(pairs)
        ins = None
        for i, (o, l, r) in enumerate(pairs):
            ins = self.nc.tensor.matmul(o, l, r, start=(i == 0), stop=(i == n - 1))
        self.cnt["pe"] += 1
        ins.then_inc(self.semh["pe"], 1)
        self._record(("pe", self.cnt["pe"]), reads, writes)
        self.n_inst += n

    def mm_raw(self, fn, reads, writes):
        self.op("pe", fn, reads, writes)

    def dma(self, q, out, in_, sb, reads=(), writes=(), **kw):
        sbb = sb.b if isinstance(sb, Tile) else sb
        reads = [x.b if isinstance(x, Tile) else x for x in reads]
        writes = [x.b if isinstance(x, Tile) else x for x in writes]
        if sbb.dsem is None:
            sbb.dsem = self.free_dsems.pop()
            self.stage_dsems.append(sbb.dsem)
        need, raw = self._need(reads, writes)
        if sbb.last_dma is not None:
            ev = sbb.last_dma
            if need.get(ev[0], 0) < ev[1]:
                need[ev[0]] = ev[1]
        self._wait(q, (need, raw), False)
        ins = self.eng[q].dma_start(out=out, in_=in_, **kw)
        k = sbb.dsem
        self.cnt[k] += 16
        ins.then_inc(self.semh[k], 16)
        ev = (k, self.cnt[k])
        sbb.last_dma = ev
        self._record(ev, reads, writes)
        self.n_inst += 1

    def barrier(self):
        need = {k: v for k, v in self.cnt.items() if v > 0}
        for e in self.eng:
            self._wait(e, (need, {}), True)
        self.free_dsems.extend(self.stage_dsems)
        self.stage_dsems = []


class Stage:
    def __init__(self, kb, name):
        self.kb = kb
        self.name = name
        self.es = ExitStack()
        self.n = 0

    def __enter__(self):
        self.es.__enter__()
        if SCOPES:
            self.es.enter_context(self.kb.nc.named_scope(self.name))
        return self

    def __exit__(self, *a):
        self.kb.barrier()
        return self.es.__exit__(*a)

    def tile(self, shape, dtype, name=None):
        self.n += 1
        nm = "%s_%s%d" % (self.name, name or "t", self.n)
        t = self.es.enter_context(self.kb.nc.sbuf_tensor(nm, list(shape), dtype))
        return Tile(t, Buf(nm))

    def ring(self, n, shape, dtype, name=None):
        return Ring([self.tile(shape, dtype, name) for _ in range(n)])


class Ring:
    def __init__(self, tiles):
        self.tiles = tiles
        self.i = 0

    def next(self):
        t = self.tiles[self.i % len(self.tiles)]
        self.i += 1
        return t


class Prog:
    def __init__(self, n_layers=L, stages=None, debug_out=None):
        self.nc = nc = bass.Bass("TRN2", target_bir_lowering=False)
        self.kb = KB(nc)
        self.n_layers = n_layers
        self.inp = {}
        self.debug_out = debug_out or {}

    def get(self, name, shape, dtype=F32):
        if name not in self.inp:
            self.inp[name] = self.nc.dram_tensor(name, list(shape), dtype, kind="ExternalInput").ap()
        return self.inp[name]

    def dump(self, name, tile, shape, dtype=F32):
        if ("dbg_" + name) not in self.debug_out:
            return
        ap = self.nc.dram_tensor("dbg_" + name, list(shape), dtype, kind="ExternalOutput").ap()
        self.kb.dma("sp", ap, tile.t[tuple(slice(None) for _ in shape)], tile, reads=[tile], writes=[self.kb.d("dbg_" + name)])

    def dscr(self, name, shape, dtype=F32):
        kind = "ExternalOutput" if name in self.debug_out else "Internal"
        return self.nc.dram_tensor(name, list(shape), dtype, kind=kind).ap()


def transpose_stage(P, src, dst, rows, cols, sname):
    kb, nc = P.kb, P.nc
    with Stage(kb, "tr" + sname) as st:
        ident = P.ident_tile
        inr = st.ring(2, [128, cols], F32, "in")
        outr = st.ring(3, [128, 512], F32, "o")
        for r in range(rows // 128):
            it = inr.next()
            kb.dma("sp", it[:, :], src[r * 128:(r + 1) * 128, :], it, reads=[kb.d(sname + "src", r)], writes=[it])
            for c4 in range(cols // 512):
                ps = kb.next_psum()

                def f(pe, ps=ps, it=it, c4=c4):
                    ins = None
                    for c in range(4):
                        ins = pe.transpose(ps[:, c * 128:(c + 1) * 128], it[:, (c4 * 4 + c) * 128:(c4 * 4 + c + 1) * 128], ident[:, :])
                    return ins
                kb.op("pe", f, reads=[it, ident], writes=[ps])
                ot = outr.next()
                eng = "act" if (c4 % 2 == 0) else "dve"
                if eng == "act":
                    kb.op("act", lambda e, ot=ot, ps=ps: e.copy(ot[:, :], ps[:, :]), reads=[ps], writes=[ot])
                else:
                    kb.op("dve", lambda e, ot=ot, ps=ps: e.tensor_copy(ot[:, :], ps[:, :]), reads=[ps], writes=[ot])
                dview = dst[c4 * 512:(c4 + 1) * 512, r * 128:(r + 1) * 128].rearrange("(c p) j -> p c j", p=128)
                kb.dma("sp", dview, ot[:, :].rearrange("p (c j) -> p c j", c=4), ot,
                       reads=[ot], writes=[kb.d(sname + "dst", c4, r)])


def rms_stats(P, st, xt, TB, sq_ring, rstd):
    kb = P.kb
    ps = kb.next_psum()
    sqs = []
    for c in range(DC):
        sq = sq_ring.next()
        import os
        if c % 2 == 0 or "c" in os.environ.get("KDBG", ""):
            kb.op("act", lambda e, sq=sq, c=c: e.activation(out=sq[:, :TB], in_=xt[:, c, :], func=AF.Square), reads=[xt], writes=[sq])
        else:
            kb.op("pool", lambda e, sq=sq, c=c: e.tensor_tensor(sq[:, :TB], xt[:, c, :], xt[:, c, :], ALU.mult), reads=[xt], writes=[sq])

        def f(pe, sq=sq, c=c, ps=ps):
            return pe.matmul(ps[:, :TB], P.ones_f32[:, :], sq[:, :TB], start=(c == 0), stop=(c == DC - 1))
        kb.op("pe", f, reads=[sq, P.ones_f32], writes=[ps])
    import os
    dbg = os.environ.get("KDBG", "")
    if "a" in dbg:
        kb.op("act", lambda e: e.copy(rstd[:, :TB], ps[:, :TB]), reads=[ps, P.eps_tile], writes=[rstd])
    else:
        kb.op("act", lambda e: e.activation(out=rstd[:, :TB], in_=ps[:, :TB], func=AF.Sqrt, bias=P.eps_tile[:, 0:1], scale=1.0 / D), reads=[ps, P.eps_tile], writes=[rstd])
    if "b" not in dbg:
        kb.op("dve", lambda e: e.reciprocal(rstd[:, :TB], rstd[:, :TB]), reads=[rstd], writes=[rstd])


def norm_all_tokens(P, st, hT, gain, nT, dname="hT"):
    kb = P.kb
    TB = 512
    xr = st.ring(2, [128, DC, TB], F32, "nx")
    sqr = st.ring(3, [128, TB], F32, "nsq")
    rr = st.ring(2, [128, TB], F32, "nr")
    hv = hT.rearrange("(c p) t -> p c t", p=128)
    for tb in range(S // TB):
        xt = xr.next()
        for half in range(2):
            cs = slice(half * 8, half * 8 + 8)
            kb.dma("sp", xt[:, cs, :], hv[:, cs, tb * TB:(tb + 1) * TB], xt,
                   reads=[kb.d(dname, c, tb) for c in range(half * 8, half * 8 + 8)], writes=[xt])
        rstd = rr.next()
        rms_stats(P, st, xt, TB, sqr, rstd)
        for c in range(DC):
            eng = "dve"
            kb.op(eng, lambda e, c=c, xt=xt, rstd=rstd: e.scalar_tensor_tensor(
                out=nT[:, c, tb * TB:(tb + 1) * TB], in0=xt[:, c, :], scalar=gain[:, c:c + 1], in1=rstd[:, :TB],
                op0=ALU.mult, op1=ALU.mult), reads=[xt, rstd, gain], writes=[nT])


def load_vec(P, st, src_ap, shape, name, q="sp"):
    t = st.tile(shape, F32, name)
    P.kb.dma(q, t[:, :], src_ap, t, reads=[], writes=[t])
    return t


def h_update_loop(P, items, mm_fn, scale, h_r, TB=512):
    kb = P.kb
    hv = P.hT.rearrange("(c p) t -> p c t", p=128)
    tiles = {}

    def load(n):
        i, tb = items[n]
        ht = h_r.next()
        kb.dma("sp", ht[:, :], hv[:, i, tb * TB:(tb + 1) * TB], ht, reads=[kb.d("hT", i, tb)], writes=[ht])
        tiles[n] = ht
    LOOK = 3
    for n in range(min(LOOK, len(items))):
        load(n)
    for n, (i, tb) in enumerate(items):
        if n + LOOK < len(items):
            load(n + LOOK)
        po = mm_fn(i, tb)
        ht = tiles.pop(n)
        if scale == 1.0:
            kb.op("dve", lambda e: e.tensor_tensor(ht[:, :], po[:, :], ht[:, :], ALU.add), reads=[po, ht], writes=[ht])
        else:
            kb.op("dve", lambda e: e.scalar_tensor_tensor(out=ht[:, :], in0=po[:, :], scalar=scale, in1=ht[:, :], op0=ALU.mult, op1=ALU.add),
                  reads=[po, ht], writes=[ht])
        kb.dma("sp", hv[:, i, tb * TB:(tb + 1) * TB], ht[:, :], ht, reads=[ht], writes=[kb.d("hT", i, tb)])


def ffn_stage(P, l, which):
    kb, nc = P.kb, P.nc
    w_in = P.get(which + "_w_in", [P.lw, D, 2 * DFF])[l]
    w_out = P.get(which + "_w_out", [P.lw, DFF, D])[l]
    hT = P.hT
    FC = DFF // 128
    NQ = 4
    QC = FC // NQ
    TB = 512
    NTB = S // TB
    w_in_v = w_in.rearrange("(kc p) n -> p kc n", p=128)
    w_out_v = w_out.rearrange("(fc p) n -> p fc n", p=128)
    hv = hT.rearrange("(c p) t -> p c t", p=128)
    with Stage(kb, "%s%d" % (which, l)) as st:
        gain = load_vec(P, st, P.get(which + "_norm_t", [P.lw, 128, DC])[l], [128, DC], "g")
        nT = st.tile([128, DC, S], BF16, "nT")
        with Stage(kb, "%s%dn" % (which, l)) as st2:
            norm_all_tokens(P, st2, hT, gain, nT)
        aT = st.tile([128, QC, S], BF16, "aT")
        wg_r = st.ring(3, [128, DC, 128], BF16, "wg")
        wu_r = st.ring(3, [128, DC, 128], BF16, "wu")
        wo_r = st.ring(2, [128, QC, 128], BF16, "wo")
        sil_r = st.ring(3, [128, TB], F32, "sil")
        h_r = st.ring(6, [128, TB], F32, "h")
        for q in range(NQ):
            for jq in range(QC):
                j = q * QC + jq
                wg = wg_r.next()
                wu = wu_r.next()
                kb.dma("pool", wg[:, :, :], w_in_v[:, :, j * 128:(j + 1) * 128], wg, reads=[], writes=[wg])
                kb.dma("pool", wu[:, :, :], w_in_v[:, :, DFF + j * 128:DFF + (j + 1) * 128], wu, reads=[], writes=[wu])
                for tb in range(NTB):
                    ts_ = slice(tb * TB, (tb + 1) * TB)
                    pg = kb.next_psum()
                    kb.mm(pg, [(pg[:, :], wg[:, k, :], nT[:, k, ts_]) for k in range(DC)], reads=[wg, nT])
                    pu = kb.next_psum()
                    kb.mm(pu, [(pu[:, :], wu[:, k, :], nT[:, k, ts_]) for k in range(DC)], reads=[wu, nT])
                    sil = sil_r.next()
                    kb.op("act", lambda e, sil=sil, pg=pg: e.activation(out=sil[:, :], in_=pg[:, :], func=AF.Silu), reads=[pg], writes=[sil])
                    kb.op("dve", lambda e, sil=sil, pu=pu, jq=jq, ts_=ts_: e.tensor_tensor(aT[:, jq, ts_], sil[:, :], pu[:, :], ALU.mult),
                          reads=[sil, pu], writes=[aT])
            wcur = {}

            def mm_fn(i, tb, q=q):
                if tb == 0:
                    wo = wo_r.next()
                    kb.dma("pool", wo[:, :, :], w_out_v[:, q * QC:(q + 1) * QC, i * 128:(i + 1) * 128], wo, reads=[], writes=[wo])
                    wcur[0] = wo
                wo = wcur[0]
                po = kb.next_psum()
                kb.mm(po, [(po[:, :], wo[:, f, :], aT[:, f, tb * TB:(tb + 1) * TB]) for f in range(QC)], reads=[wo, aT])
                return po
            h_update_loop(P, [(i, tb) for i in range(DC) for tb in range(NTB)], mm_fn, 0.5, h_r)


def proj_B(P, st, nT, w_v, col0, ncols, wring, epi):
    kb = P.kb
    nch = (ncols + 127) // 128
    for c in range(nch):
        m = min(128, ncols - c * 128)
        w = wring.next()
        kb.dma("pool", w[:, :, :m], w_v[:, :, col0 + c * 128:col0 + c * 128 + m], w, reads=[], writes=[w])
        for tb in range(S // 512):
            ps = kb.next_psum()
            kb.mm(ps, [(ps[:m, :], w[:, k, :m], nT[:, k, tb * 512:(tb + 1) * 512]) for k in range(DC)], reads=[w, nT])
            epi(c, m, tb, ps)


def proj_A(P, st, nT, w_v, col0, ncols, wring, epi):
    kb = P.kb
    w = wring.next()
    kb.dma("pool", w[:, :, :ncols], w_v[:, :, col0:col0 + ncols], w, reads=[], writes=[w])
    for tt in range(NT):
        ps = kb.next_psum()
        kb.mm(ps, [(ps[:, :ncols], nT[:, k, tt * 128:(tt + 1) * 128], w[:, k, :ncols]) for k in range(DC)], reads=[w, nT])
        epi(tt, ps)


def mixproj_stage(P, l):
    kb = P.kb
    w_in = P.get("w_in", [P.lw, D, N_IN])[l]
    w_v = w_in.rearrange("(kc p) n -> p kc n", p=128)
    sc = P.scr
    with Stage(kb, "mp%d" % l) as st:
        gain = load_vec(P, st, P.get("mix_norm_t", [P.lw, 128, DC])[l], [128, DC], "g")
        nT = st.tile([128, DC, S], BF16, "nT")
        with Stage(kb, "mp%dn" % l) as st2:
            norm_all_tokens(P, st2, P.hT, gain, nT)
        wB = st.ring(3, [128, DC, 128], BF16, "wB")
        wA = st.ring(2, [128, DC, 512], BF16, "wA")
        sB32 = st.ring(2, [128, S], F32, "sB32")
        sB16 = st.ring(2, [128, S], BF16, "sB16")
        sA32 = st.ring(3, [128, 512], F32, "sA32")
        sA16 = st.ring(3, [128, 512], BF16, "sA16")
        cnt = [0]

        def B(dst, col0, ncols, dt, func=None, scale=1.0):
            ring = sB32 if dt == F32 else sB16
            cur = [None]

            def epi(c, m, tb, ps):
                if tb == 0:
                    cur[0] = ring.next()
                stg = cur[0]
                cnt[0] += 1
                if func is not None:
                    kb.op("act", lambda e: e.activation(out=stg[:m, tb * 512:(tb + 1) * 512], in_=ps[:m, :], func=func, scale=scale), reads=[ps], writes=[stg])
                elif cnt[0] % 2 == 0:
                    kb.op("act", lambda e: e.mul(stg[:m, tb * 512:(tb + 1) * 512], ps[:m, :], scale), reads=[ps], writes=[stg])
                else:
                    kb.op("dve", lambda e: e.tensor_scalar_mul(stg[:m, tb * 512:(tb + 1) * 512], ps[:m, :], scale), reads=[ps], writes=[stg])
                if tb == S // 512 - 1:
                    kb.dma("sp", dst[c * 128:c * 128 + m, :], stg[:m, :], stg, reads=[stg], writes=[kb.d(dst.name, c)])
            proj_B(P, st, nT, w_v, col0, ncols, wB, epi)

        def A(dst, dcol0, col0, ncols, dt, func=None):
            ring = sA32 if dt == F32 else sA16

            def epi(tt, ps):
                stg = ring.next()
                cnt[0] += 1
                if func is not None:
                    kb.op("act", lambda e: e.activation(out=stg[:, :ncols], in_=ps[:, :ncols], func=func), reads=[ps], writes=[stg])
                elif cnt[0] % 2 == 0:
                    kb.op("act", lambda e: e.copy(stg[:, :ncols], ps[:, :ncols]), reads=[ps], writes=[stg])
                else:
                    kb.op("dve", lambda e: e.tensor_copy(stg[:, :ncols], ps[:, :ncols]), reads=[ps], writes=[stg])
                kb.dma("sp", dst[tt * 128:(tt + 1) * 128, dcol0:dcol0 + ncols], stg[:, :ncols], stg, reads=[stg], writes=[kb.d(dst.name, tt, dcol0)])
            proj_A(P, st, nT, w_v, col0, ncols, wA, epi)

        B(sc["uT"], O_SU, 1024, BF16, AF.Gelu_apprx_tanh)
        for hh in range(2):
            A(sc["v"], hh * 512, O_SV + hh * 512, 512, F32, AF.Gelu_apprx_tanh)
        B(sc["fqT"], O_FQ, 1024, BF16, None, 128 ** -0.5)
        B(sc["fkT"], O_FK, 1024, BF16)
        for hh in range(2):
            A(sc["fv"], hh * 512, O_FV + hh * 512, 512, BF16)
        A(sc["ff"], 0, O_FF, 8, F32)
        B(sc["gqT"], O_GQ, 512, F32, None, 128 ** -0.5)
        B(sc["gkT"], O_GK, 512, F32)
        A(sc["gk"], 0, O_GK, 512, F32)
        for hh in range(2):
            A(sc["gv"], hh * 512, O_GV + hh * 512, 512, BF16)
        B(sc["gaT"], O_GA, 16, F32)
        for hh in range(2):
            A(sc["gr"], hh * 512, O_GR + hh * 512, 512, F32, AF.Silu)
        B(sc["gates"], O_GATES, 6144, F32, AF.Sigmoid)


def fox_stage(P, l):
    kb = P.kb
    sc = P.scr
    H = 8
    qv = sc["fqT"].rearrange("(h p) t -> p h t", p=128)
    kv = sc["fkT"].rearrange("(h p) t -> p h t", p=128)
    vv = sc["fv"].rearrange("(c p) d -> p c d", p=128)
    ffv = sc["ff"].rearrange("(i p) h -> p i h", p=128)
    fo = sc["foT"]
    with Stage(kb, "fox%d" % l) as st:
        tri = P.tri_tile
        xf = st.tile([128, NT * H], F32, "xf")
        bfb = st.tile([128, NT * H], F32, "bfb")
        kb.dma("sp", xf[:, :].rearrange("p (i h) -> p i h", h=H), ffv, xf, reads=[], writes=[xf], allow_slow_non_contiguous=True)
        kb.dma("sp", bfb[:, :], P.get("fox_b_f_b", [P.lw, 128, NT * H])[l], bfb, reads=[], writes=[bfb])
        kb.op("dve", lambda e: e.tensor_tensor(xf[:, :], xf[:, :], bfb[:, :], ALU.add), reads=[xf, bfb], writes=[xf])
        kb.op("act", lambda e: e.activation(out=xf[:, :], in_=xf[:, :], func=AF.Exp, scale=-1.0), reads=[xf], writes=[xf])
        kb.op("act", lambda e: e.activation(out=xf[:, :], in_=xf[:, :], func=AF.Ln, bias=P.one_tile[:, 0:1], scale=1.0), reads=[xf, P.one_tile], writes=[xf])
        pre = st.tile([128, NT * H], F32, "pre")
        kb.op("dve", lambda e: e.memset(pre[:, 0:H], 0.0), writes=[pre])
        for i in range(1, NT):
            kb.op("dve", lambda e, i=i: e.tensor_tensor(pre[:, i * H:(i + 1) * H], pre[:, (i - 1) * H:i * H], xf[:, (i - 1) * H:i * H], ALU.add),
                  reads=[pre, xf], writes=[pre])
        ps = kb.next_psum()
        kb.mm(ps, [(ps[:, :NT * H], tri[:, :], xf[:, :]), (ps[:, :NT * H], P.ones_f32[:, :], pre[:, :])], reads=[tri, xf, pre, P.ones_f32])
        csb = st.tile([128, NT * H], F32, "csb")
        kb.op("dve", lambda e: e.tensor_copy(csb[:, :], ps[:, :NT * H]), reads=[ps], writes=[csb])
        ps2 = kb.next_psum()
        kb.mm(ps2, [(ps2[:, :NT * H], P.sel64_tile[:, :], csb[:, :])], reads=[P.sel64_tile, csb])
        cmid = st.tile([128, NT * H], F32, "cmid")
        kb.op("dve", lambda e: e.tensor_copy(cmid[:, :], ps2[:, :NT * H]), reads=[ps2], writes=[cmid])
        P.dump("fox_xf", xf, [128, NT * H])
        P.dump("fox_csb", csb, [128, NT * H])
        P.dump("fox_cmid", cmid, [128, NT * H])
        bias = st.tile([128, H, NT, NT], F32, "bias")
        cm3 = cmid[:, :].rearrange("p (i h) -> p h i", h=H)
        for h in range(H):
            for c in range(NT):
                kb.op("dve", lambda e, h=h, c=c: e.tensor_scalar(out=bias[:, h, c, :], in0=cm3[:, h, :], scalar1=csb[:, c * H + h:c * H + h + 1],
                                                                    scalar2=-1.0, op0=ALU.subtract, op1=ALU.mult), reads=[cmid, csb], writes=[bias])
        q_r = st.ring(2, [128, S], BF16, "q")
        k_r = st.ring(2, [128, S], BF16, "k")
        v_r = st.ring(2, [128, NT, 128], BF16, "v")
        e_r = st.ring(4, [128, 512], BF16, "e")
        rd_r = st.ring(2, [128, 512], F32, "rd")
        o_r = st.ring(2, [128, S], BF16, "o")
        accR = Ring(kb.psum[0:4])
        qkR = Ring(kb.psum[4:8])
        items = []
        for h in range(H):
            for j in range(4):
                for c in range(4 * j + 4):
                    items.append((h, j, c))
        state = {}

        def emit_qk(h, j, c):
            if j == 0 and c == 0:
                qt, kt, vt = q_r.next(), k_r.next(), v_r.next()
                kb.dma("sp", qt[:, :], qv[:, h, :], qt, reads=[], writes=[qt])
                kb.dma("sp", kt[:, :], kv[:, h, :], kt, reads=[], writes=[kt])
                kb.dma("sp", vt[:, :, :], vv[:, :, h * 128:(h + 1) * 128], vt, reads=[], writes=[vt])
                state[("in", h)] = (qt, kt, vt)
            qt, kt, vt = state[("in", h)]
            r = c - 4 * j
            sb0 = max(0, r)
            t0 = sb0 * 128
            ps = qkR.next()
            kb.mm(ps, [(ps[:, t0:512], kt[:, c * 128:(c + 1) * 128], qt[:, j * 512 + t0:(j + 1) * 512])], reads=[kt, qt])
            et = e_r.next()
            for sb in range(sb0, 4):
                b = 4 * j + sb
                kb.op("act", lambda e, sb=sb, b=b: e.activation(
                    out=et[:, sb * 128:(sb + 1) * 128], in_=ps[:, sb * 128:(sb + 1) * 128], func=AF.Exp,
                    bias=bias[:, h, c, b:b + 1], scale=1.0), reads=[ps, bias], writes=[et])
            if r >= 0:
                kb.op("pool", lambda e: e.tensor_tensor(et[:, r * 128:(r + 1) * 128], et[:, r * 128:(r + 1) * 128], tri[:, :], ALU.mult),
                      reads=[et, tri], writes=[et])
            state[("e", h, j, c)] = (et, t0)

        def emit_pv(h, j, c):
            qt, kt, vt = state[("in", h)]
            et, t0 = state.pop(("e", h, j, c))
            nchunks = 4 * j + 4
            if c == 0:
                state[("acc", h, j)] = (accR.next(), accR.next())
                if j == 0:
                    state[("o", h)] = o_r.next()
            po, pd = state[("acc", h, j)]
            ot = state[("o", h)]
            first, last = (c == 0), (c == nchunks - 1)
            kb.op("pe", lambda pe: pe.matmul(po[:, t0:512], vt[:, c, :], et[:, t0:512], start=first, stop=last), reads=[vt, et], writes=[po])
            kb.op("pe", lambda pe: pe.matmul(pd[:, t0:512], P.ones_bf[:, :], et[:, t0:512], start=first, stop=last), reads=[P.ones_bf, et], writes=[pd])
            if last:
                rd = rd_r.next()
                kb.op("dve", lambda e: e.reciprocal(rd[:, :], pd[:, :]), reads=[pd], writes=[rd])
                kb.op("dve", lambda e: e.tensor_tensor(ot[:, j * 512:(j + 1) * 512], po[:, :], rd[:, :], ALU.mult), reads=[po, rd], writes=[ot])
                if j == 3:
                    kb.dma("sp", fo[h * 128:(h + 1) * 128, :], ot[:, :], ot, reads=[ot], writes=[kb.d("foT", h)])

        LOOK = 2
        for n in range(min(LOOK, len(items))):
            emit_qk(*items[n])
        for n in range(len(items)):
            if n + LOOK < len(items):
                emit_qk(*items[n + LOOK])
            emit_pv(*items[n])


def sgu_stage(P, l):
    kb = P.kb
    sc = P.scr
    uv = sc["uT"].rearrange("(q p) t -> p q t", p=128)
    sav = sc["saT"].rearrange("(q p) t -> p q t", p=128)
    with Stage(kb, "sgu%d" % l) as st:
        lng = load_vec(P, st, P.get("sgu_ln_g_b", [P.lw, 128, 1024])[l], [128, 1024], "lng")
        lnb = load_vec(P, st, P.get("sgu_ln_b_b", [P.lw, 128, 1024])[l], [128, 1024], "lnb")
        bsb = load_vec(P, st, P.get("sgu_b_s_b", [P.lw, 128, 1024])[l], [128, 1024], "bsb")
        wsT = load_vec(P, st, P.get("sgu_w_sT", [P.lw, 128, 512])[l], [128, 512], "wsT")
        wm = st.tile([128, 4, 128], BF16, "wm")
        for g in range(4):
            kb.op("pool", lambda e, g=g: e.tensor_tensor(wm[:, g, :], wsT[:, g * 128:(g + 1) * 128], P.tri_tile[:, :], ALU.mult),
                  reads=[wsT, P.tri_tile], writes=[wm])
        v_r = st.ring(2, [128, 1024], F32, "v")
        sq_r = st.ring(2, [128, 1024], F32, "sq")
        vn_r = st.ring(2, [128, 1024], F32, "vn")
        vb_r = st.ring(2, [128, 1024], BF16, "vb")
        u_r = st.ring(2, [128, 8, 128], BF16, "u")
        o_r = st.ring(2, [128, 8, 128], BF16, "o")
        t_r = st.ring(2, [128, 512], F32, "t")
        s_r = st.ring(2, [128, 16], F32, "s")
        for tt in range(NT):
            vt = v_r.next()
            kb.dma("sp", vt[:, :], sc["v"][tt * 128:(tt + 1) * 128, :], vt, reads=[], writes=[vt])
            ut = u_r.next()
            kb.dma("sp", ut[:, :, :], uv[:, :, tt * 128:(tt + 1) * 128], ut, reads=[], writes=[ut])
            sq = sq_r.next()
            kb.op("pool", lambda e, sq=sq, vt=vt: e.tensor_tensor(sq[:, :], vt[:, :], vt[:, :], ALU.mult), reads=[vt], writes=[sq])
            s = s_r.next()
            kb.op("dve", lambda e, s=s, vt=vt: e.tensor_reduce(out=s[:, 0:4], in_=vt[:, :].rearrange("p (g c) -> p g c", g=4), axis=AX.X, op=ALU.add), reads=[vt], writes=[s])
            kb.op("dve", lambda e, s=s, sq=sq: e.tensor_reduce(out=s[:, 4:8], in_=sq[:, :].rearrange("p (g c) -> p g c", g=4), axis=AX.X, op=ALU.add), reads=[sq], writes=[s])
            kb.op("dve", lambda e, s=s: e.tensor_scalar_mul(s[:, 0:4], s[:, 0:4], 1.0 / 256), reads=[s], writes=[s])
            kb.op("dve", lambda e, s=s: e.tensor_tensor(s[:, 8:12], s[:, 0:4], s[:, 0:4], ALU.mult), reads=[s], writes=[s])
            kb.op("dve", lambda e, s=s: e.scalar_tensor_tensor(out=s[:, 4:8], in0=s[:, 4:8], scalar=1.0 / 256, in1=s[:, 8:12], op0=ALU.mult, op1=ALU.subtract), reads=[s], writes=[s])
            kb.op("act", lambda e, s=s: e.activation(out=s[:, 4:8], in_=s[:, 4:8], func=AF.Sqrt, bias=P.eps_tile[:, 0:1], scale=1.0), reads=[s, P.eps_tile], writes=[s])
            kb.op("dve", lambda e, s=s: e.reciprocal(s[:, 4:8], s[:, 4:8]), reads=[s], writes=[s])
            vn = vn_r.next()
            for g in range(4):
                kb.op("dve", lambda e, g=g, vn=vn, vt=vt, s=s: e.tensor_scalar(out=vn[:, g * 256:(g + 1) * 256], in0=vt[:, g * 256:(g + 1) * 256],
                      scalar1=s[:, g:g + 1], scalar2=s[:, 4 + g:5 + g], op0=ALU.subtract, op1=ALU.mult), reads=[vt, s], writes=[vn])
            kb.op("pool", lambda e, vn=vn: e.tensor_tensor(vn[:, :], vn[:, :], lng[:, :], ALU.mult), reads=[vn, lng], writes=[vn])
            vb = vb_r.next()
            kb.op("pool", lambda e, vn=vn, vb=vb: e.tensor_tensor(vb[:, :], vn[:, :], lnb[:, :], ALU.add), reads=[vn, lnb], writes=[vb])
            if tt == 0:
                P.dump("sgu_s", s, [128, 16])
                P.dump("sgu_vn", vn, [128, 1024])
                P.dump("sgu_v", vt, [128, 1024])
                P.dump("sgu_lng", lng, [128, 1024])
                P.dump("sgu_wsT", wsT, [128, 512])
                P.dump("sgu_bsb", bsb, [128, 1024])
            ot = o_r.next()
            for half in range(2):
                ps = kb.next_psum()

                def f(pe, ps=ps, vb=vb, half=half):
                    ins = None
                    for qq in range(4):
                        q = half * 4 + qq
                        ins = pe.matmul(ps[:, qq * 128:(qq + 1) * 128], vb[:, q * 128:(q + 1) * 128], wm[:, q // 2, :], start=True, stop=True)
                    return ins
                kb.op("pe", f, reads=[vb, wm], writes=[ps])
                t = t_r.next()
                kb.op("dve", lambda e, t=t, ps=ps, half=half: e.tensor_tensor(t[:, :], ps[:, :], bsb[:, half * 512:(half + 1) * 512], ALU.add), reads=[ps, bsb], writes=[t])
                kb.op("pool", lambda e, t=t, ot=ot, ut=ut, half=half: e.tensor_tensor(
                    ot[:, half * 4:(half + 1) * 4, :], t[:, :].rearrange("p (q t) -> p q t", q=4), ut[:, half * 4:(half + 1) * 4, :], ALU.mult),
                    reads=[t, ut], writes=[ot])
            kb.dma("sp", sav[:, :, tt * 128:(tt + 1) * 128], ot[:, :, :], ot, reads=[ot], writes=[kb.d("saT", tt)])


def gla_stage(P, l):
    kb = P.kb
    sc = P.scr
    H = 4
    gqv = sc["gqT"].rearrange("(h p) t -> p h t", p=128)
    gkv = sc["gkT"].rearrange("(h p) t -> p h t", p=128)
    gov = sc["goT"].rearrange("(q p) t -> p q t", p=128)
    with Stage(kb, "gla%d" % l) as st:
        TT = load_vec(P, st, P.get("c_TT", [128, 256])[:, :], [128, 256], "TT")
        REV = load_vec(P, st, P.get("c_rev", [128, 128])[:, :], [128, 128], "REV")
        M64 = load_vec(P, st, P.get("c_m64", [128, 512])[:, :], [128, 512], "M64")
        RM = load_vec(P, st, P.get("c_rowmask", [128, 2])[:, :], [128, 2], "RM")
        onb = load_vec(P, st, P.get("gla_o_norm_b", [P.lw, 128, 1024])[l], [128, 1024], "onb")
        aT = st.tile([32, S], F32, "aT")
        kb.op("dve", lambda e: e.memset(aT[:, :], 1.0), writes=[aT])
        kb.dma("sp", aT[0:16, :], sc["gaT"][:, :], aT, reads=[], writes=[aT])
        wg = st.tile([32, 512], F32, "wg")
        kb.op("dve", lambda e: e.memset(wg[:, :], 0.0), writes=[wg])
        kb.dma("sp", wg[0:16, :], P.get("gla_w_gate", [P.lw, 16, 512])[l], wg, reads=[], writes=[wg])
        kb.dma("sp", wg[16:17, :], P.get("gla_b_gate", [P.lw, 1, 512])[l], wg, reads=[], writes=[wg])
        S32 = [st.tile([128, 256], F32, "S32_%d" % h) for h in range(H)]
        S16_r = [st.ring(3, [128, 256], BF16, "S16_%d" % h) for h in range(H)]
        S16 = []
        for h in range(H):
            kb.op("pool", lambda e, h=h: e.memset(S32[h][:, :], 0.0), writes=[S32[h]])
            t0 = S16_r[h].next()
            kb.op("pool", lambda e, t0=t0: e.memset(t0[:, :], 0.0), writes=[t0])
            S16.append(t0)
        qiA_r = st.ring(2, [128, H, 128], BF16, "qiA")
        qiB_r = st.ring(2, [128, H, 128], BF16, "qiB")
        for r_ in (qiA_r, qiB_r):
            for t in r_.tiles:
                kb.op("pool", lambda e, t=t: e.memset(t[:, :, :], 0.0), writes=[t])
        sp_r = st.ring(2, [128, 512], F32, "sp")
        E_r = [st.ring(2, [128, H, 128], F32, "E%d" % i) for i in range(3)]
        q_r = st.ring(2, [128, H, 128], F32, "q")
        k_r = st.ring(2, [128, H, 128], F32, "k")
        qd_r = st.ring(2, [128, H, 128], BF16, "qd")
        kd_r = st.ring(2, [128, H, 128], BF16, "kd")
        E4_r = st.ring(2, [128, 512], F32, "E4")
        gk_r = st.ring(2, [128, 512], F32, "gk")
        ku_r = st.ring(4, [128, 512], BF16, "ku")
        gv_r = st.ring(2, [128, 1024], BF16, "gv")
        att_r = st.ring(2, [128, 512], BF16, "att")
        sq_r = st.ring(2, [128, 1024], F32, "sq")
        og_r = st.ring(2, [128, 1024], F32, "og")
        gr_r = st.ring(2, [128, 1024], F32, "gr")
        ss_r = st.ring(2, [128, 8], F32, "ss")
        go_r = st.ring(2, [128, 8, 128], BF16, "go")
        for tt in range(NT):
            tsl = slice(tt * 128, (tt + 1) * 128)
            ps = kb.next_psum()
            kb.mm(ps, [(ps[:, :], aT[:, tsl], wg[:, :])], reads=[aT, wg])
            sp = sp_r.next()
            kb.op("act", lambda e, sp=sp, ps=ps: e.activation(out=sp[:, :], in_=ps[:, :], func=AF.Exp, scale=-1.0), reads=[ps], writes=[sp])
            kb.op("act", lambda e, sp=sp: e.activation(out=sp[:, :], in_=sp[:, :], func=AF.Ln, bias=P.one_tile[:, 0:1], scale=1.0), reads=[sp, P.one_tile], writes=[sp])
            qt, kt, gkt, gvt, grt = q_r.next(), k_r.next(), gk_r.next(), gv_r.next(), gr_r.next()
            kb.dma("sp", qt[:, :, :], gqv[:, :, tsl], qt, reads=[], writes=[qt])
            kb.dma("sp", kt[:, :, :], gkv[:, :, tsl], kt, reads=[], writes=[kt])
            kb.dma("sp", gkt[:, :], sc["gk"][tsl, :], gkt, reads=[], writes=[gkt])
            kb.dma("sp", gvt[:, :], sc["gv"][tsl, :], gvt, reads=[], writes=[gvt])
            kb.dma("sp", grt[:, :], sc["gr"][tsl, :], grt, reads=[], writes=[grt])
            if GLA_STOP == 1:
                continue
            E1, E2, E3 = [r_.next() for r_ in E_r]
            for pair in range(2):
                psA = kb.next_psum()

                def fA(pe, psA=psA, sp=sp, pair=pair):
                    ins = None
                    for hh in range(2):
                        h = pair * 2 + hh
                        ins = pe.matmul(psA[:, hh * 256:(hh + 1) * 256], sp[:, h * 128:(h + 1) * 128], TT[:, :], start=True, stop=True)
                    return ins
                kb.op("pe", fA, reads=[sp, TT], writes=[psA])
                pv = psA[:, :].rearrange("p (h c) -> p h c", h=2)
                hs = slice(pair * 2, pair * 2 + 2)
                kb.op("act", lambda e, pv=pv, hs=hs, E1=E1: e.activation(out=E1[:, hs, :], in_=pv[:, :, 0:128], func=AF.Exp), reads=[psA], writes=[E1])
                kb.op("act", lambda e, pv=pv, hs=hs, E2=E2: e.activation(out=E2[:, hs, :], in_=pv[:, :, 128:256], func=AF.Exp), reads=[psA], writes=[E2])
                kb.op("act", lambda e, pv=pv, hs=hs, E3=E3: e.activation(out=E3[:, hs, :], in_=pv[:, :, 128:256], func=AF.Exp, scale=-1.0), reads=[psA], writes=[E3])
            qd, kd, qiA, qiB = qd_r.next(), kd_r.next(), qiA_r.next(), qiB_r.next()
            kb.op("dve", lambda e, qd=qd, qt=qt, E2=E2: e.tensor_tensor(qd[:, :, :], qt[:, :, :], E2[:, :, :], ALU.mult), reads=[qt, E2], writes=[qd])
            kb.op("pool", lambda e, kd=kd, kt=kt, E3=E3: e.tensor_tensor(kd[:, :, :], kt[:, :, :], E3[:, :, :], ALU.mult), reads=[kt, E3], writes=[kd])
            kb.op("dve", lambda e, qiA=qiA, qt=qt, E1=E1: e.tensor_tensor(qiA[:, :, 0:64], qt[:, :, 0:64], E1[:, :, 0:64], ALU.mult), reads=[qt, E1], writes=[qiA])
            kb.op("pool", lambda e, qiB=qiB, qt=qt, E1=E1: e.tensor_tensor(qiB[:, :, 64:128], qt[:, :, 64:128], E1[:, :, 64:128], ALU.mult), reads=[qt, E1], writes=[qiB])
            if GLA_STOP == 2:
                continue
            psR = kb.next_psum()
            kb.mm(psR, [(psR[:, :], REV[:, :], sp[:, :])], reads=[REV, sp])
            E4 = E4_r.next()
            kb.op("act", lambda e, E4=E4, psR=psR: e.activation(out=E4[:, :], in_=psR[:, :], func=AF.Exp), reads=[psR], writes=[E4])
            ku = [ku_r.next(), ku_r.next()]
            for a in range(2):
                kb.op("dve", lambda e, a=a, ku=ku, gkt=gkt, E4=E4: e.scalar_tensor_tensor(out=ku[a][:, :], in0=gkt[:, :], scalar=RM[:, a:a + 1], in1=E4[:, :],
                                                                                        op0=ALU.mult, op1=ALU.mult), reads=[gkt, E4, RM], writes=[ku[a]])
            if GLA_STOP == 3:
                continue
            psT = kb.next_psum()

            def fT(pe, psT=psT, kd=kd, qd=qd):
                ins = None
                for h in range(H):
                    ins = pe.matmul(psT[:, h * 128:(h + 1) * 128], kd[:, h, :], qd[:, h, :], start=True, stop=True)
                return ins
            kb.op("pe", fT, reads=[kd, qd], writes=[psT])
            att = att_r.next()
            kb.op("dve", lambda e, att=att, psT=psT: e.tensor_tensor(att[:, :], psT[:, :], M64[:, :], ALU.mult), reads=[psT, M64], writes=[att])
            if GLA_STOP == 4:
                continue
            psO = [kb.next_psum(), kb.next_psum()]
            for h in range(H):
                psU = kb.next_psum()

                def fU(pe, psU=psU, ku=ku, gvt=gvt, h=h):
                    ins = None
                    for a in range(2):
                        ins = pe.matmul(psU[:, a * 256:(a + 1) * 256], ku[a][:, h * 128:(h + 1) * 128],
                                        gvt[:, h * 256:(h + 1) * 256], start=True, stop=True)
                    return ins
                kb.op("pe", fU, reads=[ku[0], ku[1], gvt], writes=[psU])
                if GLA_STOP == 41:
                    continue
                Sn = S16[h]
                kb.op("dve", lambda e, h=h, psU=psU, E1=E1: e.scalar_tensor_tensor(out=S32[h][:, :], in0=S32[h][:, :], scalar=E1[:, h, 63:64], in1=psU[:, 0:256],
                                                                                    op0=ALU.mult, op1=ALU.add), reads=[S32[h], E1, psU], writes=[S32[h]])
                Sn1 = S16_r[h].next()
                kb.op("pool", lambda e, h=h, Sn1=Sn1: e.tensor_copy(Sn1[:, :], S32[h][:, :]), reads=[S32[h]], writes=[Sn1])
                kb.op("dve", lambda e, h=h, psU=psU, E1=E1: e.scalar_tensor_tensor(out=S32[h][:, :], in0=S32[h][:, :], scalar=E1[:, h, 127:128], in1=psU[:, 256:512],
                                                                                    op0=ALU.mult, op1=ALU.add), reads=[S32[h], E1, psU], writes=[S32[h]])
                Sn2 = S16_r[h].next()
                kb.op("pool", lambda e, h=h, Sn2=Sn2: e.tensor_copy(Sn2[:, :], S32[h][:, :]), reads=[S32[h]], writes=[Sn2])
                S16[h] = Sn2
                if GLA_STOP == 42:
                    continue
                po = psO[h // 2]
                oc = slice((h % 2) * 256, (h % 2 + 1) * 256)
                kb.mm(po, [(po[:, oc], att[:, h * 128:(h + 1) * 128], gvt[:, h * 256:(h + 1) * 256]),
                           (po[:, oc], qiA[:, h, :], Sn[:, :]),
                           (po[:, oc], qiB[:, h, :], Sn1[:, :])], reads=[att, gvt, qiA, qiB, Sn, Sn1])
            if GLA_STOP in (5, 41, 42):
                continue
            sq = sq_r.next()
            for i2 in range(2):
                kb.op("act", lambda e, sq=sq, i2=i2, psO=psO: e.activation(out=sq[:, i2 * 512:(i2 + 1) * 512], in_=psO[i2][:, :], func=AF.Square), reads=[psO[i2]], writes=[sq])
            ss = ss_r.next()
            kb.op("dve", lambda e, ss=ss, sq=sq: e.tensor_reduce(out=ss[:, 0:4], in_=sq[:, :].rearrange("p (g c) -> p g c", g=4), axis=AX.X, op=ALU.add), reads=[sq], writes=[ss])
            kb.op("act", lambda e, ss=ss: e.activation(out=ss[:, 0:4], in_=ss[:, 0:4], func=AF.Sqrt, bias=P.eps_tile[:, 0:1], scale=1.0 / 256), reads=[ss, P.eps_tile], writes=[ss])
            kb.op("dve", lambda e, ss=ss: e.reciprocal(ss[:, 0:4], ss[:, 0:4]), reads=[ss], writes=[ss])
            og = og_r.next()
            for h in range(H):
                kb.op("dve", lambda e, h=h, og=og, ss=ss, psO=psO: e.scalar_tensor_tensor(
                    out=og[:, h * 256:(h + 1) * 256], in0=psO[h // 2][:, (h % 2) * 256:(h % 2 + 1) * 256], scalar=ss[:, h:h + 1],
                    in1=onb[:, h * 256:(h + 1) * 256], op0=ALU.mult, op1=ALU.mult), reads=[psO[h // 2], ss, onb], writes=[og])
            kb.op("pool", lambda e, og=og, grt=grt: e.tensor_tensor(og[:, :], og[:, :], grt[:, :], ALU.mult), reads=[og, grt], writes=[og])
            if GLA_STOP == 6:
                continue
            go = go_r.next()
            for i2 in range(2):
                psX = kb.next_psum()

                def fX(pe, psX=psX, og=og, i2=i2):
                    ins = None
                    for qq in range(4):
                        q = i2 * 4 + qq
                        ins = pe.transpose(psX[:, qq * 128:(qq + 1) * 128], og[:, q * 128:(q + 1) * 128], P.ident_tile[:, :])
                    return ins
                kb.op("pe", fX, reads=[og, P.ident_tile], writes=[psX])
                pxv = psX[:, :].rearrange("p (q t) -> p q t", q=4)
                if i2 == 0:
                    kb.op("act", lambda e, go=go, pxv=pxv: e.copy(go[:, 0:4, :], pxv), reads=[psX], writes=[go])
                else:
                    kb.op("dve", lambda e, go=go, pxv=pxv: e.tensor_copy(go[:, 4:8, :], pxv), reads=[psX], writes=[go])
            kb.dma("sp", gov[:, :, tsl], go[:, :, :], go, reads=[go], writes=[kb.d("goT", tt)])


def xa_stage(P, l):
    kb = P.kb
    w_q = P.get("xa_w_q", [P.lw, D, 512])[l].rearrange("(kc p) n -> p kc n", p=128)
    w_kv = P.get("xa_w_kv", [P.lw, D, 1024])[l].rearrange("(kc p) n -> p kc n", p=128)
    w_o = P.get("xa_w_o", [P.lw, 512, D])[l].rearrange("(kc p) n -> p kc n", p=128)
    hv = P.hT.rearrange("(c p) t -> p c t", p=128)
    mv = P.memT.rearrange("(c p) t -> p c t", p=128)
    TB = 512
    with Stage(kb, "xa%d" % l) as st:
        gain = load_vec(P, st, P.get("xa_norm_t", [P.lw, 128, DC])[l], [128, DC], "g")
        mgain = load_vec(P, st, P.get("mem_norm_t", [P.lw, 128, DC])[l], [128, DC], "mg")
        kT = st.tile([128, 4, MEM], BF16, "kT")
        v16 = st.tile([128, 2, 512], BF16, "v16")
        q16 = st.tile([128, 4, S], BF16, "q16")
        oT = st.tile([128, 4, S], BF16, "oT")
        nT = st.tile([128, DC, S], BF16, "nT")
        with Stage(kb, "xa%dm" % l) as st2:
            mx = st2.tile([128, DC, MEM], F32, "mx")
            kb.dma("sp", mx[:, :, :], mv, mx, reads=[], writes=[mx])
            sqr = st2.ring(3, [128, 512], F32, "sq")
            rstd = st2.tile([128, 512], F32, "rstd")
            rms_stats(P, st2, mx, MEM, sqr, rstd)
            mn = st2.tile([128, DC, MEM], BF16, "mn")
            for c in range(DC):
                kb.op("dve", lambda e, c=c: e.scalar_tensor_tensor(out=mn[:, c, :], in0=mx[:, c, :], scalar=mgain[:, c:c + 1], in1=rstd[:, :MEM],
                                                                    op0=ALU.mult, op1=ALU.mult), reads=[mx, rstd, mgain], writes=[mn])
            wk_r = st2.ring(2, [128, DC, 128], BF16, "wk")
            for c in range(4):
                w = wk_r.next()
                kb.dma("pool", w[:, :, :], w_kv[:, :, c * 128:(c + 1) * 128], w, reads=[], writes=[w])
                ps = kb.next_psum()
                kb.mm(ps, [(ps[:, :MEM], w[:, k, :], mn[:, k, :]) for k in range(DC)], reads=[w, mn])
                kb.op("act", lambda e, c=c, ps=ps: e.copy(kT[:, c, :], ps[:, :MEM]), reads=[ps], writes=[kT])
            wv = st2.tile([128, DC, 512], BF16, "wv")
            kb.dma("pool", wv[:, :, :], w_kv[:, :, 512:1024], wv, reads=[], writes=[wv])
            for mt in range(2):
                ps = kb.next_psum()
                kb.mm(ps, [(ps[:, :], mn[:, k, mt * 128:(mt + 1) * 128], wv[:, k, :]) for k in range(DC)], reads=[wv, mn])
                kb.op("dve", lambda e, mt=mt, ps=ps: e.tensor_copy(v16[:, mt, :], ps[:, :]), reads=[ps], writes=[v16])
        with Stage(kb, "xa%dn" % l) as st2:
            norm_all_tokens(P, st2, P.hT, gain, nT)
        wq_r = st.ring(2, [128, DC, 128], BF16, "wq")
        for c in range(4):
            w = wq_r.next()
            kb.dma("pool", w[:, :, :], w_q[:, :, c * 128:(c + 1) * 128], w, reads=[], writes=[w])
            for tb in range(S // TB):
                ps = kb.next_psum()
                kb.mm(ps, [(ps[:, :], w[:, k, :], nT[:, k, tb * TB:(tb + 1) * TB]) for k in range(DC)], reads=[w, nT])
                kb.op("act", lambda e, c=c, tb=tb, ps=ps: e.mul(q16[:, c, tb * TB:(tb + 1) * TB], ps[:, :], 128 ** -0.5), reads=[ps], writes=[q16])
        e_r = st.ring(4, [128, TB], BF16, "e")
        rd_r = st.ring(2, [128, TB], F32, "rd")
        for h in range(4):
            for tb in range(S // TB):
                ts_ = slice(tb * TB, (tb + 1) * TB)
                es = []
                for mc in range(2):
                    ps = kb.next_psum()
                    kb.mm(ps, [(ps[:, :], kT[:, h, mc * 128:(mc + 1) * 128], q16[:, h, ts_])], reads=[kT, q16])
                    et = e_r.next()
                    kb.op("act", lambda e, et=et, ps=ps: e.activation(out=et[:, :], in_=ps[:, :], func=AF.Exp), reads=[ps], writes=[et])
                    es.append(et)
                po = kb.next_psum()
                kb.mm(po, [(po[:, :], v16[:, mc, h * 128:(h + 1) * 128], es[mc][:, :]) for mc in range(2)], reads=[v16] + es)
                pd = kb.next_psum()
                kb.mm(pd, [(pd[:, :], P.ones_bf[:, :], es[mc][:, :]) for mc in range(2)], reads=[P.ones_bf] + es)
                rd = rd_r.next()
                kb.op("dve", lambda e, rd=rd, pd=pd: e.reciprocal(rd[:, :], pd[:, :]), reads=[pd], writes=[rd])
                kb.op("dve", lambda e, rd=rd, po=po, h=h, ts_=ts_: e.tensor_tensor(oT[:, h, ts_], po[:, :], rd[:, :], ALU.mult), reads=[po, rd], writes=[oT])
        wo_r = st.ring(2, [128, 4, 128], BF16, "wo")
        h_r = st.ring(6, [128, TB], F32, "h")
        wcur = {}

        def mm_fn(i, tb):
            if tb == 0:
                wo = wo_r.next()
                kb.dma("pool", wo[:, :, :], w_o[:, :, i * 128:(i + 1) * 128], wo, reads=[], writes=[wo])
                wcur[0] = wo
            wo = wcur[0]
            po = kb.next_psum()
            kb.mm(po, [(po[:, :], wo[:, k, :], oT[:, k, tb * TB:(tb + 1) * TB]) for k in range(4)], reads=[wo, oT])
            return po
        h_update_loop(P, [(i, tb) for i in range(DC) for tb in range(S // TB)], mm_fn, 1.0, h_r)


def merge_stage(P, l):
    kb = P.kb
    sc = P.scr
    wbr = [P.get("w_branch_" + x, [P.lw, 1024, D])[l].rearrange("(kc p) n -> p kc n", p=128) for x in "abc"]
    w_out = P.get("w_out", [P.lw, D, D])[l].rearrange("(kc p) n -> p kc n", p=128)
    brT = [sc[x].rearrange("(kc p) t -> p kc t", p=128) for x in ("saT", "foT", "goT")]
    gv = sc["gates"].rearrange("(g c p) t -> p g c t", p=128, g=3)
    TB = 512
    NTB = S // TB
    with Stage(kb, "mg%d" % l) as st:
        mT = st.tile([128, DC, S], BF16, "mT")
        with Stage(kb, "mg%da" % l) as st2:
            br = [st2.tile([128, 8, S], BF16, "br%d" % i) for i in range(3)]
            for b in range(3):
                for hh in range(2):
                    kb.dma("sp", br[b][:, hh * 4:(hh + 1) * 4, :], brT[b][:, hh * 4:(hh + 1) * 4, :], br[b], reads=[], writes=[br[b]])
            wb_r = [st2.ring(2, [128, 8, 128], BF16, "wb%d" % i) for i in range(3)]
            g_r = st2.ring(2, [128, 3, TB], F32, "g")
            t_r = st2.ring(2, [128, 3, TB], F32, "t")
            for i in range(DC):
                wb = [r.next() for r in wb_r]
                for b in range(3):
                    kb.dma("pool", wb[b][:, :, :], wbr[b][:, :, i * 128:(i + 1) * 128], wb[b], reads=[], writes=[wb[b]])
                for tb in range(NTB):
                    ts_ = slice(tb * TB, (tb + 1) * TB)
                    g = g_r.next()
                    kb.dma("sp", g[:, :, :], gv[:, :, i, ts_], g, reads=[], writes=[g])
                    t = t_r.next()
                    for b in range(3):
                        ps = kb.next_psum()
                        kb.mm(ps, [(ps[:, :], wb[b][:, k, :], br[b][:, k, ts_]) for k in range(8)], reads=[wb[b], br[b]])
                        kb.op("dve", lambda e, b=b, ps=ps: e.tensor_tensor(t[:, b, :], g[:, b, :], ps[:, :], ALU.mult), reads=[g, ps], writes=[t])
                    kb.op("pool", lambda e: e.tensor_tensor(t[:, 0, :], t[:, 0, :], t[:, 1, :], ALU.add), reads=[t], writes=[t])
                    kb.op("pool", lambda e, i=i, ts_=ts_: e.tensor_tensor(mT[:, i, ts_], t[:, 0, :], t[:, 2, :], ALU.add), reads=[t], writes=[mT])
        wo_r = st.ring(2, [128, DC, 128], BF16, "wo")
        h_r = st.ring(6, [128, TB], F32, "h")
        wcur = {}

        def mm_fn(i, tb):
            if tb == 0:
                wo = wo_r.next()
                kb.dma("pool", wo[:, :, :], w_out[:, :, i * 128:(i + 1) * 128], wo, reads=[], writes=[wo])
                wcur[0] = wo
            wo = wcur[0]
            po = kb.next_psum()
            kb.mm(po, [(po[:, :], wo[:, k, :], mT[:, k, tb * TB:(tb + 1) * TB]) for k in range(DC)], reads=[wo, mT])
            return po
        h_update_loop(P, [(i, tb) for i in range(DC) for tb in range(NTB)], mm_fn, 1.0, h_r)


def final_stage(P):
    kb = P.kb
    with Stage(kb, "fin") as st:
        gain = load_vec(P, st, P.get("final_norm_t", [128, DC]), [128, DC], "g")
        TB = 512
        xr = st.ring(2, [128, DC, TB], F32, "x")
        sqr = st.ring(3, [128, TB], F32, "sq")
        rr = st.ring(2, [128, TB], F32, "r")
        yr = st.ring(3, [128, TB], F32, "y")
        outr = st.ring(8, [128, D], F32, "o")
        hv = P.hT.rearrange("(c p) t -> p c t", p=128)
        for tb in range(S // TB):
            xt = xr.next()
            for half in range(2):
                cs = slice(half * 8, half * 8 + 8)
                kb.dma("sp", xt[:, cs, :], hv[:, cs, tb * TB:(tb + 1) * TB], xt,
                       reads=[kb.d("hT", c, tb) for c in range(half * 8, half * 8 + 8)], writes=[xt])
            rstd = rr.next()
            import os
            dbg = os.environ.get("KDBG", "")
            if "e" in dbg:
                kb.op("dve", lambda e: e.memset(rstd[:, :], 1.0), writes=[rstd])
            else:
                rms_stats(P, st, xt, TB, sqr, rstd)
            cur = [outr.next() for _ in range(4)]
            if "d" in dbg:
                continue
            for c in range(DC):
                y = yr.next()
                if "f" in dbg:
                    kb.op("dve", lambda e, c=c, y=y: e.tensor_tensor(y[:, :], xt[:, c, :], rstd[:, :TB], ALU.mult),
                          reads=[xt, rstd, gain], writes=[y])
                else:
                    kb.op("dve", lambda e, c=c, y=y: e.scalar_tensor_tensor(
                        out=y[:, :], in0=xt[:, c, :], scalar=gain[:, c:c + 1], in1=rstd[:, :TB], op0=ALU.mult, op1=ALU.mult),
                        reads=[xt, rstd, gain], writes=[y])
                ps = kb.next_psum()

                def f(pe, ps=ps, y=y):
                    ins = None
                    for j in range(4):
                        ins = pe.transpose(ps[:, j * 128:(j + 1) * 128], y[:, j * 128:(j + 1) * 128], P.ident_tile[:, :])
                    return ins
                kb.op("pe", f, reads=[y, P.ident_tile], writes=[ps])
                for j in range(4):
                    if c % 2 == 0:
                        kb.op("act", lambda e, j=j, ps=ps, c=c: e.copy(cur[j][:, c * 128:(c + 1) * 128], ps[:, j * 128:(j + 1) * 128]),
                              reads=[ps], writes=[cur[j]])
                    else:
                        kb.op("pool" if False else "dve", lambda e, j=j, ps=ps, c=c: e.tensor_copy(cur[j][:, c * 128:(c + 1) * 128], ps[:, j * 128:(j + 1) * 128]),
                              reads=[ps], writes=[cur[j]])
            for j in range(4):
                r0 = tb * TB + j * 128
                if "g" in dbg:
                    continue
                kb.dma("sp", P.out[r0:r0 + 128, :], cur[j][:, :], cur[j], reads=[cur[j]], writes=[kb.d("out", tb, j)])


def build(n_layers=L, stages=("ffn1", "mixproj", "sgu", "fox", "gla", "merge", "xa", "ffn2"), debug_out=None, lw=L):
    P = Prog(n_layers, stages, debug_out)
    P.lw = lw
    nc, kb = P.nc, P.kb
    x = P.get("x", [S, D])
    P.out = nc.dram_tensor("out", [S, D], F32, kind="ExternalOutput").ap()
    P.hT = P.dscr("hT", [D, S])
    P.scr = {}
    for nm, shp, dt in [("uT", [1024, S], BF16), ("v", [S, 1024], F32), ("fqT", [1024, S], BF16), ("fkT", [1024, S], BF16),
                        ("fv", [S, 1024], BF16), ("ff", [S, 8], F32), ("gqT", [512, S], F32), ("gkT", [512, S], F32),
                        ("gk", [S, 512], F32), ("gv", [S, 1024], BF16), ("gaT", [16, S], F32), ("gr", [S, 1024], F32),
                        ("gates", [6144, S], F32), ("saT", [1024, S], BF16), ("foT", [1024, S], BF16), ("goT", [1024, S], BF16)]:
        P.scr[nm] = P.dscr("s_" + nm, shp, dt)
    with ExitStack() as es:
        for i in range(8):
            t = es.enter_context(nc.psum_tensor("ps%d" % i, [128, 512], F32))
            kb.psum.append(Tile(t, Buf("ps%d" % i)))
        with Stage(kb, "glob") as g:
            P.ident_tile = g.tile([128, 128], F32, "ident")
            kb.dma("sp", P.ident_tile[:, :], P.get("ident", [128, 128])[:, :], P.ident_tile, writes=[P.ident_tile])
            P.ones_f32 = g.tile([128, 128], F32, "ones")
            kb.op("dve", lambda e: e.memset(P.ones_f32[:, :], 1.0), writes=[P.ones_f32])
            P.eps_tile = g.tile([128, 1], F32, "eps")
            kb.op("dve", lambda e: e.memset(P.eps_tile[:, :], EPS), writes=[P.eps_tile])
            P.one_tile = g.tile([128, 1], F32, "one")
            kb.op("dve", lambda e: e.memset(P.one_tile[:, :], 1.0), writes=[P.one_tile])
            P.ones_bf = g.tile([128, 128], BF16, "onesbf")
            kb.op("dve", lambda e: e.memset(P.ones_bf[:, :], 1.0), writes=[P.ones_bf])
            P.tri_tile = g.tile([128, 128], F32, "tri")
            kb.dma("sp", P.tri_tile[:, :], P.get("c_tri", [128, 128])[:, :], P.tri_tile, writes=[P.tri_tile])
            P.sel64_tile = g.tile([128, 128], F32, "sel64")
            kb.dma("sp", P.sel64_tile[:, :], P.get("c_sel64", [128, 128])[:, :], P.sel64_tile, writes=[P.sel64_tile])
            if "notr" not in stages:
                transpose_stage(P, x, P.hT, S, D, "x")
            if "xa" in stages:
                P.memT = P.dscr("memT", [D, MEM])
                transpose_stage(P, P.get("mem", [MEM, D]), P.memT, MEM, D, "mem")
            for l in range(n_layers):
                if "ffn1" in stages:
                    ffn_stage(P, l, "ffn1")
                if "mixproj" in stages:
                    mixproj_stage(P, l)
                if "sgu" in stages:
                    sgu_stage(P, l)
                if "fox" in stages:
                    fox_stage(P, l)
                if "gla" in stages:
                    gla_stage(P, l)
                if "merge" in stages:
                    merge_stage(P, l)
                if "xa" in stages:
                    xa_stage(P, l)
                if "ffn2" in stages:
                    ffn_stage(P, l, "ffn2")
            if "nofin" not in stages:
                final_stage(P)
    return P


def vt(v):
    v = np.asarray(v, np.float32)
    return np.ascontiguousarray(np.swapaxes(v.reshape(v.shape[:-1] + (-1, 128)), -1, -2))


def _bcast(v, width):
    v = np.asarray(v, np.float32)
    v = v.reshape(v.shape[0], 1, width)
    return np.ascontiguousarray(np.broadcast_to(v, (v.shape[0], 128, width)))


def _consts():
    s_ = np.arange(128)[:, None]
    t_ = np.arange(128)[None, :]
    same = (s_ // 64) == (t_ // 64)
    tri64 = (same & (s_ <= t_)).astype(np.float32)
    refsel = (same & (s_ <= (t_ // 64) * 64 + 32)).astype(np.float32)
    sel64 = np.zeros((128, 128), np.float32)
    sel64[64, :] = 1.0
    rowmask = np.zeros((128, 2), np.float32)
    rowmask[:64, 0] = 1.0
    rowmask[64:, 1] = 1.0
    return {
        "ident": np.eye(128, dtype=np.float32),
        "c_tri": np.triu(np.ones((128, 128), np.float32)),
        "c_sel64": sel64,
        "c_rowmask": rowmask,
        "c_TT": np.ascontiguousarray(np.concatenate([-tri64 / 16.0, -(tri64 - refsel) / 16.0], axis=1)),
        "c_rev": np.ascontiguousarray(-(same & (s_ > t_)).astype(np.float32) / 16.0),
        "c_m64": np.ascontiguousarray(np.tile(tri64, (1, 4))),
    }


def prep_inputs(P, inputs, b):
    g = lambda k: np.asarray(inputs[k], np.float32)
    src = {
        "x": lambda: np.ascontiguousarray(g("x")[b]),
        "mem": lambda: np.ascontiguousarray(g("mem")[b]),
        "ffn1_norm_t": lambda: vt(g("ffn1_norm")),
        "ffn2_norm_t": lambda: vt(g("ffn2_norm")),
        "mix_norm_t": lambda: vt(g("mix_norm")),
        "xa_norm_t": lambda: vt(g("xa_norm")),
        "mem_norm_t": lambda: vt(g("mem_norm")),
        "final_norm_t": lambda: vt(g("final_norm")),
        "ffn1_w_in": lambda: g("ffn1_w_in"), "ffn1_w_out": lambda: g("ffn1_w_out"),
        "ffn2_w_in": lambda: g("ffn2_w_in"), "ffn2_w_out": lambda: g("ffn2_w_out"),
        "w_in": lambda: g("w_in"), "w_out": lambda: g("w_out"),
        "w_branch_a": lambda: g("w_branch_a"), "w_branch_b": lambda: g("w_branch_b"), "w_branch_c": lambda: g("w_branch_c"),
        "xa_w_q": lambda: g("xa_w_q"), "xa_w_kv": lambda: g("xa_w_kv"), "xa_w_o": lambda: g("xa_w_o"),
        "gla_w_gate": lambda: g("gla_w_gate"),
        "gla_b_gate": lambda: np.ascontiguousarray(g("gla_b_gate").reshape(-1, 1, 512)),
        "gla_o_norm_b": lambda: _bcast(g("gla_o_norm"), 1024),
        "sgu_ln_g_b": lambda: _bcast(g("sgu_ln_g"), 1024),
        "sgu_ln_b_b": lambda: _bcast(g("sgu_ln_b"), 1024),
        "sgu_b_s_b": lambda: _bcast(np.repeat(g("sgu_b_s"), 2, axis=1), 1024),
        "sgu_w_sT": lambda: np.ascontiguousarray(np.transpose(g("sgu_w_s"), (0, 3, 1, 2)).reshape(-1, 128, 512)),
        "fox_b_f_b": lambda: _bcast(np.tile(g("fox_b_f"), (1, NT)), NT * 8),
    }
    consts = _consts()
    m = {}
    for name in P.inp:
        m[name] = consts[name] if name in consts else src[name]()
    return m


def kernel(**inputs):
    P = build()
    in_maps = [prep_inputs(P, inputs, b) for b in range(8)]
    res = run_bass_kernel_spmd(P.nc, in_maps, core_ids=list(range(8)))
    return np.stack([r["out"] for r in res.results], axis=0)
```

```python
import numpy as np
from contextlib import ExitStack
import concourse.bass as bass
import concourse.mybir as mybir
from concourse.bass_utils import run_bass_kernel_spmd

F32 = mybir.dt.float32
BF16 = mybir.dt.bfloat16
AF = mybir.ActivationFunctionType
ALU = mybir.AluOpType
AX = mybir.AxisListType

D = 2048
S = 2048
L = 2
MEM = 256
DFF = 5632
NT = S // 128
DC = D // 128
EPS = 1e-6
N_IN = 14360
O_SU, O_SV, O_FQ, O_FK, O_FV, O_FF = 0, 1024, 2048, 3072, 4096, 5120
O_GQ, O_GK, O_GV, O_GA, O_GR, O_GATES = 5128, 5640, 6152, 7176, 7192, 8216

import os
GLA_STOP = int(os.environ.get("GLA_STOP", "0"))
SCOPES = bool(os.environ.get("KSCOPES", ""))
H_STORE_Q = "sp"
COMPUTE = ("pe", "act", "dve", "pool")


class Buf:
    __slots__ = ("name", "last_w", "readers", "dsem", "last_dma")

    def __init__(self, name):
        self.name = name
        self.last_w = None
        self.readers = {}
        self.dsem = None
        self.last_dma = None


class Tile:
    def __init__(self, t, buf):
        self.t = t
        self.b = buf

    def __getitem__(self, k):
        return self.t[k]


class KB:
    def __init__(self, nc):
        self.nc = nc
        self.eng = {"pe": nc.tensor, "act": nc.scalar, "dve": nc.vector, "pool": nc.gpsimd, "sp": nc.sync}
        self.semh = {}
        self.cnt = {}
        for e in COMPUTE:
            self.semh[e] = nc.alloc_semaphore(name="prog_" + e)
            self.cnt[e] = 0
        self.waited = {e: {} for e in self.eng}
        self.free_dsems = []
        for i in range(80):
            k = "d%d" % i
            self.semh[k] = nc.alloc_semaphore(name="dma_%d" % i)
            self.cnt[k] = 0
            self.free_dsems.append(k)
        self.stage_dsems = []
        self.dbufs = {}
        self.psum = []
        self.psum_i = 0
        self.ring_override = None
        self.n_inst = 0

    def d(self, name, *idx):
        key = (name,) + idx
        b = self.dbufs.get(key)
        if b is None:
            b = Buf(str(key))
            self.dbufs[key] = b
        return b

    def next_psum(self):
        if self.ring_override is not None:
            return self.ring_override.next()
        p = self.psum[self.psum_i % len(self.psum)]
        self.psum_i += 1
        return p

    def _need(self, reads, writes):
        need = {}
        raw = {}

        def add(d, ev):
            if ev is not None and d.get(ev[0], 0) < ev[1]:
                d[ev[0]] = ev[1]

        for b in reads:
            add(need, b.last_w)
            add(raw, b.last_w)
        for b in writes:
            add(need, b.last_w)
            for k, v in b.readers.items():
                add(need, (k, v))
        return need, raw

    def _wait(self, e, need_raw, skip_self):
        need, raw = need_raw
        w = self.waited[e]
        for k, v in need.items():
            if skip_self and k == e:
                v = raw.get(k, 0)
                if v == 0 or e == "pe":
                    continue
            if w.get(k, 0) >= v:
                continue
            self.eng[e].wait_ge(self.semh[k], v)
            w[k] = v

    def _record(self, ev, reads, writes):
        for b in reads:
            if b.readers.get(ev[0], 0) < ev[1]:
                b.readers[ev[0]] = ev[1]
        for b in writes:
            b.last_w = ev
            b.readers = {}

    def op(self, e, fn, reads=(), writes=()):
        reads = [x.b if isinstance(x, Tile) else x for x in reads]
        writes = [x.b if isinstance(x, Tile) else x for x in writes]
        self._wait(e, self._need(reads, writes), True)
        ins = fn(self.eng[e])
        self.cnt[e] += 1
        ins.then_inc(self.semh[e], 1)
        self._record((e, self.cnt[e]), reads, writes)
        self.n_inst += 1

    def mm(self, ps, pairs, reads, extra_writes=()):
        reads = [x.b if isinstance(x, Tile) else x for x in reads]
        writes = [ps.b] + [x.b if isinstance(x, Tile) else x for x in extra_writes]
        self._wait("pe", self._need(reads, writes), True)
        n = len(pairs)
        ins = None
        for i, (o, l, r) in enumerate(pairs):
            ins = self.nc.tensor.matmul(o, l, r, start=(i == 0), stop=(i == n - 1))
        self.cnt["pe"] += 1
        ins.then_inc(self.semh["pe"], 1)
        self._record(("pe", self.cnt["pe"]), reads, writes)
        self.n_inst += n

    def mm_raw(self, fn, reads, writes):
        self.op("pe", fn, reads, writes)

    def dma(self, q, out, in_, sb, reads=(), writes=(), **kw):
        sbb = sb.b if isinstance(sb, Tile) else sb
        reads = [x.b if isinstance(x, Tile) else x for x in reads]
        writes = [x.b if isinstance(x, Tile) else x for x in writes]
        if sbb.dsem is None:
            sbb.dsem = self.free_dsems.pop()
            self.stage_dsems.append(sbb.dsem)
        need, raw = self._need(reads, writes)
        if sbb.last_dma is not None:
            ev = sbb.last_dma
            if need.get(ev[0], 0) < ev[1]:
                need[ev[0]] = ev[1]
        self._wait(q, (need, raw), False)
        ins = self.eng[q].dma_start(out=out, in_=in_, **kw)
        k = sbb.dsem
        self.cnt[k] += 16
        ins.then_inc(self.semh[k], 16)
        ev = (k, self.cnt[k])
        sbb.last_dma = ev
        self._record(ev, reads, writes)
        self.n_inst += 1

    def barrier(self):
        need = {k: v for k, v in self.cnt.items() if v > 0}
        for e in self.eng:
            self._wait(e, (need, {}), True)
        self.free_dsems.extend(self.stage_dsems)
        self.stage_dsems = []


class Stage:
    def __init__(self, kb, name):
        self.kb = kb
        self.name = name
        self.es = ExitStack()
        self.n = 0

    def __enter__(self):
        self.es.__enter__()
        if SCOPES:
            self.es.enter_context(self.kb.nc.named_scope(self.name))
        return self

    def __exit__(self, *a):
        self.kb.barrier()
        return self.es.__exit__(*a)

    def tile(self, shape, dtype, name=None):
        self.n += 1
        nm = "%s_%s%d" % (self.name, name or "t", self.n)
        t = self.es.enter_context(self.kb.nc.sbuf_tensor(nm, list(shape), dtype))
        return Tile(t, Buf(nm))

    def ring(self, n, shape, dtype, name=None):
        return Ring([self.tile(shape, dtype, name) for _ in range(n)])


class Ring:
    def __init__(self, tiles):
        self.tiles = tiles
        self.i = 0

    def next(self):
        t = self.tiles[self.i % len(self.tiles)]
        self.i += 1
        return t


class Prog:
    def __init__(self, n_layers=L, stages=None, debug_out=None):
        self.nc = nc = bass.Bass("TRN2", target_bir_lowering=False)
        self.kb = KB(nc)
        self.n_layers = n_layers
        self.inp = {}
        self.debug_out = debug_out or {}

    def get(self, name, shape, dtype=F32):
        if name not in self.inp:
            self.inp[name] = self.nc.dram_tensor(name, list(shape), dtype, kind="ExternalInput").ap()
        return self.inp[name]

    def dump(self, name, tile, shape, dtype=F32):
        if ("dbg_" + name) not in self.debug_out:
            return
        ap = self.nc.dram_tensor("dbg_" + name, list(shape), dtype, kind="ExternalOutput").ap()
        self.kb.dma("sp", ap, tile.t[tuple(slice(None) for _ in shape)], tile, reads=[tile], writes=[self.kb.d("dbg_" + name)])

    def dscr(self, name, shape, dtype=F32):
        kind = "ExternalOutput" if name in self.debug_out else "Internal"
        return self.nc.dram_tensor(name, list(shape), dtype, kind=kind).ap()


def transpose_stage(P, src, dst, rows, cols, sname):
    kb, nc = P.kb, P.nc
    with Stage(kb, "tr" + sname) as st:
        ident = P.ident_tile
        inr = st.ring(2, [128, cols], F32, "in")
        outr = st.ring(3, [128, 512], F32, "o")
        for r in range(rows // 128):
            it = inr.next()
            kb.dma("sp", it[:, :], src[r * 128:(r + 1) * 128, :], it, reads=[kb.d(sname + "src", r)], writes=[it])
            for c4 in range(cols // 512):
                ps = kb.next_psum()

                def f(pe, ps=ps, it=it, c4=c4):
                    ins = None
                    for c in range(4):
                        ins = pe.transpose(ps[:, c * 128:(c + 1) * 128], it[:, (c4 * 4 + c) * 128:(c4 * 4 + c + 1) * 128], ident[:, :])
                    return ins
                kb.op("pe", f, reads=[it, ident], writes=[ps])
                ot = outr.next()
                eng = "act" if (c4 % 2 == 0) else "dve"
                if eng == "act":
                    kb.op("act", lambda e, ot=ot, ps=ps: e.copy(ot[:, :], ps[:, :]), reads=[ps], writes=[ot])
                else:
                    kb.op("dve", lambda e, ot=ot, ps=ps: e.tensor_copy(ot[:, :], ps[:, :]), reads=[ps], writes=[ot])
                dview = dst[c4 * 512:(c4 + 1) * 512, r * 128:(r + 1) * 128].rearrange("(c p) j -> p c j", p=128)
                kb.dma("sp", dview, ot[:, :].rearrange("p (c j) -> p c j", c=4), ot,
                       reads=[ot], writes=[kb.d(sname + "dst", c4, r)])


def rms_stats(P, st, xt, TB, sq_ring, rstd):
    kb = P.kb
    ps = kb.next_psum()
    sqs = []
    for c in range(DC):
        sq = sq_ring.next()
        import os
        if c % 2 == 0 or "c" in os.environ.get("KDBG", ""):
            kb.op("act", lambda e, sq=sq, c=c: e.activation(out=sq[:, :TB], in_=xt[:, c, :], func=AF.Square), reads=[xt], writes=[sq])
        else:
            kb.op("pool", lambda e, sq=sq, c=c: e.tensor_tensor(sq[:, :TB], xt[:, c, :], xt[:, c, :], ALU.mult), reads=[xt], writes=[sq])

        def f(pe, sq=sq, c=c, ps=ps):
            return pe.matmul(ps[:, :TB], P.ones_f32[:, :], sq[:, :TB], start=(c == 0), stop=(c == DC - 1))
        kb.op("pe", f, reads=[sq, P.ones_f32], writes=[ps])
    import os
    dbg = os.environ.get("KDBG", "")
    if "a" in dbg:
        kb.op("act", lambda e: e.copy(rstd[:, :TB], ps[:, :TB]), reads=[ps, P.eps_tile], writes=[rstd])
    else:
        kb.op("act", lambda e: e.activation(out=rstd[:, :TB], in_=ps[:, :TB], func=AF.Sqrt, bias=P.eps_tile[:, 0:1], scale=1.0 / D), reads=[ps, P.eps_tile], writes=[rstd])
    if "b" not in dbg:
        kb.op("dve", lambda e: e.reciprocal(rstd[:, :TB], rstd[:, :TB]), reads=[rstd], writes=[rstd])


def norm_all_tokens(P, st, hT, gain, nT, dname="hT"):
    kb = P.kb
    TB = 512
    xr = st.ring(2, [128, DC, TB], F32, "nx")
    sqr = st.ring(3, [128, TB], F32, "nsq")
    rr = st.ring(2, [128, TB], F32, "nr")
    hv = hT.rearrange("(c p) t -> p c t", p=128)
    for tb in range(S // TB):
        xt = xr.next()
        kb.dma("sp", xt[:, :, :], hv[:, :, tb * TB:(tb + 1) * TB], xt, reads=[kb.d(dname, c, tb) for c in range(DC)], writes=[xt])
        rstd = rr.next()
        rms_stats(P, st, xt, TB, sqr, rstd)
        for c in range(DC):
            eng = "dve"
            kb.op(eng, lambda e, c=c, xt=xt, rstd=rstd: e.scalar_tensor_tensor(
                out=nT[:, c, tb * TB:(tb + 1) * TB], in0=xt[:, c, :], scalar=gain[:, c:c + 1], in1=rstd[:, :TB],
                op0=ALU.mult, op1=ALU.mult), reads=[xt, rstd, gain], writes=[nT])


def load_vec(P, st, src_ap, shape, name, q="sp"):
    t = st.tile(shape, F32, name)
    P.kb.dma(q, t[:, :], src_ap, t, reads=[], writes=[t])
    return t


def h_update_loop(P, items, mm_fn, scale, h_r, TB=512):
    kb = P.kb
    hv = P.hT.rearrange("(c p) t -> p c t", p=128)
    tiles = {}

    def load(n):
        i, tb = items[n]
        ht = h_r.next()
        kb.dma("sp", ht[:, :], hv[:, i, tb * TB:(tb + 1) * TB], ht, reads=[kb.d("hT", i, tb)], writes=[ht])
        tiles[n] = ht
    LOOK = 3
    for n in range(min(LOOK, len(items))):
        load(n)
    for n, (i, tb) in enumerate(items):
        if n + LOOK < len(items):
            load(n + LOOK)
        po = mm_fn(i, tb)
        ht = tiles.pop(n)
        if scale == 1.0:
            kb.op("dve", lambda e: e.tensor_tensor(ht[:, :], po[:, :], ht[:, :], ALU.add), reads=[po, ht], writes=[ht])
        else:
            kb.op("dve", lambda e: e.scalar_tensor_tensor(out=ht[:, :], in0=po[:, :], scalar=scale, in1=ht[:, :], op0=ALU.mult, op1=ALU.add),
                  reads=[po, ht], writes=[ht])
        kb.dma(H_STORE_Q, hv[:, i, tb * TB:(tb + 1) * TB], ht[:, :], ht, reads=[ht], writes=[kb.d("hT", i, tb)])


def ffn_stage(P, l, which):
    kb, nc = P.kb, P.nc
    w_in = P.get(which + "_w_in", [P.lw, D, 2 * DFF])[l]
    w_out = P.get(which + "_w_out", [P.lw, DFF, D])[l]
    hT = P.hT
    FC = DFF // 128
    NQ = 2
    QC = FC // NQ
    TB = 512
    NTB = S // TB
    w_in_v = w_in.rearrange("(kc p) n -> p kc n", p=128)
    w_out_v = w_out.rearrange("(fc p) n -> p fc n", p=128)
    hv = hT.rearrange("(c p) t -> p c t", p=128)
    with Stage(kb, "%s%d" % (which, l)) as st:
        gain = load_vec(P, st, P.get(which + "_norm_t", [P.lw, 128, DC])[l], [128, DC], "g")
        nT = st.tile([128, DC, S], BF16, "nT")
        with Stage(kb, "%s%dn" % (which, l)) as st2:
            norm_all_tokens(P, st2, hT, gain, nT)
        aT = st.tile([128, QC, S], BF16, "aT")
        wg_r = st.ring(2, [128, DC, 128], BF16, "wg")
        wu_r = st.ring(2, [128, DC, 128], BF16, "wu")
        wo_r = st.ring(2, [128, QC, 128], BF16, "wo")
        sil_r = st.ring(3, [128, TB], F32, "sil")
        h_r = st.ring(6, [128, TB], F32, "h")
        for q in range(NQ):
            for jq in range(QC):
                j = q * QC + jq
                wg = wg_r.next()
                wu = wu_r.next()
                kb.dma("pool", wg[:, :, :], w_in_v[:, :, j * 128:(j + 1) * 128], wg, reads=[], writes=[wg])
                kb.dma("pool", wu[:, :, :], w_in_v[:, :, DFF + j * 128:DFF + (j + 1) * 128], wu, reads=[], writes=[wu])
                for tb in range(NTB):
                    ts_ = slice(tb * TB, (tb + 1) * TB)
                    pg = kb.next_psum()
                    kb.mm(pg, [(pg[:, :], wg[:, k, :], nT[:, k, ts_]) for k in range(DC)], reads=[wg, nT])
                    pu = kb.next_psum()
                    kb.mm(pu, [(pu[:, :], wu[:, k, :], nT[:, k, ts_]) for k in range(DC)], reads=[wu, nT])
                    sil = sil_r.next()
                    kb.op("act", lambda e, sil=sil, pg=pg: e.activation(out=sil[:, :], in_=pg[:, :], func=AF.Silu), reads=[pg], writes=[sil])
                    kb.op("dve", lambda e, sil=sil, pu=pu, jq=jq, ts_=ts_: e.tensor_tensor(aT[:, jq, ts_], sil[:, :], pu[:, :], ALU.mult),
                          reads=[sil, pu], writes=[aT])
            wcur = {}

            def mm_fn(i, tb, q=q):
                if tb == 0:
                    wo = wo_r.next()
                    kb.dma("pool", wo[:, :, :], w_out_v[:, q * QC:(q + 1) * QC, i * 128:(i + 1) * 128], wo, reads=[], writes=[wo])
                    wcur[0] = wo
                wo = wcur[0]
                po = kb.next_psum()
                kb.mm(po, [(po[:, :], wo[:, f, :], aT[:, f, tb * TB:(tb + 1) * TB]) for f in range(QC)], reads=[wo, aT])
                return po
            h_update_loop(P, [(i, tb) for i in range(DC) for tb in range(NTB)], mm_fn, 0.5, h_r)


def proj_B(P, st, nT, w_v, col0, ncols, wring, epi):
    kb = P.kb
    nch = (ncols + 127) // 128
    for c in range(nch):
        m = min(128, ncols - c * 128)
        w = wring.next()
        kb.dma("pool", w[:, :, :m], w_v[:, :, col0 + c * 128:col0 + c * 128 + m], w, reads=[], writes=[w])
        for tb in range(S // 512):
            ps = kb.next_psum()
            kb.mm(ps, [(ps[:m, :], w[:, k, :m], nT[:, k, tb * 512:(tb + 1) * 512]) for k in range(DC)], reads=[w, nT])
            epi(c, m, tb, ps)


def proj_A(P, st, nT, w_v, col0, ncols, wring, epi):
    kb = P.kb
    w = wring.next()
    kb.dma("pool", w[:, :, :ncols], w_v[:, :, col0:col0 + ncols], w, reads=[], writes=[w])
    for tt in range(NT):
        ps = kb.next_psum()
        kb.mm(ps, [(ps[:, :ncols], nT[:, k, tt * 128:(tt + 1) * 128], w[:, k, :ncols]) for k in range(DC)], reads=[w, nT])
        epi(tt, ps)


def mixproj_stage(P, l, side_sgu=False):
    kb = P.kb
    w_in = P.get("w_in", [P.lw, D, N_IN])[l]
    w_v = w_in.rearrange("(kc p) n -> p kc n", p=128)
    sc = P.scr
    with Stage(kb, "mp%d" % l) as st:
        gain = load_vec(P, st, P.get("mix_norm_t", [P.lw, 128, DC])[l], [128, DC], "g")
        nT = st.tile([128, DC, S], BF16, "nT")
        with Stage(kb, "mp%dn" % l) as st2:
            norm_all_tokens(P, st2, P.hT, gain, nT)
        wB = st.ring(3, [128, DC, 128], BF16, "wB")
        wA = st.ring(2, [128, DC, 512], BF16, "wA")
        sB32 = st.ring(2, [128, S], F32, "sB32")
        sB16 = st.ring(2, [128, S], BF16, "sB16")
        sA32 = st.ring(3, [128, 512], F32, "sA32")
        sA16 = st.ring(3, [128, 512], BF16, "sA16")
        cnt = [0]

        side = [None, 0]

        def B(dst, col0, ncols, dt, func=None, scale=1.0):
            ring = sB32 if dt == F32 else sB16
            cur = [None]

            def epi(c, m, tb, ps):
                if side[0] is not None:
                    side[1] += 1
                    if side[1] % 5 == 0:
                        next(side[0], None)
                if tb == 0:
                    cur[0] = ring.next()
                stg = cur[0]
                cnt[0] += 1
                if func is not None:
                    kb.op("act", lambda e: e.activation(out=stg[:m, tb * 512:(tb + 1) * 512], in_=ps[:m, :], func=func, scale=scale), reads=[ps], writes=[stg])
                elif cnt[0] % 2 == 0:
                    kb.op("act", lambda e: e.mul(stg[:m, tb * 512:(tb + 1) * 512], ps[:m, :], scale), reads=[ps], writes=[stg])
                else:
                    kb.op("dve", lambda e: e.tensor_scalar_mul(stg[:m, tb * 512:(tb + 1) * 512], ps[:m, :], scale), reads=[ps], writes=[stg])
                if tb == S // 512 - 1:
                    kb.dma("sp", dst[c * 128:c * 128 + m, :], stg[:m, :], stg, reads=[stg], writes=[kb.d(dst.name, c)])
            proj_B(P, st, nT, w_v, col0, ncols, wB, epi)

        def A(dst, dcol0, col0, ncols, dt, func=None):
            ring = sA32 if dt == F32 else sA16

            def epi(tt, ps):
                stg = ring.next()
                cnt[0] += 1
                if func is not None:
                    kb.op("act", lambda e: e.activation(out=stg[:, :ncols], in_=ps[:, :ncols], func=func), reads=[ps], writes=[stg])
                elif cnt[0] % 2 == 0:
                    kb.op("act", lambda e: e.copy(stg[:, :ncols], ps[:, :ncols]), reads=[ps], writes=[stg])
                else:
                    kb.op("dve", lambda e: e.tensor_copy(stg[:, :ncols], ps[:, :ncols]), reads=[ps], writes=[stg])
                kb.dma("sp", dst[tt * 128:(tt + 1) * 128, dcol0:dcol0 + ncols], stg[:, :ncols], stg, reads=[stg], writes=[kb.d(dst.name, tt, dcol0)])
            proj_A(P, st, nT, w_v, col0, ncols, wA, epi)

        B(sc["uT"], O_SU, 1024, BF16, AF.Gelu_apprx_tanh)
        for hh in range(2):
            A(sc["v"], hh * 512, O_SV + hh * 512, 512, F32, AF.Gelu_apprx_tanh)
        B(sc["fqT"], O_FQ, 1024, BF16, None, 128 ** -0.5)
        B(sc["fkT"], O_FK, 1024, BF16)
        for hh in range(2):
            A(sc["fv"], hh * 512, O_FV + hh * 512, 512, BF16)
        A(sc["ff"], 0, O_FF, 8, F32)
        B(sc["gqT"], O_GQ, 512, F32, None, 128 ** -0.5)
        B(sc["gkT"], O_GK, 512, F32)
        A(sc["gk"], 0, O_GK, 512, F32)
        for hh in range(2):
            A(sc["gv"], hh * 512, O_GV + hh * 512, 512, BF16)
        B(sc["gaT"], O_GA, 16, F32)
        for hh in range(2):
            A(sc["gr"], hh * 512, O_GR + hh * 512, 512, F32, AF.Silu)
        if side_sgu:
            kb.barrier()
            kb.ring_override = Ring(kb.psum[0:6])
            tf = sgu_setup(P, l, st, Ring(kb.psum[6:8]))

            def gen():
                prev = None
                for tt in range(NT):
                    ctx = tf.A(tt)
                    yield
                    if prev is not None:
                        tf.B(prev)
                        yield
                    prev = ctx
                tf.B(prev)
                yield
            side[0] = gen()
        B(sc["gates"], O_GATES, 6144, F32, AF.Sigmoid)
        if side_sgu:
            for _ in side[0]:
                pass
            side[0] = None
            kb.ring_override = None


def fox_stage(P, l, side_setup=None):
    kb = P.kb
    sc = P.scr
    H = 8
    qv = sc["fqT"].rearrange("(h p) t -> p h t", p=128)
    kv = sc["fkT"].rearrange("(h p) t -> p h t", p=128)
    vv = sc["fv"].rearrange("(c p) d -> p c d", p=128)
    ffv = sc["ff"].rearrange("(i p) h -> p i h", p=128)
    fo = sc["foT"]
    with Stage(kb, "fox%d" % l) as st:
        tri = P.tri_tile
        xf = st.tile([128, NT * H], F32, "xf")
        bfb = st.tile([128, NT * H], F32, "bfb")
        kb.dma("sp", xf[:, :].rearrange("p (i h) -> p i h", h=H), ffv, xf, reads=[], writes=[xf], allow_slow_non_contiguous=True)
        kb.dma("sp", bfb[:, :], P.get("fox_b_f_b", [P.lw, 128, NT * H])[l], bfb, reads=[], writes=[bfb])
        kb.op("dve", lambda e: e.tensor_tensor(xf[:, :], xf[:, :], bfb[:, :], ALU.add), reads=[xf, bfb], writes=[xf])
        kb.op("act", lambda e: e.activation(out=xf[:, :], in_=xf[:, :], func=AF.Exp, scale=-1.0), reads=[xf], writes=[xf])
        kb.op("act", lambda e: e.activation(out=xf[:, :], in_=xf[:, :], func=AF.Ln, bias=P.one_tile[:, 0:1], scale=1.0), reads=[xf, P.one_tile], writes=[xf])
        pre = st.tile([128, NT * H], F32, "pre")
        kb.op("dve", lambda e: e.memset(pre[:, 0:H], 0.0), writes=[pre])
        for i in range(1, NT):
            kb.op("dve", lambda e, i=i: e.tensor_tensor(pre[:, i * H:(i + 1) * H], pre[:, (i - 1) * H:i * H], xf[:, (i - 1) * H:i * H], ALU.add),
                  reads=[pre, xf], writes=[pre])
        ps = kb.next_psum()
        kb.mm(ps, [(ps[:, :NT * H], tri[:, :], xf[:, :]), (ps[:, :NT * H], P.ones_f32[:, :], pre[:, :])], reads=[tri, xf, pre, P.ones_f32])
        csb = st.tile([128, NT * H], F32, "csb")
        kb.op("dve", lambda e: e.tensor_copy(csb[:, :], ps[:, :NT * H]), reads=[ps], writes=[csb])
        ps2 = kb.next_psum()
        kb.mm(ps2, [(ps2[:, :NT * H], P.sel64_tile[:, :], csb[:, :])], reads=[P.sel64_tile, csb])
        cmid = st.tile([128, NT * H], F32, "cmid")
        kb.op("dve", lambda e: e.tensor_copy(cmid[:, :], ps2[:, :NT * H]), reads=[ps2], writes=[cmid])
        P.dump("fox_xf", xf, [128, NT * H])
        P.dump("fox_csb", csb, [128, NT * H])
        P.dump("fox_cmid", cmid, [128, NT * H])
        bias = st.tile([128, H, NT, NT], F32, "bias")
        cm3 = cmid[:, :].rearrange("p (i h) -> p h i", h=H)
        for h in range(H):
            for c in range(NT):
                kb.op("dve", lambda e, h=h, c=c: e.tensor_scalar(out=bias[:, h, c, :], in0=cm3[:, h, :], scalar1=csb[:, c * H + h:c * H + h + 1],
                                                                    scalar2=-1.0, op0=ALU.subtract, op1=ALU.mult), reads=[cmid, csb], writes=[bias])
        q_r = st.ring(2, [128, S], BF16, "q")
        k_r = st.ring(2, [128, S], BF16, "k")
        v_r = st.ring(2, [128, NT, 128], BF16, "v")
        e_r = st.ring(4, [128, 512], BF16, "e")
        rd_r = st.ring(2, [128, 512], F32, "rd")
        o_r = st.ring(2, [128, S], BF16, "o")
        accR = Ring(kb.psum[0:4])
        qkR = Ring(kb.psum[4:7] if side_setup is not None else kb.psum[4:8])
        side_fn = side_setup(st, Ring(kb.psum[7:8])) if side_setup is not None else None
        items = []
        for h in range(H):
            for j in range(4):
                for c in range(4 * j + 4):
                    items.append((h, j, c))
        state = {}

        def emit_qk(h, j, c):
            if j == 0 and c == 0:
                qt, kt, vt = q_r.next(), k_r.next(), v_r.next()
                kb.dma("sp", qt[:, :], qv[:, h, :], qt, reads=[], writes=[qt])
                kb.dma("sp", kt[:, :], kv[:, h, :], kt, reads=[], writes=[kt])
                kb.dma("sp", vt[:, :, :], vv[:, :, h * 128:(h + 1) * 128], vt, reads=[], writes=[vt])
                state[("in", h)] = (qt, kt, vt)
            qt, kt, vt = state[("in", h)]
            r = c - 4 * j
            sb0 = max(0, r)
            t0 = sb0 * 128
            ps = qkR.next()
            kb.mm(ps, [(ps[:, t0:512], kt[:, c * 128:(c + 1) * 128], qt[:, j * 512 + t0:(j + 1) * 512])], reads=[kt, qt])
            et = e_r.next()
            for sb in range(sb0, 4):
                b = 4 * j + sb
                kb.op("act", lambda e, sb=sb, b=b: e.activation(
                    out=et[:, sb * 128:(sb + 1) * 128], in_=ps[:, sb * 128:(sb + 1) * 128], func=AF.Exp,
                    bias=bias[:, h, c, b:b + 1], scale=1.0), reads=[ps, bias], writes=[et])
            if r >= 0:
                kb.op("pool", lambda e: e.tensor_tensor(et[:, r * 128:(r + 1) * 128], et[:, r * 128:(r + 1) * 128], tri[:, :], ALU.mult),
                      reads=[et, tri], writes=[et])
            state[("e", h, j, c)] = (et, t0)

        def emit_pv(h, j, c):
            qt, kt, vt = state[("in", h)]
            et, t0 = state.pop(("e", h, j, c))
            nchunks = 4 * j + 4
            if c == 0:
                state[("acc", h, j)] = (accR.next(), accR.next())
                if j == 0:
                    state[("o", h)] = o_r.next()
            po, pd = state[("acc", h, j)]
            ot = state[("o", h)]
            first, last = (c == 0), (c == nchunks - 1)
            kb.op("pe", lambda pe: pe.matmul(po[:, t0:512], vt[:, c, :], et[:, t0:512], start=first, stop=last), reads=[vt, et], writes=[po])
            kb.op("pe", lambda pe: pe.matmul(pd[:, t0:512], P.ones_bf[:, :], et[:, t0:512], start=first, stop=last), reads=[P.ones_bf, et], writes=[pd])
            if last:
                rd = rd_r.next()
                kb.op("dve", lambda e: e.reciprocal(rd[:, :], pd[:, :]), reads=[pd], writes=[rd])
                kb.op("dve", lambda e: e.tensor_tensor(ot[:, j * 512:(j + 1) * 512], po[:, :], rd[:, :], ALU.mult), reads=[po, rd], writes=[ot])
                if j == 3:
                    kb.dma("sp", fo[h * 128:(h + 1) * 128, :], ot[:, :], ot, reads=[ot], writes=[kb.d("foT", h)])

        LOOK = 2
        for n in range(min(LOOK, len(items))):
            emit_qk(*items[n])
        side_every = len(items) // NT
        for n in range(len(items)):
            if n + LOOK < len(items):
                emit_qk(*items[n + LOOK])
            emit_pv(*items[n])
            if side_fn is not None and n % side_every == 0 and n // side_every < NT:
                side_fn(n // side_every)


def sgu_stage(P, l):
    kb = P.kb
    with Stage(kb, "sgu%d" % l) as st:
        tile_fn = sgu_setup(P, l, st, None)
        for tt in range(NT):
            tile_fn(tt)


def sgu_setup(P, l, st, psum_ring):
    kb = P.kb
    sc = P.scr
    uv = sc["uT"].rearrange("(q p) t -> p q t", p=128)
    sav = sc["saT"].rearrange("(q p) t -> p q t", p=128)
    if True:
        lng = load_vec(P, st, P.get("sgu_ln_g_b", [P.lw, 128, 1024])[l], [128, 1024], "lng")
        lnb = load_vec(P, st, P.get("sgu_ln_b_b", [P.lw, 128, 1024])[l], [128, 1024], "lnb")
        bsb = load_vec(P, st, P.get("sgu_b_s_b", [P.lw, 128, 1024])[l], [128, 1024], "bsb")
        wsT = load_vec(P, st, P.get("sgu_w_sT", [P.lw, 128, 512])[l], [128, 512], "wsT")
        wm = st.tile([128, 4, 128], BF16, "wm")
        for g in range(4):
            kb.op("pool", lambda e, g=g: e.tensor_tensor(wm[:, g, :], wsT[:, g * 128:(g + 1) * 128], P.tri_tile[:, :], ALU.mult),
                  reads=[wsT, P.tri_tile], writes=[wm])
        v_r = st.ring(2, [128, 1024], F32, "v")
        sq_r = st.ring(2, [128, 1024], F32, "sq")
        vn_r = st.ring(2, [128, 1024], F32, "vn")
        vb_r = st.ring(2, [128, 1024], BF16, "vb")
        u_r = st.ring(2, [128, 8, 128], BF16, "u")
        o_r = st.ring(2, [128, 8, 128], BF16, "o")
        t_r = st.ring(2, [128, 512], F32, "t")
        s_r = st.ring(2, [128, 16], F32, "s")
        def tile_A(tt):
            vt = v_r.next()
            kb.dma("sp", vt[:, :], sc["v"][tt * 128:(tt + 1) * 128, :], vt, reads=[], writes=[vt])
            ut = u_r.next()
            kb.dma("sp", ut[:, :, :], uv[:, :, tt * 128:(tt + 1) * 128], ut, reads=[], writes=[ut])
            sq = sq_r.next()
            kb.op("pool", lambda e, sq=sq, vt=vt: e.tensor_tensor(sq[:, :], vt[:, :], vt[:, :], ALU.mult), reads=[vt], writes=[sq])
            s = s_r.next()
            kb.op("dve", lambda e, s=s, vt=vt: e.tensor_reduce(out=s[:, 0:4], in_=vt[:, :].rearrange("p (g c) -> p g c", g=4), axis=AX.X, op=ALU.add), reads=[vt], writes=[s])
            kb.op("dve", lambda e, s=s, sq=sq: e.tensor_reduce(out=s[:, 4:8], in_=sq[:, :].rearrange("p (g c) -> p g c", g=4), axis=AX.X, op=ALU.add), reads=[sq], writes=[s])
            kb.op("dve", lambda e, s=s: e.tensor_scalar_mul(s[:, 0:4], s[:, 0:4], 1.0 / 256), reads=[s], writes=[s])
            kb.op("dve", lambda e, s=s: e.tensor_tensor(s[:, 8:12], s[:, 0:4], s[:, 0:4], ALU.mult), reads=[s], writes=[s])
            kb.op("dve", lambda e, s=s: e.scalar_tensor_tensor(out=s[:, 4:8], in0=s[:, 4:8], scalar=1.0 / 256, in1=s[:, 8:12], op0=ALU.mult, op1=ALU.subtract), reads=[s], writes=[s])
            kb.op("act", lambda e, s=s: e.activation(out=s[:, 4:8], in_=s[:, 4:8], func=AF.Sqrt, bias=P.eps_tile[:, 0:1], scale=1.0), reads=[s, P.eps_tile], writes=[s])
            kb.op("dve", lambda e, s=s: e.reciprocal(s[:, 4:8], s[:, 4:8]), reads=[s], writes=[s])
            vn = vn_r.next()
            for g in range(4):
                kb.op("dve", lambda e, g=g, vn=vn, vt=vt, s=s: e.tensor_scalar(out=vn[:, g * 256:(g + 1) * 256], in0=vt[:, g * 256:(g + 1) * 256],
                      scalar1=s[:, g:g + 1], scalar2=s[:, 4 + g:5 + g], op0=ALU.subtract, op1=ALU.mult), reads=[vt, s], writes=[vn])
            kb.op("pool", lambda e, vn=vn: e.tensor_tensor(vn[:, :], vn[:, :], lng[:, :], ALU.mult), reads=[vn, lng], writes=[vn])
            vb = vb_r.next()
            kb.op("pool", lambda e, vn=vn, vb=vb: e.tensor_tensor(vb[:, :], vn[:, :], lnb[:, :], ALU.add), reads=[vn, lnb], writes=[vb])
            if tt == 0:
                P.dump("sgu_s", s, [128, 16])
                P.dump("sgu_vn", vn, [128, 1024])
                P.dump("sgu_v", vt, [128, 1024])
                P.dump("sgu_lng", lng, [128, 1024])
                P.dump("sgu_wsT", wsT, [128, 512])
                P.dump("sgu_bsb", bsb, [128, 1024])
            return (tt, vb, ut)

        def tile_B(ctx):
            tt, vb, ut = ctx
            ot = o_r.next()
            for half in range(2):
                ps = psum_ring.next() if psum_ring is not None else kb.next_psum()

                def f(pe, ps=ps, vb=vb, half=half):
                    ins = None
                    for qq in range(4):
                        q = half * 4 + qq
                        ins = pe.matmul(ps[:, qq * 128:(qq + 1) * 128], vb[:, q * 128:(q + 1) * 128], wm[:, q // 2, :], start=True, stop=True)
                    return ins
                kb.op("pe", f, reads=[vb, wm], writes=[ps])
                t = t_r.next()
                kb.op("dve", lambda e, t=t, ps=ps, half=half: e.tensor_tensor(t[:, :], ps[:, :], bsb[:, half * 512:(half + 1) * 512], ALU.add), reads=[ps, bsb], writes=[t])
                kb.op("pool", lambda e, t=t, ot=ot, ut=ut, half=half: e.tensor_tensor(
                    ot[:, half * 4:(half + 1) * 4, :], t[:, :].rearrange("p (q t) -> p q t", q=4), ut[:, half * 4:(half + 1) * 4, :], ALU.mult),
                    reads=[t, ut], writes=[ot])
            kb.dma("sp", sav[:, :, tt * 128:(tt + 1) * 128], ot[:, :, :], ot, reads=[ot], writes=[kb.d("saT", tt)])
        def tile_fn(tt):
            tile_B(tile_A(tt))
        tile_fn.A = tile_A
        tile_fn.B = tile_B
        return tile_fn


def gla_stage(P, l):
    kb = P.kb
    sc = P.scr
    H = 4
    gqv = sc["gqT"].rearrange("(h p) t -> p h t", p=128)
    gkv = sc["gkT"].rearrange("(h p) t -> p h t", p=128)
    gov = sc["goT"].rearrange("(q p) t -> p q t", p=128)
    with Stage(kb, "gla%d" % l) as st:
        TT = load_vec(P, st, P.get("c_TT", [128, 256])[:, :], [128, 256], "TT")
        REV = load_vec(P, st, P.get("c_rev", [128, 128])[:, :], [128, 128], "REV")
        M64 = load_vec(P, st, P.get("c_m64", [128, 512])[:, :], [128, 512], "M64")
        RM = load_vec(P, st, P.get("c_rowmask", [128, 2])[:, :], [128, 2], "RM")
        onb = load_vec(P, st, P.get("gla_o_norm_b", [P.lw, 128, 1024])[l], [128, 1024], "onb")
        aT = st.tile([32, S], F32, "aT")
        kb.op("dve", lambda e: e.memset(aT[:, :], 1.0), writes=[aT])
        kb.dma("sp", aT[0:16, :], sc["gaT"][:, :], aT, reads=[], writes=[aT])
        wg = st.tile([32, 512], F32, "wg")
        kb.op("dve", lambda e: e.memset(wg[:, :], 0.0), writes=[wg])
        kb.dma("sp", wg[0:16, :], P.get("gla_w_gate", [P.lw, 16, 512])[l], wg, reads=[], writes=[wg])
        kb.dma("sp", wg[16:17, :], P.get("gla_b_gate", [P.lw, 1, 512])[l], wg, reads=[], writes=[wg])
        S32 = [st.tile([128, 256], F32, "S32_%d" % h) for h in range(H)]
        S16_r = [st.ring(3, [128, 256], BF16, "S16_%d" % h) for h in range(H)]
        S16 = []
        for h in range(H):
            kb.op("pool", lambda e, h=h: e.memset(S32[h][:, :], 0.0), writes=[S32[h]])
            t0 = S16_r[h].next()
            kb.op("pool", lambda e, t0=t0: e.memset(t0[:, :], 0.0), writes=[t0])
            S16.append(t0)
        qiA_r = st.ring(2, [128, H, 128], BF16, "qiA")
        qiB_r = st.ring(2, [128, H, 128], BF16, "qiB")
        for r_ in (qiA_r, qiB_r):
            for t in r_.tiles:
                kb.op("pool", lambda e, t=t: e.memset(t[:, :, :], 0.0), writes=[t])
        sp_r = st.ring(2, [128, 512], F32, "sp")
        E_r = [st.ring(2, [128, H, 128], F32, "E%d" % i) for i in range(3)]
        q_r = st.ring(2, [128, H, 128], F32, "q")
        k_r = st.ring(2, [128, H, 128], F32, "k")
        qd_r = st.ring(2, [128, H, 128], BF16, "qd")
        kd_r = st.ring(2, [128, H, 128], BF16, "kd")
        E4_r = st.ring(2, [128, 512], F32, "E4")
        gk_r = st.ring(2, [128, 512], F32, "gk")
        ku_r = st.ring(4, [128, 512], BF16, "ku")
        gv_r = st.ring(2, [128, 1024], BF16, "gv")
        att_r = st.ring(2, [128, 512], BF16, "att")
        sq_r = st.ring(2, [128, 1024], F32, "sq")
        og_r = st.ring(2, [128, 1024], F32, "og")
        gr_r = st.ring(2, [128, 1024], F32, "gr")
        ss_r = st.ring(2, [128, 8], F32, "ss")
        go_r = st.ring(2, [128, 8, 128], BF16, "go")
        def body_A(tt):
            tsl = slice(tt * 128, (tt + 1) * 128)
            ps = kb.next_psum()
            kb.mm(ps, [(ps[:, :], aT[:, tsl], wg[:, :])], reads=[aT, wg])
            sp = sp_r.next()
            kb.op("act", lambda e, sp=sp, ps=ps: e.activation(out=sp[:, :], in_=ps[:, :], func=AF.Exp, scale=-1.0), reads=[ps], writes=[sp])
            kb.op("act", lambda e, sp=sp: e.activation(out=sp[:, :], in_=sp[:, :], func=AF.Ln, bias=P.one_tile[:, 0:1], scale=1.0), reads=[sp, P.one_tile], writes=[sp])
            qt, kt, gkt, gvt, grt = q_r.next(), k_r.next(), gk_r.next(), gv_r.next(), gr_r.next()
            kb.dma("sp", qt[:, :, :], gqv[:, :, tsl], qt, reads=[], writes=[qt])
            kb.dma("sp", kt[:, :, :], gkv[:, :, tsl], kt, reads=[], writes=[kt])
            kb.dma("sp", gkt[:, :], sc["gk"][tsl, :], gkt, reads=[], writes=[gkt])
            kb.dma("sp", gvt[:, :], sc["gv"][tsl, :], gvt, reads=[], writes=[gvt])
            kb.dma("sp", grt[:, :], sc["gr"][tsl, :], grt, reads=[], writes=[grt])
            E1, E2, E3 = [r_.next() for r_ in E_r]
            for pair in range(2):
                psA = kb.next_psum()

                def fA(pe, psA=psA, sp=sp, pair=pair):
                    ins = None
                    for hh in range(2):
                        h = pair * 2 + hh
                        ins = pe.matmul(psA[:, hh * 256:(hh + 1) * 256], sp[:, h * 128:(h + 1) * 128], TT[:, :], start=True, stop=True)
                    return ins
                kb.op("pe", fA, reads=[sp, TT], writes=[psA])
                pv = psA[:, :].rearrange("p (h c) -> p h c", h=2)
                hs = slice(pair * 2, pair * 2 + 2)
                kb.op("act", lambda e, pv=pv, hs=hs, E1=E1: e.activation(out=E1[:, hs, :], in_=pv[:, :, 0:128], func=AF.Exp), reads=[psA], writes=[E1])
                kb.op("act", lambda e, pv=pv, hs=hs, E2=E2: e.activation(out=E2[:, hs, :], in_=pv[:, :, 128:256], func=AF.Exp), reads=[psA], writes=[E2])
                kb.op("act", lambda e, pv=pv, hs=hs, E3=E3: e.activation(out=E3[:, hs, :], in_=pv[:, :, 128:256], func=AF.Exp, scale=-1.0), reads=[psA], writes=[E3])
            qd, kd, qiA, qiB = qd_r.next(), kd_r.next(), qiA_r.next(), qiB_r.next()
            kb.op("dve", lambda e, qd=qd, qt=qt, E2=E2: e.tensor_tensor(qd[:, :, :], qt[:, :, :], E2[:, :, :], ALU.mult), reads=[qt, E2], writes=[qd])
            kb.op("pool", lambda e, kd=kd, kt=kt, E3=E3: e.tensor_tensor(kd[:, :, :], kt[:, :, :], E3[:, :, :], ALU.mult), reads=[kt, E3], writes=[kd])
            kb.op("dve", lambda e, qiA=qiA, qt=qt, E1=E1: e.tensor_tensor(qiA[:, :, 0:64], qt[:, :, 0:64], E1[:, :, 0:64], ALU.mult), reads=[qt, E1], writes=[qiA])
            kb.op("pool", lambda e, qiB=qiB, qt=qt, E1=E1: e.tensor_tensor(qiB[:, :, 64:128], qt[:, :, 64:128], E1[:, :, 64:128], ALU.mult), reads=[qt, E1], writes=[qiB])
            psR = kb.next_psum()
            kb.mm(psR, [(psR[:, :], REV[:, :], sp[:, :])], reads=[REV, sp])
            E4 = E4_r.next()
            kb.op("act", lambda e, E4=E4, psR=psR: e.activation(out=E4[:, :], in_=psR[:, :], func=AF.Exp), reads=[psR], writes=[E4])
            ku = [ku_r.next(), ku_r.next()]
            for a in range(2):
                kb.op("dve", lambda e, a=a, ku=ku, gkt=gkt, E4=E4: e.scalar_tensor_tensor(out=ku[a][:, :], in0=gkt[:, :], scalar=RM[:, a:a + 1], in1=E4[:, :],
                                                                                        op0=ALU.mult, op1=ALU.mult), reads=[gkt, E4, RM], writes=[ku[a]])
            psT = kb.next_psum()

            def fT(pe, psT=psT, kd=kd, qd=qd):
                ins = None
                for h in range(H):
                    ins = pe.matmul(psT[:, h * 128:(h + 1) * 128], kd[:, h, :], qd[:, h, :], start=True, stop=True)
                return ins
            kb.op("pe", fT, reads=[kd, qd], writes=[psT])
            att = att_r.next()
            kb.op("dve", lambda e, att=att, psT=psT: e.tensor_tensor(att[:, :], psT[:, :], M64[:, :], ALU.mult), reads=[psT, M64], writes=[att])
            return (tsl, E1, ku, gvt, att, qiA, qiB, grt)

        def body_B(ctx):
            tsl, E1, ku, gvt, att, qiA, qiB, grt = ctx
            psO = [kb.next_psum(), kb.next_psum()]
            for h in range(H):
                psU = kb.next_psum()

                def fU(pe, psU=psU, ku=ku, gvt=gvt, h=h):
                    ins = None
                    for a in range(2):
                        ins = pe.matmul(psU[:, a * 256:(a + 1) * 256], ku[a][:, h * 128:(h + 1) * 128],
                                        gvt[:, h * 256:(h + 1) * 256], start=True, stop=True)
                    return ins
                kb.op("pe", fU, reads=[ku[0], ku[1], gvt], writes=[psU])
                Sn = S16[h]
                kb.op("dve", lambda e, h=h, psU=psU, E1=E1: e.scalar_tensor_tensor(out=S32[h][:, :], in0=S32[h][:, :], scalar=E1[:, h, 63:64], in1=psU[:, 0:256],
                                                                                    op0=ALU.mult, op1=ALU.add), reads=[S32[h], E1, psU], writes=[S32[h]])
                Sn1 = S16_r[h].next()
                kb.op("pool", lambda e, h=h, Sn1=Sn1: e.tensor_copy(Sn1[:, :], S32[h][:, :]), reads=[S32[h]], writes=[Sn1])
                kb.op("dve", lambda e, h=h, psU=psU, E1=E1: e.scalar_tensor_tensor(out=S32[h][:, :], in0=S32[h][:, :], scalar=E1[:, h, 127:128], in1=psU[:, 256:512],
                                                                                    op0=ALU.mult, op1=ALU.add), reads=[S32[h], E1, psU], writes=[S32[h]])
                Sn2 = S16_r[h].next()
                kb.op("pool", lambda e, h=h, Sn2=Sn2: e.tensor_copy(Sn2[:, :], S32[h][:, :]), reads=[S32[h]], writes=[Sn2])
                S16[h] = Sn2
                po = psO[h // 2]
                oc = slice((h % 2) * 256, (h % 2 + 1) * 256)
                kb.mm(po, [(po[:, oc], att[:, h * 128:(h + 1) * 128], gvt[:, h * 256:(h + 1) * 256]),
                           (po[:, oc], qiA[:, h, :], Sn[:, :]),
                           (po[:, oc], qiB[:, h, :], Sn1[:, :])], reads=[att, gvt, qiA, qiB, Sn, Sn1])
            sq = sq_r.next()
            for i2 in range(2):
                kb.op("act", lambda e, sq=sq, i2=i2, psO=psO: e.activation(out=sq[:, i2 * 512:(i2 + 1) * 512], in_=psO[i2][:, :], func=AF.Square), reads=[psO[i2]], writes=[sq])
            ss = ss_r.next()
            kb.op("dve", lambda e, ss=ss, sq=sq: e.tensor_reduce(out=ss[:, 0:4], in_=sq[:, :].rearrange("p (g c) -> p g c", g=4), axis=AX.X, op=ALU.add), reads=[sq], writes=[ss])
            kb.op("act", lambda e, ss=ss: e.activation(out=ss[:, 0:4], in_=ss[:, 0:4], func=AF.Sqrt, bias=P.eps_tile[:, 0:1], scale=1.0 / 256), reads=[ss, P.eps_tile], writes=[ss])
            kb.op("dve", lambda e, ss=ss: e.reciprocal(ss[:, 0:4], ss[:, 0:4]), reads=[ss], writes=[ss])
            og = og_r.next()
            for h in range(H):
                kb.op("dve", lambda e, h=h, og=og, ss=ss, psO=psO: e.scalar_tensor_tensor(
                    out=og[:, h * 256:(h + 1) * 256], in0=psO[h // 2][:, (h % 2) * 256:(h % 2 + 1) * 256], scalar=ss[:, h:h + 1],
                    in1=onb[:, h * 256:(h + 1) * 256], op0=ALU.mult, op1=ALU.mult), reads=[psO[h // 2], ss, onb], writes=[og])
            kb.op("pool", lambda e, og=og, grt=grt: e.tensor_tensor(og[:, :], og[:, :], grt[:, :], ALU.mult), reads=[og, grt], writes=[og])
            go = go_r.next()
            for i2 in range(2):
                psX = kb.next_psum()

                def fX(pe, psX=psX, og=og, i2=i2):
                    ins = None
                    for qq in range(4):
                        q = i2 * 4 + qq
                        ins = pe.transpose(psX[:, qq * 128:(qq + 1) * 128], og[:, q * 128:(q + 1) * 128], P.ident_tile[:, :])
                    return ins
                kb.op("pe", fX, reads=[og, P.ident_tile], writes=[psX])
                pxv = psX[:, :].rearrange("p (q t) -> p q t", q=4)
                if i2 == 0:
                    kb.op("act", lambda e, go=go, pxv=pxv: e.copy(go[:, 0:4, :], pxv), reads=[psX], writes=[go])
                else:
                    kb.op("dve", lambda e, go=go, pxv=pxv: e.tensor_copy(go[:, 4:8, :], pxv), reads=[psX], writes=[go])
            kb.dma("sp", gov[:, :, tsl], go[:, :, :], go, reads=[go], writes=[kb.d("goT", tsl.start)])

        ctxs = {0: body_A(0)}
        for tt in range(NT):
            if tt + 1 < NT:
                ctxs[tt + 1] = body_A(tt + 1)
            body_B(ctxs.pop(tt))


def xa_stage(P, l):
    kb = P.kb
    w_q = P.get("xa_w_q", [P.lw, D, 512])[l].rearrange("(kc p) n -> p kc n", p=128)
    w_kv = P.get("xa_w_kv", [P.lw, D, 1024])[l].rearrange("(kc p) n -> p kc n", p=128)
    w_o = P.get("xa_w_o", [P.lw, 512, D])[l].rearrange("(kc p) n -> p kc n", p=128)
    hv = P.hT.rearrange("(c p) t -> p c t", p=128)
    mv = P.memT.rearrange("(c p) t -> p c t", p=128)
    TB = 512
    with Stage(kb, "xa%d" % l) as st:
        gain = load_vec(P, st, P.get("xa_norm_t", [P.lw, 128, DC])[l], [128, DC], "g")
        mgain = load_vec(P, st, P.get("mem_norm_t", [P.lw, 128, DC])[l], [128, DC], "mg")
        kT = st.tile([128, 4, MEM], BF16, "kT")
        v16 = st.tile([128, 2, 512], BF16, "v16")
        q16 = st.tile([128, 4, S], BF16, "q16")
        oT = st.tile([128, 4, S], BF16, "oT")
        nT = st.tile([128, DC, S], BF16, "nT")
        with Stage(kb, "xa%dm" % l) as st2:
            mx = st2.tile([128, DC, MEM], F32, "mx")
            kb.dma("sp", mx[:, :, :], mv, mx, reads=[], writes=[mx])
            sqr = st2.ring(3, [128, 512], F32, "sq")
            rstd = st2.tile([128, 512], F32, "rstd")
            rms_stats(P, st2, mx, MEM, sqr, rstd)
            mn = st2.tile([128, DC, MEM], BF16, "mn")
            for c in range(DC):
                kb.op("dve", lambda e, c=c: e.scalar_tensor_tensor(out=mn[:, c, :], in0=mx[:, c, :], scalar=mgain[:, c:c + 1], in1=rstd[:, :MEM],
                                                                    op0=ALU.mult, op1=ALU.mult), reads=[mx, rstd, mgain], writes=[mn])
            wk_r = st2.ring(2, [128, DC, 128], BF16, "wk")
            for c in range(4):
                w = wk_r.next()
                kb.dma("pool", w[:, :, :], w_kv[:, :, c * 128:(c + 1) * 128], w, reads=[], writes=[w])
                ps = kb.next_psum()
                kb.mm(ps, [(ps[:, :MEM], w[:, k, :], mn[:, k, :]) for k in range(DC)], reads=[w, mn])
                kb.op("act", lambda e, c=c, ps=ps: e.copy(kT[:, c, :], ps[:, :MEM]), reads=[ps], writes=[kT])
            wv = st2.tile([128, DC, 512], BF16, "wv")
            kb.dma("pool", wv[:, :, :], w_kv[:, :, 512:1024], wv, reads=[], writes=[wv])
            for mt in range(2):
                ps = kb.next_psum()
                kb.mm(ps, [(ps[:, :], mn[:, k, mt * 128:(mt + 1) * 128], wv[:, k, :]) for k in range(DC)], reads=[wv, mn])
                kb.op("dve", lambda e, mt=mt, ps=ps: e.tensor_copy(v16[:, mt, :], ps[:, :]), reads=[ps], writes=[v16])
        with Stage(kb, "xa%dn" % l) as st2:
            norm_all_tokens(P, st2, P.hT, gain, nT)
        wq_r = st.ring(2, [128, DC, 128], BF16, "wq")
        for c in range(4):
            w = wq_r.next()
            kb.dma("pool", w[:, :, :], w_q[:, :, c * 128:(c + 1) * 128], w, reads=[], writes=[w])
            for tb in range(S // TB):
                ps = kb.next_psum()
                kb.mm(ps, [(ps[:, :], w[:, k, :], nT[:, k, tb * TB:(tb + 1) * TB]) for k in range(DC)], reads=[w, nT])
                kb.op("act", lambda e, c=c, tb=tb, ps=ps: e.mul(q16[:, c, tb * TB:(tb + 1) * TB], ps[:, :], 128 ** -0.5), reads=[ps], writes=[q16])
        e_r = st.ring(4, [128, TB], BF16, "e")
        rd_r = st.ring(2, [128, TB], F32, "rd")
        xitems = [(h, tb) for h in range(4) for tb in range(S // TB)]
        xes = {}

        def x_qk(h, tb):
            ts_ = slice(tb * TB, (tb + 1) * TB)
            es = []
            for mc in range(2):
                ps = kb.next_psum()
                kb.mm(ps, [(ps[:, :], kT[:, h, mc * 128:(mc + 1) * 128], q16[:, h, ts_])], reads=[kT, q16])
                et = e_r.next()
                kb.op("act", lambda e, et=et, ps=ps: e.activation(out=et[:, :], in_=ps[:, :], func=AF.Exp), reads=[ps], writes=[et])
                es.append(et)
            xes[(h, tb)] = es

        def x_pv(h, tb):
            ts_ = slice(tb * TB, (tb + 1) * TB)
            es = xes.pop((h, tb))
            po = kb.next_psum()
            kb.mm(po, [(po[:, :], v16[:, mc, h * 128:(h + 1) * 128], es[mc][:, :]) for mc in range(2)], reads=[v16] + es)
            pd = kb.next_psum()
            kb.mm(pd, [(pd[:, :], P.ones_bf[:, :], es[mc][:, :]) for mc in range(2)], reads=[P.ones_bf] + es)
            rd = rd_r.next()
            kb.op("dve", lambda e: e.reciprocal(rd[:, :], pd[:, :]), reads=[pd], writes=[rd])
            kb.op("dve", lambda e: e.tensor_tensor(oT[:, h, ts_], po[:, :], rd[:, :], ALU.mult), reads=[po, rd], writes=[oT])
        x_qk(*xitems[0])
        for n in range(len(xitems)):
            if n + 1 < len(xitems):
                x_qk(*xitems[n + 1])
            x_pv(*xitems[n])
        wo_r = st.ring(2, [128, 4, 128], BF16, "wo")
        h_r = st.ring(6, [128, TB], F32, "h")
        wcur = {}

        def mm_fn(i, tb):
            if tb == 0:
                wo = wo_r.next()
                kb.dma("pool", wo[:, :, :], w_o[:, :, i * 128:(i + 1) * 128], wo, reads=[], writes=[wo])
                wcur[0] = wo
            wo = wcur[0]
            po = kb.next_psum()
            kb.mm(po, [(po[:, :], wo[:, k, :], oT[:, k, tb * TB:(tb + 1) * TB]) for k in range(4)], reads=[wo, oT])
            return po
        h_update_loop(P, [(i, tb) for i in range(DC) for tb in range(S // TB)], mm_fn, 1.0, h_r)


def merge_stage(P, l):
    kb = P.kb
    sc = P.scr
    wbr = [P.get("w_branch_" + x, [P.lw, 1024, D])[l].rearrange("(kc p) n -> p kc n", p=128) for x in "abc"]
    w_out = P.get("w_out", [P.lw, D, D])[l].rearrange("(kc p) n -> p kc n", p=128)
    brT = [sc[x].rearrange("(kc p) t -> p kc t", p=128) for x in ("saT", "foT", "goT")]
    gv = sc["gates"].rearrange("(g c p) t -> p g c t", p=128, g=3)
    TB = 512
    NTB = S // TB
    with Stage(kb, "mg%d" % l) as st:
        mT = st.tile([128, DC, S], BF16, "mT")
        with Stage(kb, "mg%da" % l) as st2:
            brt = [[st2.tile([128, 8, TB], BF16, "br%d_%d" % (i, tb)) for tb in range(NTB)] for i in range(3)]
            for tb in range(NTB):
                for b in range(3):
                    kb.dma("sp", brt[b][tb][:, :, :], brT[b][:, :, tb * TB:(tb + 1) * TB], brt[b][tb], reads=[], writes=[brt[b][tb]])
            wb_r = [st2.ring(2, [128, 8, 128], BF16, "wb%d" % i) for i in range(3)]
            g_r = st2.ring(2, [128, 3, TB], F32, "g")
            t_r = st2.ring(2, [128, 3, TB], F32, "t")
            for i in range(DC):
                wb = [r.next() for r in wb_r]
                for b in range(3):
                    kb.dma("pool", wb[b][:, :, :], wbr[b][:, :, i * 128:(i + 1) * 128], wb[b], reads=[], writes=[wb[b]])
                for tb in range(NTB):
                    ts_ = slice(tb * TB, (tb + 1) * TB)
                    g = g_r.next()
                    kb.dma("sp", g[:, :, :], gv[:, :, i, ts_], g, reads=[], writes=[g])
                    t = t_r.next()
                    for b in range(3):
                        ps = kb.next_psum()
                        kb.mm(ps, [(ps[:, :], wb[b][:, k, :], brt[b][tb][:, k, :]) for k in range(8)], reads=[wb[b], brt[b][tb]])
                        kb.op("dve", lambda e, b=b, ps=ps: e.tensor_tensor(t[:, b, :], g[:, b, :], ps[:, :], ALU.mult), reads=[g, ps], writes=[t])
                    kb.op("pool", lambda e: e.tensor_tensor(t[:, 0, :], t[:, 0, :], t[:, 1, :], ALU.add), reads=[t], writes=[t])
                    kb.op("pool", lambda e, i=i, ts_=ts_: e.tensor_tensor(mT[:, i, ts_], t[:, 0, :], t[:, 2, :], ALU.add), reads=[t], writes=[mT])
        wo_r = st.ring(2, [128, DC, 128], BF16, "wo")
        h_r = st.ring(6, [128, TB], F32, "h")
        wcur = {}

        def mm_fn(i, tb):
            if tb == 0:
                wo = wo_r.next()
                kb.dma("pool", wo[:, :, :], w_out[:, :, i * 128:(i + 1) * 128], wo, reads=[], writes=[wo])
                wcur[0] = wo
            wo = wcur[0]
            po = kb.next_psum()
            kb.mm(po, [(po[:, :], wo[:, k, :], mT[:, k, tb * TB:(tb + 1) * TB]) for k in range(DC)], reads=[wo, mT])
            return po
        h_update_loop(P, [(i, tb) for i in range(DC) for tb in range(NTB)], mm_fn, 1.0, h_r)


def final_stage(P):
    kb = P.kb
    with Stage(kb, "fin") as st:
        gain = load_vec(P, st, P.get("final_norm_t", [128, DC]), [128, DC], "g")
        TB = 512
        xr = st.ring(2, [128, DC, TB], F32, "x")
        sqr = st.ring(3, [128, TB], F32, "sq")
        rr = st.ring(2, [128, TB], F32, "r")
        yr = st.ring(3, [128, TB], F32, "y")
        outr = st.ring(8, [128, D], F32, "o")
        hv = P.hT.rearrange("(c p) t -> p c t", p=128)
        for tb in range(S // TB):
            xt = xr.next()
            for half in range(2):
                cs = slice(half * 8, half * 8 + 8)
                kb.dma("sp", xt[:, cs, :], hv[:, cs, tb * TB:(tb + 1) * TB], xt,
                       reads=[kb.d("hT", c, tb) for c in range(half * 8, half * 8 + 8)], writes=[xt])
            rstd = rr.next()
            import os
            dbg = os.environ.get("KDBG", "")
            if "e" in dbg:
                kb.op("dve", lambda e: e.memset(rstd[:, :], 1.0), writes=[rstd])
            else:
                rms_stats(P, st, xt, TB, sqr, rstd)
            cur = [outr.next() for _ in range(4)]
            if "d" in dbg:
                continue
            for c in range(DC):
                y = yr.next()
                if "f" in dbg:
                    kb.op("dve", lambda e, c=c, y=y: e.tensor_tensor(y[:, :], xt[:, c, :], rstd[:, :TB], ALU.mult),
                          reads=[xt, rstd, gain], writes=[y])
                else:
                    kb.op("dve", lambda e, c=c, y=y: e.scalar_tensor_tensor(
                        out=y[:, :], in0=xt[:, c, :], scalar=gain[:, c:c + 1], in1=rstd[:, :TB], op0=ALU.mult, op1=ALU.mult),
                        reads=[xt, rstd, gain], writes=[y])
                ps = kb.next_psum()

                def f(pe, ps=ps, y=y):
                    ins = None
                    for j in range(4):
                        ins = pe.transpose(ps[:, j * 128:(j + 1) * 128], y[:, j * 128:(j + 1) * 128], P.ident_tile[:, :])
                    return ins
                kb.op("pe", f, reads=[y, P.ident_tile], writes=[ps])
                for j in range(4):
                    if c % 2 == 0:
                        kb.op("act", lambda e, j=j, ps=ps, c=c: e.copy(cur[j][:, c * 128:(c + 1) * 128], ps[:, j * 128:(j + 1) * 128]),
                              reads=[ps], writes=[cur[j]])
                    else:
                        kb.op("pool" if False else "dve", lambda e, j=j, ps=ps, c=c: e.tensor_copy(cur[j][:, c * 128:(c + 1) * 128], ps[:, j * 128:(j + 1) * 128]),
                              reads=[ps], writes=[cur[j]])
            for j in range(4):
                r0 = tb * TB + j * 128
                if "g" in dbg:
                    continue
                kb.dma("sp", P.out[r0:r0 + 128, :], cur[j][:, :], cur[j], reads=[cur[j]], writes=[kb.d("out", tb, j)])


def build(n_layers=L, stages=("ffn1", "mixproj", "sgu", "fox", "gla", "merge", "xa", "ffn2"), debug_out=None, lw=L):
    P = Prog(n_layers, stages, debug_out)
    P.lw = lw
    nc, kb = P.nc, P.kb
    x = P.get("x", [S, D])
    P.out = nc.dram_tensor("out", [S, D], F32, kind="ExternalOutput").ap()
    P.hT = P.dscr("hT", [D, S])
    P.scr = {}
    for nm, shp, dt in [("uT", [1024, S], BF16), ("v", [S, 1024], F32), ("fqT", [1024, S], BF16), ("fkT", [1024, S], BF16),
                        ("fv", [S, 1024], BF16), ("ff", [S, 8], F32), ("gqT", [512, S], F32), ("gkT", [512, S], F32),
                        ("gk", [S, 512], F32), ("gv", [S, 1024], BF16), ("gaT", [16, S], F32), ("gr", [S, 1024], F32),
                        ("gates", [6144, S], F32), ("saT", [1024, S], BF16), ("foT", [1024, S], BF16), ("goT", [1024, S], BF16)]:
        P.scr[nm] = P.dscr("s_" + nm, shp, dt)
    with ExitStack() as es:
        for i in range(8):
            t = es.enter_context(nc.psum_tensor("ps%d" % i, [128, 512], F32))
            kb.psum.append(Tile(t, Buf("ps%d" % i)))
        with Stage(kb, "glob") as g:
            P.ident_tile = g.tile([128, 128], F32, "ident")
            kb.dma("sp", P.ident_tile[:, :], P.get("ident", [128, 128])[:, :], P.ident_tile, writes=[P.ident_tile])
            P.ones_f32 = g.tile([128, 128], F32, "ones")
            kb.op("dve", lambda e: e.memset(P.ones_f32[:, :], 1.0), writes=[P.ones_f32])
            P.eps_tile = g.tile([128, 1], F32, "eps")
            kb.op("dve", lambda e: e.memset(P.eps_tile[:, :], EPS), writes=[P.eps_tile])
            P.one_tile = g.tile([128, 1], F32, "one")
            kb.op("dve", lambda e: e.memset(P.one_tile[:, :], 1.0), writes=[P.one_tile])
            P.ones_bf = g.tile([128, 128], BF16, "onesbf")
            kb.op("dve", lambda e: e.memset(P.ones_bf[:, :], 1.0), writes=[P.ones_bf])
            P.tri_tile = g.tile([128, 128], F32, "tri")
            kb.dma("sp", P.tri_tile[:, :], P.get("c_tri", [128, 128])[:, :], P.tri_tile, writes=[P.tri_tile])
            P.sel64_tile = g.tile([128, 128], F32, "sel64")
            kb.dma("sp", P.sel64_tile[:, :], P.get("c_sel64", [128, 128])[:, :], P.sel64_tile, writes=[P.sel64_tile])
            if "notr" not in stages:
                transpose_stage(P, x, P.hT, S, D, "x")
            if "xa" in stages:
                P.memT = P.dscr("memT", [D, MEM])
                transpose_stage(P, P.get("mem", [MEM, D]), P.memT, MEM, D, "mem")
            for l in range(n_layers):
                if "ffn1" in stages:
                    ffn_stage(P, l, "ffn1")
                if "mixproj" in stages:
                    mixproj_stage(P, l)
                if "mixproj_sgu" in stages:
                    mixproj_stage(P, l, side_sgu=True)
                if "sgufox" in stages:
                    fox_stage(P, l, side_setup=lambda st, ring, l=l: sgu_setup(P, l, st, ring))
                if "sgu" in stages:
                    sgu_stage(P, l)
                if "fox" in stages:
                    fox_stage(P, l)
                if "gla" in stages:
                    gla_stage(P, l)
                if "merge" in stages:
                    merge_stage(P, l)
                if "xa" in stages:
                    xa_stage(P, l)
                if "ffn2" in stages:
                    ffn_stage(P, l, "ffn2")
            if "nofin" not in stages:
                final_stage(P)
    return P


def vt(v):
    v = np.asarray(v, np.float32)
    return np.ascontiguousarray(np.swapaxes(v.reshape(v.shape[:-1] + (-1, 128)), -1, -2))


def _bcast(v, width):
    v = np.asarray(v, np.float32)
    v = v.reshape(v.shape[0], 1, width)
    return np.ascontiguousarray(np.broadcast_to(v, (v.shape[0], 128, width)))


def _consts():
    s_ = np.arange(128)[:, None]
    t_ = np.arange(128)[None, :]
    same = (s_ // 64) == (t_ // 64)
    tri64 = (same & (s_ <= t_)).astype(np.float32)
    refsel = (same & (s_ <= (t_ // 64) * 64 + 32)).astype(np.float32)
    sel64 = np.zeros((128, 128), np.float32)
    sel64[64, :] = 1.0
    rowmask = np.zeros((128, 2), np.float32)
    rowmask[:64, 0] = 1.0
    rowmask[64:, 1] = 1.0
    return {
        "ident": np.eye(128, dtype=np.float32),
        "c_tri": np.triu(np.ones((128, 128), np.float32)),
        "c_sel64": sel64,
        "c_rowmask": rowmask,
        "c_TT": np.ascontiguousarray(np.concatenate([-tri64 / 16.0, -(tri64 - refsel) / 16.0], axis=1)),
        "c_rev": np.ascontiguousarray(-(same & (s_ > t_)).astype(np.float32) / 16.0),
        "c_m64": np.ascontiguousarray(np.tile(tri64, (1, 4))),
    }


def prep_inputs(P, inputs, b):
    g = lambda k: np.asarray(inputs[k], np.float32)
    src = {
        "x": lambda: np.ascontiguousarray(g("x")[b]),
        "mem": lambda: np.ascontiguousarray(g("mem")[b]),
        "ffn1_norm_t": lambda: vt(g("ffn1_norm")),
        "ffn2_norm_t": lambda: vt(g("ffn2_norm")),
        "mix_norm_t": lambda: vt(g("mix_norm")),
        "xa_norm_t": lambda: vt(g("xa_norm")),
        "mem_norm_t": lambda: vt(g("mem_norm")),
        "final_norm_t": lambda: vt(g("final_norm")),
        "ffn1_w_in": lambda: g("ffn1_w_in"), "ffn1_w_out": lambda: g("ffn1_w_out"),
        "ffn2_w_in": lambda: g("ffn2_w_in"), "ffn2_w_out": lambda: g("ffn2_w_out"),
        "w_in": lambda: g("w_in"), "w_out": lambda: g("w_out"),
        "w_branch_a": lambda: g("w_branch_a"), "w_branch_b": lambda: g("w_branch_b"), "w_branch_c": lambda: g("w_branch_c"),
        "xa_w_q": lambda: g("xa_w_q"), "xa_w_kv": lambda: g("xa_w_kv"), "xa_w_o": lambda: g("xa_w_o"),
        "gla_w_gate": lambda: g("gla_w_gate"),
        "gla_b_gate": lambda: np.ascontiguousarray(g("gla_b_gate").reshape(-1, 1, 512)),
        "gla_o_norm_b": lambda: _bcast(g("gla_o_norm"), 1024),
        "sgu_ln_g_b": lambda: _bcast(g("sgu_ln_g"), 1024),
        "sgu_ln_b_b": lambda: _bcast(g("sgu_ln_b"), 1024),
        "sgu_b_s_b": lambda: _bcast(np.repeat(g("sgu_b_s"), 2, axis=1), 1024),
        "sgu_w_sT": lambda: np.ascontiguousarray(np.transpose(g("sgu_w_s"), (0, 3, 1, 2)).reshape(-1, 128, 512)),
        "fox_b_f_b": lambda: _bcast(np.tile(g("fox_b_f"), (1, NT)), NT * 8),
    }
    consts = _consts()
    m = {}
    for name in P.inp:
        m[name] = consts[name] if name in consts else src[name]()
    return m


def kernel(**inputs):
    P = build()
    in_maps = [prep_inputs(P, inputs, b) for b in range(8)]
    res = run_bass_kernel_spmd(P.nc, in_maps, core_ids=list(range(8)))
    return np.stack([r["out"] for r in res.results], axis=0)
```
